# Optimizing a Trainium2 kernel written in Bass

```python
import math
import functools
import jax
import jax.numpy as jnp
from jax import lax
import numpy as np

D_MODEL = 1024
BATCH = 8
SEQ = 2048
DEPTH = 2
DEC_BATCH = 128
DEC_SEQ = 4
PAST_LEN = 2048
PAGE_SIZE = 128

N_META = 16
BLOCK = 128
D_FF = 2816
EPS = 1e-6
FOX_HEADS = 4
FOX_HD = 64
FOX_W = FOX_HEADS * FOX_HD
SSD_HEADS = 8
SSD_HD = 64
SSD_W = SSD_HEADS * SSD_HD
SSD_GROUPS = 2
SSD_STATE = 64
SSD_CONV = 4
SSD_CONV_DIM = SSD_W + 2 * SSD_GROUPS * SSD_STATE
GLA_HEADS = 4
GLA_DK = 32
GLA_DV = 64
GLA_KW = GLA_HEADS * GLA_DK
GLA_W = GLA_HEADS * GLA_DV
GLA_RANK = 16
GLA_TAU = 16.0
D_MIX = FOX_W + SSD_W + GLA_W
IN_SIZES = (FOX_W, FOX_W, FOX_W, FOX_HEADS, SSD_W, SSD_CONV_DIM, SSD_HEADS,
            GLA_KW, GLA_KW, GLA_W, GLA_RANK, GLA_W)
N_IN = sum(IN_SIZES)
IN_SPLITS = tuple(int(v) for v in np.cumsum(IN_SIZES)[:-1])

kernel_name = 'hybrid_fox_ssd_gla_decoder_step'


def rmsnorm(x, g):
    xf = x.astype(jnp.float32)
    y = xf * lax.rsqrt(jnp.mean(xf * xf, axis=-1, keepdims=True) + EPS)
    return (y * g.astype(jnp.float32)).astype(x.dtype)


def swiglu(x, w_in, w_out):
    g, u = jnp.split(x @ w_in, 2, axis=-1)
    return (jax.nn.silu(g) * u) @ w_out


def causal_conv(u, buf, w, bias):
    width = w.shape[0]
    full = jnp.concatenate([buf.astype(u.dtype), u], axis=1)
    out = lax.conv_general_dilated(full, w[:, None, :].astype(u.dtype), window_strides=(1,),
                                   padding='VALID', dimension_numbers=('NWC', 'WIO', 'NWC'),
                                   feature_group_count=u.shape[-1])
    return jax.nn.silu(out + bias), full[:, full.shape[1] - (width - 1):]


def fox_block(q, k, v, fq, fk, qpos, kpos):
    s = jnp.einsum('bqhd,bkhd->bhqk', q, k).astype(jnp.float32) * (FOX_HD ** -0.5)
    s = s + (jnp.transpose(fq, (0, 2, 1))[..., :, None] - jnp.transpose(fk, (0, 2, 1))[..., None, :])
    s = jnp.where(kpos[None, :] <= qpos[:, None], s, -jnp.inf)
    p = jax.nn.softmax(s, axis=-1).astype(v.dtype)
    return jnp.einsum('bhqk,bkhd->bqhd', p, v)


def fox_prompt(q, k, v, logf):
    b, t = q.shape[0], q.shape[1]
    F = jnp.cumsum(logf.astype(jnp.float32), axis=1)
    pos = jnp.arange(t)
    o_meta = fox_block(q[:, :N_META], k[:, :N_META], v[:, :N_META], F[:, :N_META], F[:, :N_META],
                       pos[:N_META], pos[:N_META])
    nb = (t - N_META) // BLOCK
    qb = jnp.swapaxes(q[:, N_META:].reshape(b, nb, BLOCK, FOX_HEADS, FOX_HD), 0, 1)
    fb = jnp.swapaxes(F[:, N_META:].reshape(b, nb, BLOCK, FOX_HEADS), 0, 1)
    pb = pos[N_META:].reshape(nb, BLOCK)
    o = lax.map(lambda a: fox_block(a[0], k, v, a[1], F, a[2], pos), (qb, fb, pb))
    o = jnp.swapaxes(o, 0, 1).reshape(b, nb * BLOCK, FOX_HEADS, FOX_HD)
    return jnp.concatenate([o_meta, o], axis=1)


def fox_sample(q, k, v, logf, k_past, v_past, logf_past):
    kk = jnp.concatenate([k_past.astype(k.dtype), k], axis=1)
    vv = jnp.concatenate([v_past.astype(v.dtype), v], axis=1)
    F = jnp.cumsum(jnp.concatenate([logf_past.astype(jnp.float32), logf.astype(jnp.float32)], axis=1), axis=1)
    p_len = k_past.shape[1]
    pos = jnp.arange(kk.shape[1])
    return fox_block(q, kk, vv, F[:, p_len:], F, pos[p_len:], pos)


def ssd_chunks(xdt, da, bh, ch, h, chunk):
    b, L = xdt.shape[0], xdt.shape[1]
    nc = L // chunk
    tri = jnp.tril(jnp.ones((chunk, chunk), dtype=bool))

    def blocks(a):
        return jnp.swapaxes(a.reshape((b, nc, chunk) + a.shape[2:]), 0, 1)

    def step(hc, inp):
        xc, dac, bc, cc = inp
        cs = jnp.cumsum(dac, axis=1)
        seg = jnp.where(tri[None, :, :, None], cs[:, :, None, :] - cs[:, None, :, :], -jnp.inf)
        w_ts = jnp.einsum('bthn,bshn->btsh', cc, bc) * jnp.exp(seg)
        y = jnp.einsum('btsh,bshp->bthp', w_ts, xc) + jnp.einsum('bthn,bhpn->bthp', cc, hc) * jnp.exp(cs)[..., None]
        end = cs[:, -1]
        hc = hc * jnp.exp(end)[:, :, None, None] + jnp.einsum('bshn,bsh,bshp->bhpn', bc, jnp.exp(end[:, None] - cs), xc)
        return hc, y

    h, ys = lax.scan(step, h, (blocks(xdt), blocks(da), blocks(bh), blocks(ch)))
    return jnp.swapaxes(ys, 0, 1).reshape(xdt.shape), h


def ssd_mix(x, dt, a, bm, cm, h0, segments):
    f32 = jnp.float32
    rep = SSD_HEADS // SSD_GROUPS
    xdt = x.astype(f32) * dt[..., None]
    da = dt * a
    bh = jnp.repeat(bm.astype(f32), rep, axis=2)
    ch = jnp.repeat(cm.astype(f32), rep, axis=2)
    h = h0.astype(f32)
    ys, start = [], 0
    for length, chunk in segments:
        sl = slice(start, start + length)
        y, h = ssd_chunks(xdt[:, sl], da[:, sl], bh[:, sl], ch[:, sl], h, chunk)
        ys.append(y)
        start += length
    return jnp.concatenate(ys, axis=1), h


def gla_chunks(q, k, v, g, s, chunk):
    b, L = q.shape[0], q.shape[1]
    nc = L // chunk
    tri = jnp.tril(jnp.ones((chunk, chunk), dtype=bool))

    def blocks(a):
        return jnp.swapaxes(a.reshape((b, nc, chunk) + a.shape[2:]), 0, 1)

    def step(sc, inp):
        qc, kc, vc, gc = inp
        bc = jnp.cumsum(gc, axis=1)
        diff = jnp.where(tri[None, :, :, None, None], bc[:, :, None] - bc[:, None, :], -jnp.inf)
        att = jnp.einsum('bthk,bshk,btshk->btsh', qc, kc, jnp.exp(diff))
        o = jnp.einsum('btsh,bshv->bthv', att, vc) + jnp.einsum('bthk,bhkv->bthv', qc * jnp.exp(bc), sc)
        last = bc[:, -1]
        sc = sc * jnp.exp(last)[..., None] + jnp.einsum('bshk,bshv->bhkv', kc * jnp.exp(last[:, None] - bc), vc)
        return sc, o

    s, os_ = lax.scan(step, s, (blocks(q), blocks(k), blocks(v), blocks(g)))
    return jnp.swapaxes(os_, 0, 1).reshape(v.shape), s


def gla_mix(q, k, v, g, s0, segments):
    f32 = jnp.float32
    q, k, v, g = q.astype(f32), k.astype(f32), v.astype(f32), g.astype(f32)
    s = s0.astype(f32)
    os_, start = [], 0
    for length, chunk in segments:
        sl = slice(start, start + length)
        o, s = gla_chunks(q[:, sl], k[:, sl], v[:, sl], g[:, sl], s, chunk)
        os_.append(o)
        start += length
    return jnp.concatenate(os_, axis=1), s


def trunk_layer(x, lp, conv_buf, h0, s0, segments, fox_fn):
    f32 = jnp.float32
    b, t = x.shape[0], x.shape[1]
    x = x + 0.5 * swiglu(rmsnorm(x, lp['ffn1_norm']), lp['ffn1_w_in'], lp['ffn1_w_out'])
    h = rmsnorm(x, lp['mix_norm'])
    fq, fk, fv, ff, sz, sxbc, sdt, gq, gk, gv, glr, gg = jnp.split(h @ lp['w_mix_in'], IN_SPLITS, axis=-1)
    fq = rmsnorm(fq.reshape(b, t, FOX_HEADS, FOX_HD), lp['fox_q_norm'])
    fk = rmsnorm(fk.reshape(b, t, FOX_HEADS, FOX_HD), lp['fox_k_norm'])
    fv = fv.reshape(b, t, FOX_HEADS, FOX_HD)
    logf = jax.nn.log_sigmoid((ff + lp['fox_f_bias']).astype(f32))
    fox_o = fox_fn(fq, fk, fv, logf).reshape(b, t, FOX_W)
    xbc, conv_new = causal_conv(sxbc, conv_buf, lp['ssd_conv_w'], lp['ssd_conv_b'])
    sx, sb, sc = jnp.split(xbc, (SSD_W, SSD_W + SSD_GROUPS * SSD_STATE), axis=-1)
    sx = sx.reshape(b, t, SSD_HEADS, SSD_HD)
    dt = jax.nn.softplus((sdt + lp['ssd_dt_bias']).astype(f32))
    a = -jnp.exp(lp['ssd_a_log'].astype(f32))
    y, h_new = ssd_mix(sx, dt, a, sb.reshape(b, t, SSD_GROUPS, SSD_STATE),
                       sc.reshape(b, t, SSD_GROUPS, SSD_STATE), h0, segments)
    y = (y + sx.astype(f32) * lp['ssd_d'].astype(f32)[:, None]).astype(x.dtype)
    y = y.reshape(b, t, SSD_W) * jax.nn.silu(sz)
    ssd_o = rmsnorm(y.reshape(b, t, SSD_GROUPS, SSD_W // SSD_GROUPS),
                    lp['ssd_norm'].reshape(SSD_GROUPS, SSD_W // SSD_GROUPS)).reshape(b, t, SSD_W)
    glog = jax.nn.log_sigmoid((glr @ lp['gla_w_gate'] + lp['gla_gate_bias']).astype(f32)) / GLA_TAU
    go, s_new = gla_mix(gq.reshape(b, t, GLA_HEADS, GLA_DK) * (GLA_DK ** -0.5),
                        gk.reshape(b, t, GLA_HEADS, GLA_DK), gv.reshape(b, t, GLA_HEADS, GLA_DV),
                        glog.reshape(b, t, GLA_HEADS, GLA_DK), s0, segments)
    gla_o = rmsnorm(go.astype(x.dtype), lp['gla_norm']).reshape(b, t, GLA_W) * jax.nn.silu(gg)
    x = x + jnp.concatenate([fox_o, ssd_o, gla_o], axis=-1) @ lp['w_mix_out']
    x = x + 0.5 * swiglu(rmsnorm(x, lp['ffn2_norm']), lp['ffn2_w_in'], lp['ffn2_w_out'])
    return x, (fk, fv, logf, conv_new, h_new, s_new)


def setup_inputs(seed: int = 0) -> dict:
    key = jax.random.key(seed)
    keys = iter(jax.random.split(key, 48))
    f32 = jnp.float32

    def nrm(shape, scale=1.0):
        return jax.random.normal(next(keys), shape, f32) * scale

    def gain(shape):
        return 1.0 + nrm(shape, 0.02)

    n_pages = PAST_LEN // PAGE_SIZE
    n_used = DEC_BATCH * n_pages
    n_pool = n_used + n_used // 4
    page_table = jax.random.permutation(next(keys), n_pool)[:n_used].reshape(DEC_BATCH, n_pages).astype(jnp.int32)
    dt0 = jnp.exp(jax.random.uniform(next(keys), (DEPTH, SSD_HEADS), f32, math.log(1e-3), math.log(1e-1)))
    ssd_dt_bias = dt0 + jnp.log(-jnp.expm1(-dt0))
    return {
        'x_prompt': nrm((BATCH, SEQ, D_MODEL)),
        'x_sample': nrm((DEC_BATCH, DEC_SEQ, D_MODEL)),
        'cache_fox_k': nrm((DEPTH, n_pool, PAGE_SIZE, FOX_HEADS, FOX_HD)),
        'cache_fox_v': nrm((DEPTH, n_pool, PAGE_SIZE, FOX_HEADS, FOX_HD)),
        'cache_fox_logf': jax.nn.log_sigmoid(nrm((DEPTH, n_pool, PAGE_SIZE, FOX_HEADS)) + 3.0),
        'state_ssm': nrm((DEPTH, DEC_BATCH, SSD_HEADS, SSD_HD, SSD_STATE), 0.1),
        'state_conv': nrm((DEPTH, DEC_BATCH, SSD_CONV - 1, SSD_CONV_DIM)),
        'state_gla': nrm((DEPTH, DEC_BATCH, GLA_HEADS, GLA_DK, GLA_DV), 0.3),
        'page_table': page_table,
        'meta_tokens': nrm((N_META, D_MODEL)),
        'ffn1_norm': gain((DEPTH, D_MODEL)),
        'ffn1_w_in': nrm((DEPTH, D_MODEL, 2 * D_FF), D_MODEL ** -0.5),
        'ffn1_w_out': nrm((DEPTH, D_FF, D_MODEL), D_FF ** -0.5),
        'mix_norm': gain((DEPTH, D_MODEL)),
        'w_mix_in': nrm((DEPTH, D_MODEL, N_IN), D_MODEL ** -0.5),
        'fox_q_norm': gain((DEPTH, FOX_HD)),
        'fox_k_norm': gain((DEPTH, FOX_HD)),
        'fox_f_bias': jax.random.uniform(next(keys), (DEPTH, FOX_HEADS), f32, 1.0, 4.0),
        'ssd_conv_w': nrm((DEPTH, SSD_CONV, SSD_CONV_DIM), SSD_CONV ** -0.5),
        'ssd_conv_b': nrm((DEPTH, SSD_CONV_DIM), 0.02),
        'ssd_dt_bias': ssd_dt_bias,
        'ssd_a_log': jnp.log(jax.random.uniform(next(keys), (DEPTH, SSD_HEADS), f32, 1.0, 16.0)),
        'ssd_d': gain((DEPTH, SSD_HEADS)),
        'ssd_norm': gain((DEPTH, SSD_W)),
        'gla_w_gate': nrm((DEPTH, GLA_RANK, GLA_KW), GLA_RANK ** -0.5),
        'gla_gate_bias': nrm((DEPTH, GLA_KW), 0.1),
        'gla_norm': gain((DEPTH, GLA_DV)),
        'w_mix_out': nrm((DEPTH, D_MIX, D_MODEL), D_MIX ** -0.5),
        'ffn2_norm': gain((DEPTH, D_MODEL)),
        'ffn2_w_in': nrm((DEPTH, D_MODEL, 2 * D_FF), D_MODEL ** -0.5),
        'ffn2_w_out': nrm((DEPTH, D_FF, D_MODEL), D_FF ** -0.5),
    }


def reference(x_prompt, x_sample, cache_fox_k, cache_fox_v, cache_fox_logf, state_ssm, state_conv,
              state_gla, page_table, meta_tokens, ffn1_norm, ffn1_w_in, ffn1_w_out, mix_norm, w_mix_in,
              fox_q_norm, fox_k_norm, fox_f_bias, ssd_conv_w, ssd_conv_b, ssd_dt_bias, ssd_a_log, ssd_d,
              ssd_norm, gla_w_gate, gla_gate_bias, gla_norm, w_mix_out, ffn2_norm, ffn2_w_in, ffn2_w_out):
    bp, seq = x_prompt.shape[0], x_prompt.shape[1]
    db, dseq = x_sample.shape[0], x_sample.shape[1]
    meta = jnp.broadcast_to(meta_tokens.astype(x_prompt.dtype)[None], (bp, N_META, D_MODEL))
    xp = jnp.concatenate([meta, x_prompt], axis=1)
    xs = x_sample
    seg_p = ((N_META, N_META), (seq, BLOCK))
    seg_s = ((dseq, dseq),)
    zero_conv = jnp.zeros((bp, SSD_CONV - 1, SSD_CONV_DIM), xp.dtype)
    zero_ssm = jnp.zeros((bp, SSD_HEADS, SSD_HD, SSD_STATE), jnp.float32)
    zero_gla = jnp.zeros((bp, GLA_HEADS, GLA_DK, GLA_DV), jnp.float32)
    new_p = [[] for _ in range(6)]
    new_s = [[] for _ in range(6)]
    for l in range(DEPTH):
        lp = dict(ffn1_norm=ffn1_norm[l], ffn1_w_in=ffn1_w_in[l], ffn1_w_out=ffn1_w_out[l],
                  mix_norm=mix_norm[l], w_mix_in=w_mix_in[l], fox_q_norm=fox_q_norm[l],
                  fox_k_norm=fox_k_norm[l], fox_f_bias=fox_f_bias[l], ssd_conv_w=ssd_conv_w[l],
                  ssd_conv_b=ssd_conv_b[l], ssd_dt_bias=ssd_dt_bias[l], ssd_a_log=ssd_a_log[l],
                  ssd_d=ssd_d[l], ssd_norm=ssd_norm[l], gla_w_gate=gla_w_gate[l],
                  gla_gate_bias=gla_gate_bias[l], gla_norm=gla_norm[l], w_mix_out=w_mix_out[l],
                  ffn2_norm=ffn2_norm[l], ffn2_w_in=ffn2_w_in[l], ffn2_w_out=ffn2_w_out[l])
        xp, st_p = trunk_layer(xp, lp, zero_conv, zero_ssm, zero_gla, seg_p, fox_prompt)
        k_past = cache_fox_k[l, page_table].reshape(db, -1, FOX_HEADS, FOX_HD)
        v_past = cache_fox_v[l, page_table].reshape(db, -1, FOX_HEADS, FOX_HD)
        lf_past = cache_fox_logf[l, page_table].reshape(db, -1, FOX_HEADS)
        fox_s = functools.partial(fox_sample, k_past=k_past, v_past=v_past, logf_past=lf_past)
        xs, st_s = trunk_layer(xs, lp, state_conv[l], state_ssm[l], state_gla[l], seg_s, fox_s)
        for i in range(6):
            new_p[i].append(st_p[i])
            new_s[i].append(st_s[i])
    k_p, v_p, lf_p, conv_p, ssm_p, gla_p = [jnp.stack(a) for a in new_p]
    k_s, v_s, lf_s, conv_s, ssm_s, gla_s = [jnp.stack(a) for a in new_s]
    return (xp[:, N_META:], xs, k_p, v_p, lf_p, ssm_p, conv_p, gla_p, k_s, v_s, lf_s, ssm_s, conv_s, gla_s)
```

```python
import numpy as np
import ml_dtypes
import concourse.bass as bass
import concourse.mybir as mybir
from concourse.bass_utils import run_bass_kernel_spmd

F32 = mybir.dt.float32
BF16 = mybir.dt.bfloat16
I32 = mybir.dt.int32
AF = mybir.ActivationFunctionType
ALU = mybir.AluOpType
AX = mybir.AxisListType

CFG = dict(NCH=16, NS=16, NPG=16, NPOOL=2560, NCORES=8, DEPTH=2, STOP=99, MIXER=True, SAMPLE=True, GATHER=True)
D = 1024
KC = 8
DFF = 2816
GSZ = 2
NIN = 2844
EPS = 1e-6
ENG = ('pe', 'act', 'dve', 'pool', 'sp')
SAME_SYNC = True


class Res:
    __slots__ = ('w', 'r', 'excl')

    def __init__(self, excl=False):
        self.w = None
        self.r = {}
        self.excl = excl


class Sched:
    def __init__(self, ndma=8):
        self.q = {e: [] for e in ENG}
        self.cnt = {e: 0 for e in ENG}
        self.seen = {e: {} for e in ENG}
        self.ndma = ndma
        self.dcnt = {qn: [0] * ndma for qn in ('sp', 'pool', 'act')}
        self.drr = {qn: 0 for qn in ('sp', 'pool', 'act')}

    def _waits(self, eng, toks):
        need = {}
        for (k, v) in toks:
            if v <= 0:
                continue
            if k == eng and (eng == 'pe' or not SAME_SYNC):
                continue
            if need.get(k, 0) < v:
                need[k] = v
        out = []
        for k, v in need.items():
            if self.seen[eng].get(k, 0) >= v:
                continue
            self.seen[eng][k] = v
            out.append((k, v))
        return out

    def _collect(self, reads, writes):
        toks = []
        for r in reads:
            if r.w is not None:
                toks.append(r.w)
        for w in writes:
            if w.w is not None:
                toks.append(w.w)
            toks.extend(w.r.items())
        return toks

    def _commit(self, tok, reads, writes):
        for r in reads:
            if r.r.get(tok[0], 0) < tok[1]:
                r.r[tok[0]] = tok[1]
        for w in writes:
            w.w = tok
            w.r = {}

    def op(self, eng, fn, reads=(), writes=()):
        ex = [r for r in reads if r.excl]
        if ex:
            reads = [r for r in reads if not r.excl]
            writes = list(writes) + ex
        toks = self._collect(reads, writes)
        self.cnt[eng] += 1
        tok = (eng, self.cnt[eng])
        self.q[eng].append((self._waits(eng, toks), fn, tok, 1))
        self._commit(tok, reads, writes)
        return tok

    def dma(self, qn, fn, reads=(), writes=()):
        k = self.drr[qn]
        self.drr[qn] = (k + 1) % self.ndma
        key = 'd_%s_%d' % (qn, k)
        toks = self._collect(reads, writes)
        toks.append((key, self.dcnt[qn][k]))
        self.dcnt[qn][k] += 16
        tok = (key, self.dcnt[qn][k])
        self.q[qn].append((self._waits(qn, toks), fn, tok, 16))
        self._commit(tok, reads, writes)
        return tok

    def barrier(self):
        alltoks = [(e, self.cnt[e]) for e in ENG]
        for qn in self.dcnt:
            for k in range(self.ndma):
                alltoks.append(('d_%s_%d' % (qn, k), self.dcnt[qn][k]))
        for e in ENG:
            w = self._waits(e, alltoks)
            if w:
                self.q[e].append((w, None, None, 0))

    def sem_keys(self):
        ks = ['pe', 'act', 'dve', 'pool']
        for qn in self.dcnt:
            for k in range(self.ndma):
                ks.append('d_%s_%d' % (qn, k))
        return ks


def MMS(lst, explicit=False):
    def f(e):
        ins = None
        for ent in lst:
            (o, l, r, st, sp) = ent[:5]
            if explicit or (len(ent) > 5 and ent[5] is not None):
                base = ent[5] if len(ent) > 5 else 0
                ins = e.matmul(o, l, r, start=st, stop=sp, tile_position=(base, 0))
            else:
                ins = e.matmul(o, l, r, start=st, stop=sp)
        return ins
    return f


def MMX(lst):
    return MMS(lst, explicit=False)


def MM(o, l, r, st=True, sp=True):
    return MMS([(o, l, r, st, sp)])


def MMx(o, l, r, st=True, sp=True):
    return MMS([(o, l, r, st, sp)], explicit=False)


def TRS(lst):
    def f(e):
        ins = None
        for (o, i, idn) in lst:
            ins = e.transpose(o, i, idn)
        return ins
    return f


def ACTF(out, in_, func, bias=None, scale=None):
    def f(e):
        kw = {}
        if bias is not None:
            kw['bias'] = bias
        if scale is not None:
            kw['scale'] = scale
        return e.activation(out=out, in_=in_, func=func, **kw)
    return f


def TT(out, a, b, op):
    return lambda e: e.tensor_tensor(out=out, in0=a, in1=b, op=op)


def TS(out, a, s1, op0, s2=None, op1=None):
    def f(e):
        if op1 is None:
            return e.tensor_scalar(out=out, in0=a, scalar1=s1, scalar2=None, op0=op0)
        return e.tensor_scalar(out=out, in0=a, scalar1=s1, scalar2=s2, op0=op0, op1=op1)
    return f


def STT(out, in0, scalar, in1, op0, op1):
    return lambda e: e.scalar_tensor_tensor(out=out, in0=in0, scalar=scalar, in1=in1, op0=op0, op1=op1)


def CP(out, in_):
    def f(e):
        if hasattr(e, 'tensor_copy'):
            return e.tensor_copy(out=out, in_=in_)
        return e.activation(out=out, in_=in_, func=AF.Copy)
    return f


def RECIP(out, in_):
    return lambda e: e.reciprocal(out=out, in_=in_)


def RED(out, in_, op=None):
    return lambda e: e.tensor_reduce(out=out, in_=in_, axis=AX.X, op=(op or ALU.add))


def SCAN(out, d0, d1):
    return lambda e: e.tensor_tensor_scan(out=out, data0=d0, data1=d1, initial=0.0, op0=ALU.mult, op1=ALU.add)


def MSET(ap, v):
    return lambda e: e.memset(ap, v)


def IDMA(out, in_, idx):
    return lambda e: e.indirect_dma_start(out=out, out_offset=None, in_=in_, in_offset=bass.IndirectOffsetOnAxis(ap=idx, axis=0))


def DMA(out, in_, nonc=False):
    def f(e):
        if nonc:
            return e.dma_start(out=out, in_=in_, allow_slow_non_contiguous=True)
        return e.dma_start(out=out, in_=in_)
    return f


def token_tiles(tok):
    t = []
    c = 0
    while c < tok:
        n = min(512, tok - c)
        t.append((c, n))
        c += n
    return t


O_FQ, O_FK, O_FV, O_FF, O_SZ, O_XBC, O_DT, O_GQ, O_GK, O_GV, O_LR, O_GG = (
    0, 256, 512, 768, 772, 1284, 2052, 2060, 2188, 2316, 2572, 2588)


def make_consts(NS, NPG=16):
    c = {}
    c['ident'] = np.eye(128, dtype=np.float32)
    c['ones'] = np.ones((128, 128), np.float32)
    k = np.arange(128)
    U = (k[:, None] <= k[None, :]).astype(np.float32)
    c['U'] = U
    c['LT'] = np.ascontiguousarray(U.T)
    c['negm'] = np.where(U > 0, 0.0, -30000.0).astype(np.float32)
    c['m01'] = U.copy()
    ns4 = NS * 4
    s = np.arange(128)
    same = (s[:, None] // 4 == s[None, :] // 4)
    Us = (same & (s[:, None] <= s[None, :])).astype(np.float32)
    pad = lambda a: np.pad(a, ((0, 128 - a.shape[0]), (0, 128 - a.shape[1])))
    c['Us'] = pad(Us)
    c['blk'] = pad(same.astype(np.float32))
    c['negms'] = pad(np.where(Us > 0, 0.0, -30000.0).astype(np.float32))
    c['m01s'] = pad(Us)
    c['bd64'] = (k[:, None] // 64 == k[None, :] // 64).astype(np.float32)
    order = (k % NPG) * 8 + k // NPG
    c['Mgt'] = (k[:, None] > k[None, :]).astype(np.float32)
    c['radd'] = k.astype(np.float32).reshape(128, 1)
    hm = np.zeros((128, 4), np.float32)
    hm[k, k // 32] = 1.0
    c['hm'] = hm
    sm = np.zeros((128, NS, ns4), np.float32)
    for b in range(NS):
        sm[:, b, 4 * b:4 * b + 4] = 1.0
    c['seqmask'] = sm.reshape(128, NS * ns4)
    rm = np.zeros((128, 32), np.float32)
    rm[s, s // 4] = 1.0
    c['rowmask'] = rm[:, :NS].copy()
    c['rowmask32'] = rm
    rs = np.ones((128, 128), np.float32)
    rs[:, 0::4] = 0.0
    c['rst'] = rs[:, :ns4].copy()
    c['rst128'] = rs
    s = np.arange(ns4)
    sel = np.zeros((128, 128), np.float32)
    sel[:ns4] = ((s[:, None] // 4) % 2 == (k[None, :] // 64)).astype(np.float32)
    c['sel'] = sel
    pm = np.zeros((128, max(NS // 2, 1)), np.float32)
    pm[s, s // 8] = 1.0
    c['pairmask'] = pm
    return c


CONST_ORDER = ['ident', 'ones', 'U', 'negm', 'm01', 'bd64']
CONST_S = ['Us', 'blk', 'negms', 'Mgt']


def build(cfg):
    NCH, NS, NPG, NPOOL, DEPTH = cfg['NCH'], cfg['NS'], cfg['NPG'], cfg['NPOOL'], cfg['DEPTH']
    STOP = cfg.get('STOP', 99)
    MSTEP = cfg.get('MSTEP', 99)
    MIXER = cfg.get('MIXER', False)
    NP = NCH * 128
    NS4 = NS * 4
    TP = 16 + NP
    TOK = TP + NS4
    NBP = NS // 2
    nc = bass.Bass("TRN2", target_bir_lowering=False)
    S = Sched()

    def din(name, shape, dt=F32):
        return nc.dram_tensor(name, list(shape), dt, kind="ExternalInput").ap()

    def dout(name, shape, dt=F32):
        return nc.dram_tensor(name, list(shape), dt, kind="ExternalOutput").ap()

    xp_d = din('xp', [NP, D])
    xs_d = din('xs', [NS4, D])
    meta_d = din('meta', [16, D])
    if cfg.get('GATHER', False):
        ck_d = din('ck', [DEPTH, NPOOL * 128, 256])
        cv_d = din('cv', [DEPTH, NPOOL * 128, 256])
        clf_d = din('clf', [DEPTH, NPOOL * 128, 4])
    sssm_d = din('sssm', [DEPTH, NS, 8, 64, 64])
    sconv_d = din('sconv', [DEPTH, NS * 3, 768])
    sgla_d = din('sgla', [DEPTH, NS, 128, 64])
    pt_d = din('pt', [NS, NPG], I32)
    w1i_d = din('w1i', [DEPTH, D, 2 * DFF])
    w1o_d = din('w1o', [DEPTH, DFF, D])
    w2i_d = din('w2i', [DEPTH, D, 2 * DFF])
    w2o_d = din('w2o', [DEPTH, DFF, D])
    wmi_d = din('wmi', [DEPTH, D, NIN])
    wmo_d = din('wmo', [DEPTH, D, D])
    gains_d = din('gains', [DEPTH, 3, D])
    fqn_d = din('fqn', [DEPTH, 64])
    fkn_d = din('fkn', [DEPTH, 64])
    fb_d = din('fb', [DEPTH, 4])
    cw_d = din('cw', [DEPTH, 4, 768])
    cb_d = din('cb', [DEPTH, 768])
    dtb_d = din('dtb', [DEPTH, 8])
    alog_d = din('alog', [DEPTH, 8])
    sd_d = din('sd', [DEPTH, 8])
    sn_d = din('sn', [DEPTH, 512])
    wg_d = din('wg', [DEPTH, 16, 128])
    gb_d = din('gb', [DEPTH, 128])
    gn_d = din('gn', [DEPTH, 64])
    SAMPLE = cfg.get('SAMPLE', False)
    CORD = CONST_ORDER + (CONST_S if SAMPLE else [])
    cst_d = din('cst', [len(CORD), 128, 128])
    GATHER = cfg.get('GATHER', False)
    if GATHER:
        radd_d = din('radd', [128, 1])
    if SAMPLE:
        rowm32_d = din('rowm32', [128, 32])
        rst128_d = din('rst128', [128, 128])
    hm_d = din('hmc', [128, 4])
    seqm_d = din('seqm', [128, NS * NS4])
    rowm_d = din('rowm', [128, NS])
    rst_d = din('rstm', [128, NS4])
    pairm_d = din('pairm', [128, max(NBP, 1)])

    yp_d = dout('yp', [NP, D])
    ys_d = dout('ys', [NS4, D])
    kp_d = dout('kp', [DEPTH, TP, 256])
    vp_d = dout('vp', [DEPTH, TP, 256])
    lfp_d = dout('lfp', [DEPTH, TP, 4])
    ssmp_d = dout('ssmp', [DEPTH, 512, 64])
    convp_d = dout('convp', [DEPTH, 3, 768])
    glap_d = dout('glap', [DEPTH, 128, 64])
    ks_d = dout('ks', [DEPTH, NS4, 256])
    vs_d = dout('vs', [DEPTH, NS4, 256])
    lfs_d = dout('lfs', [DEPTH, NS4, 4])
    ssms_d = dout('ssms', [DEPTH, NS, 8, 64, 64])
    convs_d = dout('convs', [DEPTH, NS, 3, 768])
    glas_d = dout('glas', [DEPTH, NS, 128, 64])
    xsp_d = nc.dram_tensor('xspill', [128, 8, TOK], F32, kind="Internal").ap()

    SB_BASE, SB_END = 16512, cfg.get("SB_END", 229376 - 256)
    st = {'p': SB_BASE, 'n': 0, 'peak': SB_BASE}

    def sb(shape, dt=F32, at=None):
        if at is not None:
            st['n'] += 1
            return nc.alloc_sbuf_tensor_at('t%d' % st['n'], list(shape), dt, offset=(SB_BASE if cfg.get('DRY') else at))
        nbytes = int(np.prod(shape[1:])) * (2 if dt == BF16 else 4)
        off = (st['p'] + 63) // 64 * 64
        st['last'] = off
        st['p'] = off + nbytes
        st['peak'] = max(st['peak'], st['p'])
        if cfg.get('DRY'):
            st['n'] += 1
            return nc.alloc_sbuf_tensor_at('t%d' % st['n'], list(shape), dt, offset=SB_BASE)
        assert st['p'] <= SB_END, ('SBUF overflow', st['p'] - SB_END)
        st['n'] += 1
        import sys as _s
        return nc.alloc_sbuf_tensor_at('t%d_%d' % (st['n'], _s._getframe(1).f_lineno), list(shape), dt, offset=off)

    PS = [nc.alloc_psum_tensor('ps%d' % i, [128, 512], F32) for i in range(8)]
    PR = [Res(excl=True) for _ in range(8)]

    xT = sb([128, 8, TOK])
    xR = None
    cst = sb([128, len(CORD), 128])
    C = {n: cst[:, i, :] for i, n in enumerate(CORD)}
    ones_b = sb([128, 128], BF16)
    bd64_b = sb([128, 128], BF16)
    hm = sb([128, 4])
    rowm = sb([128, NS])
    rstm = sb([128, NS4])
    pairm = sb([128, max(NBP, 1)])
    gains = sb([128, DEPTH * 3, 8])
    cvals = sb([128, 4])
    constR = Res()

    S.dma('sp', DMA(cst[:, :, :], cst_d.rearrange("c p n -> p c n")), [], [constR])
    S.dma('pool', DMA(ones_b[:, :], cst_d[CONST_ORDER.index('ones')]), [], [constR])
    S.dma('pool', DMA(bd64_b[:, :], cst_d[CONST_ORDER.index('bd64')]), [], [constR])
    S.dma('sp', DMA(hm[:, :], hm_d), [], [constR])
    S.dma('sp', DMA(rowm[:, :], rowm_d), [], [constR])
    S.dma('sp', DMA(rstm[:, :], rst_d), [], [constR])
    S.dma('sp', DMA(pairm[:, :], pairm_d), [], [constR])
    if SAMPLE:
        rowm32 = sb([128, 32]); rst128 = sb([128, 128])
        S.dma('sp', DMA(rowm32[:, :], rowm32_d), [], [constR])
        S.dma('sp', DMA(rst128[:, :], rst128_d), [], [constR])
    S.dma('sp', DMA(gains[:, :, :], gains_d.rearrange("l w (c p) -> p (l w) c", p=128), nonc=True), [], [constR])
    S.op('dve', MSET(cvals[:, 0:1], EPS), [], [constR])
    S.op('dve', MSET(cvals[:, 1:2], 1.0), [], [constR])
    S.op('dve', MSET(cvals[:, 2:3], 0.0), [], [constR])
    epsc = cvals[:, 0:1]
    onec = cvals[:, 1:2]
    ident = C['ident']

    mark_persist = st['p']

    def load_phase():
        xin = [sb([128, D]) for _ in range(2)]
        xinR = [Res(), Res()]
        groups = [(meta_d, 16, 0)]
        for c in range(NCH):
            groups.append((xp_d[c * 128:(c + 1) * 128, :], 128, 16 + c * 128))
        groups.append((xs_d, NS4, TP))
        for gi, (src, L, col0) in enumerate(groups):
            b = gi % 2
            S.dma('sp', DMA(xin[b][:L, :], src), [], [xinR[b]])
            for half in range(2):
                pb = 2 * (gi % 2) + half
                S.op('pe', TRS([(PS[pb][:, j * L:(j + 1) * L], xin[b][:L, (half * 4 + j) * 128:(half * 4 + j + 1) * 128],
                                 ident[:L, :L]) for j in range(4)]), [xinR[b], constR], [PR[pb]])
                S.op('act' if half else 'dve',
                     CP(xT[:, half * 4:half * 4 + 4, col0:col0 + L],
                        PS[pb][:, 0:4 * L].rearrange("p (j l) -> p j l", l=L)), [PR[pb]], [xTall])

    xTall = Res()

    def norm_T(src3, n, gidx, dst3, sq, tmp, rstd, psb, rd, wr):
        S.op('act', ACTF(sq[:, :, :n], src3, AF.Square), rd, [sqR])
        S.op('pe', MMS([(PS[psb][:, :n], ones_b[:, :], sq[:, k, :n], k == 0, k == 7) for k in range(8)]),
             [sqR, constR], [PR[psb]])
        S.op('act', ACTF(rstd[:, :n], PS[psb][:, :n], AF.Sqrt, bias=epsc, scale=1.0 / D), [PR[psb], constR], [rstdR])
        S.op('dve', RECIP(rstd[:, :n], rstd[:, :n]), [rstdR], [rstdR])
        S.op('dve', TT(tmp[:, :, :n], src3, rstd[:, None, :n].to_broadcast([128, 8, n]), ALU.mult),
             list(rd) + [rstdR], [tmpR])
        S.op('dve', TT(dst3, tmp[:, :, :n], gains[:, gidx, :, None].to_broadcast([128, 8, n]), ALU.mult),
             [tmpR, constR], wr)

    sqR, rstdR, tmpR = Res(), Res(), Res()

    def ffn_phase(l, which):
        st['p'] = mark_persist
        wi_d = (w1i_d if which == 0 else w2i_d)[l]
        wo_d = (w1o_d if which == 0 else w2o_d)[l]
        gidx = l * 3 + (0 if which == 0 else 2)
        tiles = token_tiles(TOK)
        NT = len(tiles)
        hT = sb([128, 8, TOK], BF16)
        hR = [Res() for _ in tiles]
        xtR = [Res() for _ in tiles]
        sq = sb([128, 8, 512], BF16)
        tmp = sb([128, 8, 512])
        rstd = sb([128, 512])
        NB = 3
        wi = [sb([128, 8, 512], BF16) for _ in range(NB)]
        wo = [sb([128, GSZ, D], BF16) for _ in range(NB)]
        wR = [Res() for _ in range(NB)]
        act = [sb([128, GSZ, TOK], BF16) for _ in range(2)]
        actR = [[Res() for _ in tiles] for _ in range(2)]
        sg = [sb([128, 512], BF16) for _ in range(2)]
        sgR = [Res(), Res()]
        for ti, (c0, n) in enumerate(tiles):
            norm_T(xT[:, :, c0:c0 + n], n, gidx, hT[:, :, c0:c0 + n], sq, tmp, rstd, 6, [xtR[ti]], [hR[ti]])
        NG = DFF // (128 * GSZ)
        wi_v = wi_d.rearrange("(k p) n -> p k n", p=128)
        wo_v = wo_d.rearrange("(j p) n -> p j n", p=128)
        cnt = 0
        for g in range(NG):
            b = g % NB
            S.dma('pool', DMA(wi[b][:, :, 0:256], wi_v[:, :, g * 256:(g + 1) * 256]), [], [wR[b]])
            S.dma('pool', DMA(wi[b][:, :, 256:512], wi_v[:, :, DFF + g * 256:DFF + (g + 1) * 256]), [], [wR[b]])
            S.dma('pool', DMA(wo[b][:, :, :], wo_v[:, g * GSZ:(g + 1) * GSZ, :]), [], [wR[b]])
            ab = g % 2
            for j in range(GSZ):
                for ti, (c0, n) in enumerate(tiles):
                    pg, pu = cnt % 2, 2 + cnt % 2
                    sb_ = cnt % 2
                    cnt += 1
                    S.op('pe', MMS([(PS[pg][:, :n], wi[b][:, k, j * 128:(j + 1) * 128], hT[:, k, c0:c0 + n], k == 0, k == 7)
                                    for k in range(8)]), [wR[b], hR[ti]], [PR[pg]])
                    S.op('pe', MMS([(PS[pu][:, :n], wi[b][:, k, 256 + j * 128:256 + (j + 1) * 128], hT[:, k, c0:c0 + n],
                                     k == 0, k == 7) for k in range(8)]), [wR[b], hR[ti]], [PR[pu]])
                    S.op('act', ACTF(sg[sb_][:, :n], PS[pg][:, :n], AF.Silu), [PR[pg]], [sgR[sb_]])
                    S.op('dve', TT(act[ab][:, j, c0:c0 + n], PS[pu][:, :n], sg[sb_][:, :n], ALU.mult),
                         [PR[pu], sgR[sb_]], [actR[ab][ti]])
            oc_cnt = 0
            for oc in range(8):
                for ti, (c0, n) in enumerate(tiles):
                    po = 4 + oc_cnt % 2
                    oc_cnt += 1
                    S.op('pe', MMS([(PS[po][:, :n], wo[b][:, j, oc * 128:(oc + 1) * 128], act[ab][:, j, c0:c0 + n],
                                     j == 0, j == GSZ - 1) for j in range(GSZ)]), [wR[b], actR[ab][ti]], [PR[po]])
                    S.op('dve', STT(xT[:, oc, c0:c0 + n], PS[po][:, :n], 0.5, xT[:, oc, c0:c0 + n], ALU.mult, ALU.add),
                         [PR[po]], [xtR[ti]])
        S.barrier()

    def store_phase():
        st['p'] = mark_persist
        xo = [sb([128, D]) for _ in range(2)]
        xoR = [Res(), Res()]
        groups = []
        for c in range(NCH):
            groups.append((yp_d[c * 128:(c + 1) * 128, :], 128, 16 + c * 128))
        groups.append((ys_d, NS4, TP))
        for gi, (dst, L, col0) in enumerate(groups):
            b = gi % 2
            for half in range(2):
                pb = 2 * (gi % 2) + half
                S.op('pe', TRS([(PS[pb][:L, j * 128:(j + 1) * 128], xT[:, half * 4 + j, col0:col0 + L], ident)
                                for j in range(4)]), [constR], [PR[pb]])
                S.op('act' if half else 'dve', CP(xo[b][:L, half * 512:(half + 1) * 512], PS[pb][:L, :]),
                     [PR[pb]], [xoR[b]])
            S.dma('sp', DMA(dst, xo[b][:L, :]), [xoR[b]], [])


    def mixer_phase(l):
        st['p'] = mark_persist
        R_ = Res
        Wmi = sb([128, 8, NIN], BF16)
        wo_t = [sb([128, 8, 128], BF16) for _ in range(2)]
        woR = [R_(), R_()]
        wmo_v = wmo_d[l].rearrange("(k p) n -> p k n", p=128)
        wRm = R_()
        wmi_v = wmi_d[l].rearrange("(k p) n -> p k n", p=128)
        for c0_ in range(0, NIN, 948):
            S.dma('pool', DMA(Wmi[:, :, c0_:c0_ + 948], wmi_v[:, :, c0_:c0_ + 948]), [], [wRm])
        prm = R_()
        gcol4 = sb([128, 4])
        for c4, src in ((0, fqn_d), (1, fqn_d), (2, fkn_d), (3, fkn_d)):
            for hf in range(2):
                S.dma('sp', DMA(gcol4[hf * 64:(hf + 1) * 64, c4:c4 + 1], src[l].rearrange("(p o) -> p o", o=1), nonc=True),
                      [], [prm])
        fb_bc = sb([128, 4]); S.dma('sp', DMA(fb_bc[:, :], fb_d[l].partition_broadcast(128)), [], [prm])
        cw = sb([128, 4, 6])
        for tp_ in range(4):
            S.dma('sp', DMA(cw[:, tp_, :], cw_d[l, tp_].rearrange("(j p) -> p j", p=128), nonc=True), [], [prm])
        cbb = sb([128, 6]); S.dma('sp', DMA(cbb[:, :], cb_d[l].rearrange("(j p) -> p j", p=128), nonc=True), [], [prm])
        dtb_bc = sb([128, 8]); S.dma('sp', DMA(dtb_bc[:, :], dtb_d[l].partition_broadcast(128)), [], [prm])
        a_bc = sb([128, 8]); S.dma('sp', DMA(a_bc[:, :], alog_d[l].partition_broadcast(128)), [], [prm])
        sd_bc = sb([128, 8]); S.dma('sp', DMA(sd_bc[:, :], sd_d[l].partition_broadcast(128)), [], [prm])
        sn_bc = sb([128, 512]); S.dma('sp', DMA(sn_bc[:, :], sn_d[l].partition_broadcast(128)), [], [prm])
        gn_bc = sb([128, 64]); S.dma('sp', DMA(gn_bc[:, :], gn_d[l].partition_broadcast(128)), [], [prm])
        Wg = sb([128, 128]); S.dma('sp', DMA(Wg[:16, :], wg_d[l]), [], [prm])
        ngb = sb([128, 1]); S.dma('sp', DMA(ngb[:, :], gb_d[l].rearrange("(p o) -> p o", o=1), nonc=True), [], [prm])
        S.op('act', ACTF(a_bc[:, :], a_bc[:, :], AF.Exp), [prm], [prm])
        S.op('dve', TS(a_bc[:, :], a_bc[:, :], -1.0, ALU.mult), [prm], [prm])
        S.op('dve', TS(ngb[:, :], ngb[:, :], -1.0, ALU.mult), [prm], [prm])

        NCK = NCH + 1
        kT_all = sb([128, 2, TP], BF16); off_kT = st['last']
        V_all = sb([128, NCK, 4, 80], BF16); R1 = (off_kT, st['p'] - off_kT)
        F_all = sb([128, NCK, 4])
        Ftot = sb([128, 4])
        hstT = sb([128, 4, 64])
        Sg = sb([128, 64])
        Sg_b = sb([128, 64], BF16)
        kTR, VR, FR, FtR, hsR, SgR = R_(), R_(), R_(), R_(), R_(), R_()
        S.op('dve', MSET(V_all[:, :, :, 64:65], 1.0), [], [VR])
        S.op('dve', MSET(Ftot[:, :], 0.0), [], [FtR])
        S.op('dve', MSET(hstT[:, :, :], 0.0), [], [hsR])
        S.op('dve', MSET(Sg[:, :], 0.0), [], [SgR])
        S.op('dve', MSET(Sg_b[:, :], 0.0), [], [SgR])
        S.op('dve', MSET(F_all[:, :, :], 0.0), [], [FR])

        LM = 128
        sq = sb([128, 8, LM], BF16); mixoN = sb([128, 1024]); tmp = sb([128, 8, LM], at=st['last']); rstd = sb([128, LM]); hc = sb([128, 8, LM], BF16)
        qk4 = sb([128, 4, LM]); sq4 = sb([128, 4, LM], BF16); ytmp = sb([128, 512]); ytmp_off = st['last']; rstd4 = sb([128, 4, LM], at=st['last'])
        qT = sb([128, 2, 2, LM], BF16); kN = sb([128, 256]); vN = sb([128, 260])
        t4 = sb([128, 4]); lf = sb([128, 4]); nb_all = sb([128, NCK, 4])
        Pt = [sb([128, 4, LM], BF16) for _ in range(2)]
        p0_ = (st['p'] + 63) // 64 * 64
        cin = [sb([128, 6, 3 + LM]) for _ in range(2)]
        R2 = (p0_, st['p'] - p0_)
        acc = sb([128, 6, LM]); off_acc = st['last']; xbcT = sb([128, 6, LM]); tmpc = xbcT; bcT_b = sb([128, 2, LM], BF16); Cblk_f = sb([128, 2, LM]); Cblk_b = sb([128, 2, LM], BF16)
        zs = sb([128, 512]); dtgv = sb([128, 264]); zg = sb([128, 256]); glr = sb([128, LM]); gqk = sb([128, 2, LM])
        dt = sb([128, 8]); da = sb([128, 8]); cs = sb([128, 8]); negcs = sb([128, 8]); wdec = sb([128, 8])
        expcs = sb([128, 8]); edec = sb([128, 8]); t8 = sb([128, 8])
        daU = sb([128, 8, LM]); off_daU = st['last']; Em = daU; MT = sb([128, 8, LM], BF16)
        xN = sb([128, 512]); BN = sb([128, 128]); xdt = sb([128, 8, 64], BF16); Bw = sb([128, 4, 2, 64], BF16); xdt2 = sb([128, 4, 2, 64], BF16)
        yy = sb([128, 512]); off_yy = st['last']; ss2 = sb([128, 2]); hst_t = sb([128, 4, 64])
        lg = sb([128, LM]); cl = sb([128, LM]); eq = sb([128, LM]); ek = sb([128, LM]); dl = sb([128, LM])
        qt = sb([128, LM], BF16); kt = sb([128, LM], BF16); khT = sb([128, LM]); qblk = sb([128, 4, LM], BF16)
        attm = sb([128, 4, LM], BF16); vg = sb([128, 256], BF16); khN = sb([128, 128], BF16)
        tmpg = sb([128, 4, 64]); red = sb([128, 64]); elast = sb([128, 1]); ss4 = sb([128, 4]); og = kN
        mixoT = sq; rec4 = sb([128, 4]); sso = sb([128, 4, 128], at=ytmp_off)
        T = {}
        T['mixoN'] = tmpR
        T['mixoT'] = sqR

        def tr(name):
            if name not in T:
                T[name] = R_()
            return T[name]

        T['sso'] = tr('ytmp'); T['rstd4'] = tr('ytmp'); T['og'] = tr('kN')
        xcR = R_()

        def sample_chunk():
            L = 128
            col0 = TP
            Us_, BLK, NEGMS, M01S = C['Us'], C['blk'], C['negms'], C['Us']
            v0 = lambda b, n: PS[b][:, 0:n * L].rearrange("p (j l) -> p j l", l=L)
            def region(off, size):
                reg = {'p': off, 'end': off + size}
                def alloc(shape, dt=F32):
                    nbytes = int(np.prod(shape[1:])) * (2 if dt == BF16 else 4)
                    o = (reg['p'] + 63) // 64 * 64
                    if o + nbytes <= reg['end']:
                        reg['p'] = o + nbytes
                        return sb(shape, dt, at=o)
                    return sb(shape, dt)
                return alloc
            a1 = region(*R1); a2 = region(*R2)
            cin_s = a1([128, 6, 32, 7])
            hN = sb([128, 768], at=off_daU); convN = hN; T['hN'] = tr('daU'); T['convN'] = tr('daU')
            edec_all = a2([128, 16, 8]); elast_all = a2([128, 32])
            kTs = a2([128, 2, L], BF16); Vs = a2([128, 4, 80], BF16); Fs = a2([128, 4])
            h0N = [a1([128, 8, 64])]; hTb = [a1([128, 4, 64])]
            ysum = sb([128, 512], at=off_acc); T['ysum'] = tr('acc')
            osum = kN; T['osum'] = tr('kN')
            tmpy = ytmp; T['tmpy'] = tr('ytmp')
            Sg0 = [a1([128, 64]) for _ in range(2)]; Sg0b = [a1([128, 64], BF16) for _ in range(2)]
            Bwb = a1([128, 4, 2, 64], BF16); khNb = a1([128, 128], BF16)
            newT = a1([128, 4, 64]); sso2 = sb([128, 4, 128], at=off_yy); T['sso2'] = tr('yy'); og2 = a1([128, 256])
            S.op('dve', MSET(cin_s[:, :, :, :], 0.0), [], [tr('cin_s')])
            S.op('dve', MSET(Vs[:, :, 64:65], 1.0), [], [tr('Vs')])
            S.dma('sp', DMA(hN[:NS * 3, :], sconv_d[l]), [], [tr('hN')])
            S.op('pe', TRS([(PS[0][:, j * 64:j * 64 + NS * 3], hN[:NS * 3, j * 128:(j + 1) * 128], ident[:NS * 3, :NS * 3])
                            for j in range(6)]), [tr('hN'), constR], [PR[0]])
            S.op('dve', CP(cin_s[:, :, 0:NS, 0:3],
                           PS[0][:, 0:384].rearrange("p (j x) -> p j x", x=64)[:, :, 0:NS * 3].rearrange("p j (b r) -> p j b r", r=3)),
                 [PR[0]], [tr('cin_s')])
            norm_T(xT[:, :, col0:col0 + NS4], NS4, l * 3 + 1, hc[:, :, :NS4], sq, tmp, rstd, 7, [xcR], [tr('hc')])
            S.op('dve', MSET(hc[:, :, NS4:L], 0.0), [], [tr('hc')])
            def tproj(bank, idx, c0, M=128):
                S.op('pe', MMX([(PS[bank][:M, idx * L:(idx + 1) * L], Wmi[:, k, c0:c0 + M], hc[:, k, :L], k == 0, k == 7)
                                for k in range(8)]), [wRm, tr('hc')], [PR[bank]])
            def nproj(bank, o0, c0, n):
                S.op('pe', MMX([(PS[bank][:L, o0:o0 + n], hc[:, k, :L], Wmi[:, k, c0:c0 + n], k == 0, k == 7)
                                for k in range(8)]), [wRm, tr('hc')], [PR[bank]])
            for i, c0 in enumerate((O_FQ, O_FQ + 128, O_FK, O_FK + 128)):
                tproj(0, i, c0)
            for i in range(4):
                tproj(1, i, O_XBC + i * 128)
            for i, c0 in enumerate((O_XBC + 512, O_XBC + 640, O_GQ, O_GK)):
                tproj(2, i, c0)
            nproj(3, 0, O_FV, 260)
            nproj(4, 0, O_SZ, 512)
            nproj(5, 0, O_DT, 8)
            nproj(5, 8, O_GV, 256)
            S.op('pe', MMX([(PS[5][:16, 320:320 + L], Wmi[:, k, O_LR:O_LR + 16], hc[:, k, :L], k == 0, k == 7)
                            for k in range(8)]), [wRm, tr('hc')], [PR[5]])
            nproj(6, 0, O_GG, 256)
            S.op('dve', CP(qk4[:, :, :L], v0(0, 4)), [PR[0]], [tr('qk4')])
            S.op('act', ACTF(sq4[:, :, :L], qk4[:, :, :L], AF.Square), [tr('qk4')], [tr('sq4')])
            S.op('act', CP(cin_s[:, 0:4, :, 3:7], v0(1, 4).rearrange("p j (b t) -> p j b t", t=4)), [PR[1]], [tr('cin_s')])
            S.op('dve', CP(cin_s[:, 4:6, :, 3:7], v0(2, 2).rearrange("p j (b t) -> p j b t", t=4)), [PR[2]], [tr('cin_s')])
            S.op('dve', CP(gqk[:, :, :L], PS[2][:, 2 * L:4 * L].rearrange("p (j l) -> p j l", l=L)), [PR[2]], [tr('gqk')])
            S.op('act', CP(vN[:L, :], PS[3][:L, 0:260]), [PR[3]], [tr('vN')])
            S.op('act', ACTF(zs[:L, :], PS[4][:L, :], AF.Silu), [PR[4]], [tr('zs')])
            S.op('dve', CP(dtgv[:L, :], PS[5][:L, 0:264]), [PR[5]], [tr('dtgv')])
            S.op('dve', CP(glr[:16, :L], PS[5][:16, 320:320 + L]), [PR[5]], [tr('glr')])
            S.op('act', ACTF(zg[:L, :], PS[6][:L, 0:256], AF.Silu), [PR[6]], [tr('zg')])
            S.op('pe', MMx(PS[7][:, 0:4 * L], bd64_b[:, :], sq4[:, :, :L]), [tr('sq4'), constR], [PR[7]])
            S.op('act', ACTF(rstd4[:, :, :L], v0(7, 4), AF.Sqrt, bias=epsc, scale=1.0 / 64), [PR[7], constR], [tr('rstd4')])
            S.op('dve', RECIP(rstd4[:, :, :L], rstd4[:, :, :L]), [tr('rstd4')], [tr('rstd4')])
            S.op('dve', TT(qk4[:, :, :L], qk4[:, :, :L], rstd4[:, :, :L], ALU.mult), [tr('rstd4'), tr('qk4')], [tr('qk4')])
            S.op('dve', TT(qk4[:, :, :L], qk4[:, :, :L], gcol4[:, :].unsqueeze(2).to_broadcast([128, 4, L]), ALU.mult),
                 [prm, tr('qk4')], [tr('qk4')])
            for e_ in range(2):
                S.op('dve', TS(qT[:, :, e_, :L], qk4[:, 0:2, :L], C['bd64'][:, 64 * e_:64 * e_ + 1], ALU.mult),
                     [tr('qk4'), constR], [tr('qT')])
            S.op('dve', CP(kTs[:, :, :L], qk4[:, 2:4, :L]), [tr('qk4')], [tr('kTs')])
            S.op('pe', TRS([(PS[0][:L, j * 128:(j + 1) * 128], qk4[:, 2 + j, :L], ident) for j in range(2)]),
                 [tr('qk4'), constR], [PR[0]])
            S.op('act', CP(kN[:L, :], PS[0][:L, 0:256]), [PR[0]], [tr('kN')])
            S.dma('sp', DMA(ks_d[l], kN[:NS4, :]), [tr('kN')], [])
            S.dma('sp', DMA(vs_d[l], vN[:NS4, 0:256]), [tr('vN')], [])
            S.op('dve', CP(Vs[:L, :, 0:64], vN[:L, 0:256].rearrange("p (h d) -> p h d", d=64)), [tr('vN')], [tr('Vs')])
            S.op('dve', TT(t4[:L, :], vN[:L, 256:260], fb_bc[:L, :], ALU.add), [tr('vN'), prm], [tr('t4')])
            S.op('act', ACTF(t4[:L, :], t4[:L, :], AF.Exp, scale=-1.0), [tr('t4')], [tr('t4')])
            S.op('act', ACTF(t4[:L, :], t4[:L, :], AF.Ln, bias=onec[:L, :]), [tr('t4'), constR], [tr('t4')])
            S.op('dve', TS(lf[:L, :], t4[:L, :], -1.0, ALU.mult), [tr('t4')], [tr('lf')])
            S.dma('sp', DMA(lfs_d[l], lf[:NS4, :]), [tr('lf')], [])
            S.op('pe', MMx(PS[3][:L, 0:4], Us_[:L, :L], lf[:L, :]), [tr('lf'), constR], [PR[3]])
            S.op('dve', TS(Fs[:L, :], PS[3][:L, 0:4], -1.0, ALU.mult), [PR[3]], [tr('Fs')])
            S.op('pe', MMX([(PS[0][:L, 2 * hp * L:(2 * hp + 2) * L], kTs[:, hp, :L], qT[:, hp, :, :L], True, True)
                            for hp in range(2)]), [tr('kTs'), tr('qT')], [PR[0]])
            for h in range(4):
                S.op('act', ACTF(Pt[0][:L, h, :L], PS[0][:L, h * L:(h + 1) * L], AF.Exp, bias=Fs[:L, h:h + 1], scale=0.125),
                     [PR[0], tr('Fs')], [tr('P0')])
            S.op('dve', TT(Pt[0][:L, :, :L], Pt[0][:L, :, :L], M01S[:L, :L].unsqueeze(1).to_broadcast([L, 4, L]), ALU.mult),
                 [tr('P0'), constR], [tr('P0')])
            S.op('pe', MMX([(PS[2][:64, h * 80:h * 80 + 65], Pt[0][:L, h, 0:64], Vs[:L, h, 0:65], h == 0, (h == 3 and not GATHER))
                            for h in range(4)]), [tr('P0'), tr('Vs')], [PR[2]])
            if GATHER:
                NPGS = NS * NPG
                lf_all = a2([128, NPG, 4]); rs = a2([128, NPG, 4]); biasT = a2([128, NPG, 4]); scb = a2([128, 16])
                Kp = [a1([128, 256]) for _ in range(2)]; Vp = [a1([128, 256]) for _ in range(2)]
                Vbp = [a1([128, 4, 80], BF16) for _ in range(2)]; kTt = [a1([128, 2, 128], BF16) for _ in range(2)]
                Pz = [a2([128, 4, 64], BF16) for _ in range(2)]; qb = a2([128, 2, 16, 8], BF16)
                ptB = a2([128, NPGS], I32); idxF = a2([128, NPGS]); idxI = ptB; raddF = sb([128, 1])
                ck2 = ck_d.rearrange("l r c -> (l r) c"); cv2 = cv_d.rearrange("l r c -> (l r) c"); clf2 = clf_d.rearrange("l r c -> (l r) c")
                for hp in range(2):
                    S.op('dve', CP(qb[:, hp, 0:NS, :].rearrange("p b (e q) -> p b e q", q=4),
                                   qT[:, hp, :, 0:NS4].rearrange("p e (b q) -> p b e q", q=4)), [tr('qT')], [tr('qb')])
                for kb in range(2):
                    S.op('dve', MSET(Pz[kb][:, :, :], 0.0), [], [tr('Pz%d' % kb)])
                    S.op('dve', MSET(Vbp[kb][:, :, 64:65], 1.0), [], [tr('Vbp%d' % kb)])
                S.dma('sp', DMA(ptB[:, :], pt_d.rearrange("b g -> (b g)").partition_broadcast(128)), [], [tr('idx')])
                S.dma('sp', DMA(raddF[:, :], radd_d), [], [tr('idx')])
                S.op('dve', CP(idxF[:, :], ptB[:, :]), [tr('idx')], [tr('idx')])
                S.op('dve', TS(idxF[:, :], idxF[:, :], 128.0, ALU.mult, float(l * NPOOL * 128), ALU.add), [tr('idx')], [tr('idx')])
                S.op('dve', TT(idxF[:, :], idxF[:, :], raddF[:, 0:1].to_broadcast([128, NPGS]), ALU.add), [tr('idx')], [tr('idx')])
                S.op('dve', CP(idxI[:, :], idxF[:, :]), [tr('idx')], [tr('idx')])
                for b in range(NS):
                    for pg in range(NPG):
                        S.dma('pool', IDMA(lf_all[:, pg, :], clf2, idxI[:, b * NPG + pg:b * NPG + pg + 1]), [tr('idx')], [tr('lf_all')])
                    S.op('dve', MSET(rs[:, NPG - 1, :], 0.0), [], [tr('rs')])
                    for pg in range(NPG - 2, -1, -1):
                        S.op('dve', TT(rs[:, pg, :], rs[:, pg + 1, :], lf_all[:, pg + 1, :], ALU.add), [tr('lf_all')], [tr('rs')])
                    S.op('pe', MMX([(PS[3][:, 0:NPG * 4], C['Mgt'], lf_all[:, :, :].rearrange("p g h -> p (g h)"), True, False),
                                    (PS[3][:, 0:NPG * 4], C['ones'], rs[:, :, :].rearrange("p g h -> p (g h)"), False, True)]),
                         [tr('lf_all'), tr('rs'), constR], [PR[3]])
                    S.op('dve', CP(biasT[:, :, :], PS[3][:, 0:NPG * 4].rearrange("p (g h) -> p g h", h=4)), [PR[3]], [tr('biasT')])
                    for pg in range(NPG):
                        kb = pg % 2
                        col = b * NPG + pg
                        S.dma('pool', IDMA(Kp[kb][:, :], ck2, idxI[:, col:col + 1]), [tr('idx')], [tr('Kp%d' % kb)])
                        S.dma('pool', IDMA(Vp[kb][:, :], cv2, idxI[:, col:col + 1]), [tr('idx')], [tr('Vp%d' % kb)])
                        S.op('act', CP(Vbp[kb][:, :, 0:64], Vp[kb][:, :].rearrange("p (h d) -> p h d", d=64)), [tr('Vp%d' % kb)], [tr('Vbp%d' % kb)])
                        S.op('pe', TRS([(PS[kb][:, j * 128:(j + 1) * 128], Kp[kb][:, j * 128:(j + 1) * 128], ident) for j in range(2)]),
                             [tr('Kp%d' % kb), constR], [PR[kb]])
                        S.op('dve', CP(kTt[kb][:, :, :], PS[kb][:, 0:256].rearrange("p (j x) -> p j x", x=128)), [PR[kb]], [tr('kTt%d' % kb)])
                        S.op('pe', MMX([(PS[4][:, hp * 8:hp * 8 + 8], kTt[kb][:, hp, :], qb[:, hp, b, :], True, True) for hp in range(2)]),
                             [tr('kTt%d' % kb), tr('qb')], [PR[4]])
                        S.op('dve', STT(scb[:, 0:16].rearrange("p (h q) -> p h q", q=4), PS[4][:, 0:16].rearrange("p (h q) -> p h q", q=4), 0.125,
                                        biasT[:, pg, :].unsqueeze(2).to_broadcast([128, 4, 4]), ALU.mult, ALU.add),
                             [PR[4], tr('biasT')], [tr('scb')])
                        S.op('act', ACTF(Pz[kb][:, :, 4 * b:4 * b + 4], scb[:, 0:16].rearrange("p (h q) -> p h q", q=4), AF.Exp),
                             [tr('scb')], [tr('Pz%d' % kb)])
                        S.op('pe', MMX([(PS[2][:64, h * 80:h * 80 + 65], Pz[kb][:, h, :], Vbp[kb][:, h, 0:65], False,
                                         (b == NS - 1 and pg == NPG - 1 and h == 3)) for h in range(4)]),
                             [tr('Pz%d' % kb), tr('Vbp%d' % kb)], [PR[2]])
                    for kb in range(2):
                        S.op('dve', MSET(Pz[kb][:, :, 4 * b:4 * b + 4], 0.0), [tr('Pz%d' % kb)], [tr('Pz%d' % kb)])
            o4 = PS[2][:64, 0:320].rearrange("p (h d) -> p h d", d=80)
            S.op('dve', RECIP(rec4[:64, :].unsqueeze(2), o4[:, :, 64:65]), [PR[2]], [tr('rec4')])
            S.op('dve', TT(mixoN[:64, 0:256].rearrange("p (h d) -> p h d", d=64), o4[:, :, 0:64],
                           rec4[:64, :].unsqueeze(2).to_broadcast([64, 4, 64]), ALU.mult), [PR[2], tr('rec4')], [tr('mixoN')])
            S.op('dve', MSET(mixoN[64:128, 0:256], 0.0), [], [tr('mixoN')])
            if GATHER:
                S.barrier()
            S.op('dve', MSET(osum[:, :], 0.0), [], [tr('osum')])
            for j in range(6):
                for tp in range(4):
                    dst = acc if tp == 0 else tmpc
                    S.op('dve', TS(dst[:, j, :L].rearrange("p (b t) -> p b t", t=4), cin_s[:, j, :, tp:tp + 4], cw[:, tp, j:j + 1], ALU.mult),
                         [tr('cin_s'), prm], [tr('acc' if tp == 0 else 'xbcT')])
                    if tp > 0:
                        S.op('dve', TT(acc[:, j, :L], acc[:, j, :L], tmpc[:, j, :L], ALU.add), [tr('xbcT')], [tr('acc')])
            S.op('dve', TT(acc[:, :, :L], acc[:, :, :L], cbb[:, :].unsqueeze(2).to_broadcast([128, 6, L]), ALU.add), [prm], [tr('acc')])
            S.op('act', ACTF(xbcT[:, :, :L], acc[:, :, :L], AF.Silu), [tr('acc')], [tr('xbcT')])
            S.op('dve', CP(acc[:, :, 0:NS * 3].rearrange("p j (b r) -> p j b r", r=3), cin_s[:, :, 0:NS, 4:7]), [tr('cin_s'), tr('xbcT')], [tr('acc')])
            for a_, (j0, j1) in enumerate(((0, 4), (4, 6))):
                S.op('pe', TRS([(PS[a_][:NS * 3, (j - j0) * 128:(j - j0 + 1) * 128], acc[:, j, 0:NS * 3], ident) for j in range(j0, j1)]),
                     [tr('acc'), constR], [PR[a_]])
                S.op('act' if a_ else 'dve', CP(convN[:NS * 3, j0 * 128:j1 * 128], PS[a_][:NS * 3, 0:(j1 - j0) * 128]), [PR[a_]], [tr('convN')])
            S.dma('sp', DMA(convs_d[l].rearrange("b r c -> (b r) c"), convN[:NS * 3, :]), [tr('convN')], [])
            S.op('dve', TT(t8[:L, :], dtgv[:L, 0:8], dtb_bc[:L, :], ALU.add), [tr('dtgv'), prm], [tr('t8')])
            S.op('act', ACTF(t8[:L, :], t8[:L, :], AF.Exp), [tr('t8')], [tr('t8')])
            S.op('act', ACTF(dt[:L, :], t8[:L, :], AF.Ln, bias=onec[:L, :]), [tr('t8'), constR], [tr('dt')])
            S.op('dve', TT(da[:L, :], dt[:L, :], a_bc[:L, :], ALU.mult), [tr('dt'), prm], [tr('da')])
            S.op('pe', MMX([(PS[3][:L, 16:24], Us_[:L, :L], da[:L, :], True, True),
                            (PS[3][:L, 32:40], BLK[:L, :L], da[:L, :], True, True)]), [tr('da'), constR], [PR[3]])
            S.op('dve', CP(cs[:L, :], PS[3][:L, 16:24]), [PR[3]], [tr('cs')])
            S.op('dve', TS(negcs[:L, :], PS[3][:L, 16:24], -1.0, ALU.mult), [PR[3]], [tr('negcs')])
            S.op('dve', TT(wdec[:L, :], PS[3][:L, 32:40], cs[:L, :], ALU.subtract), [PR[3], tr('cs')], [tr('wdec')])
            S.op('act', ACTF(wdec[:L, :], wdec[:L, :], AF.Exp), [tr('wdec')], [tr('wdec')])
            S.op('act', ACTF(expcs[:L, :], cs[:L, :], AF.Exp), [tr('cs')], [tr('expcs')])
            S.op('pe', MMX([(PS[4][:, b * 8:b * 8 + 8], rowm32[:, b:b + 1].to_broadcast([128, 128]), da[:L, :], True, True)
                            for b in range(NS)]), [tr('da'), constR], [PR[4]])
            S.op('act', ACTF(edec_all[:, 0:NS, :], PS[4][:, 0:NS * 8].rearrange("p (b h) -> p b h", h=8), AF.Exp), [PR[4]], [tr('edec_all')])
            S.op('dve', TT(daU[:L, :, :L], Us_[:L, :L].unsqueeze(1).to_broadcast([L, 8, L]),
                           da[:L, :].unsqueeze(2).to_broadcast([L, 8, L]), ALU.mult), [tr('da'), constR], [tr('daU')])
            for g in range(2):
                S.op('pe', MMx(PS[g][:L, 0:4 * L], BLK[:L, :L], daU[:L, 4 * g:4 * g + 4, :L]), [tr('daU'), constR], [PR[g]])
                S.op('dve', TT(Em[:L, 4 * g:4 * g + 4, :L], PS[g][:L, 0:4 * L].rearrange("p (h l) -> p h l", l=L),
                               NEGMS[:L, :L].unsqueeze(1).to_broadcast([L, 4, L]), ALU.add), [PR[g], constR], [tr('daU')])
            for h in range(8):
                S.op('act', ACTF(Em[:L, h, :L], Em[:L, h, :L], AF.Exp, bias=negcs[:L, h:h + 1]), [tr('daU'), tr('negcs')], [tr('daU')])
            S.op('dve', CP(bcT_b[:, 0, :L], xbcT[:, 4, :L]), [tr('xbcT')], [tr('bcT')])
            for g in range(2):
                S.op('dve', TS(Cblk_f[:, g, :L], xbcT[:, 5, :L], C['bd64'][:, 64 * g:64 * g + 1], ALU.mult),
                     [tr('xbcT'), constR], [tr('Cblk_f')])
            S.op('dve', CP(Cblk_b[:, :, :L], Cblk_f[:, :, :L]), [tr('Cblk_f')], [tr('Cblk_b')])
            S.op('pe', MMx(PS[2][:L, 0:2 * L], bcT_b[:, 0, :L], Cblk_b[:, :, :L]), [tr('bcT'), tr('Cblk_b')], [PR[2]])
            S.op('dve', TT(MT[:L, :, :L].rearrange("p (g h) l -> p g h l", g=2),
                           Em[:L, :, :L].rearrange("p (g h) l -> p g h l", g=2),
                           PS[2][:L, 0:2 * L].rearrange("p (g l) -> p g l", l=L).unsqueeze(2).to_broadcast([L, 2, 4, L]),
                           ALU.mult), [tr('daU'), PR[2]], [tr('MT')])
            S.op('pe', TRS([(PS[3][:L, j * 128:(j + 1) * 128], xbcT[:, j, :L], ident) for j in range(4)]),
                 [tr('xbcT'), constR], [PR[3]])
            S.op('act', CP(xN[:L, :], PS[3][:L, :]), [PR[3]], [tr('xN')])
            S.op('pe', TRS([(PS[4][:L, 0:128], xbcT[:, 4, :L], ident)]), [tr('xbcT'), constR], [PR[4]])
            S.op('act', CP(BN[:L, :], PS[4][:L, 0:128]), [PR[4]], [tr('BN')])
            S.op('dve', TT(xdt[:L, :, :], xN[:L, :].rearrange("p (h d) -> p h d", d=64),
                           dt[:L, :].unsqueeze(2).to_broadcast([L, 8, 64]), ALU.mult), [tr('xN'), tr('dt')], [tr('xdt')])
            S.op('dve', CP(xdt2[:L, :, :, :].rearrange("p h g n -> p g h n"),
                           xdt[:L, :, :].rearrange("p (g h) n -> p g h n", g=2)), [tr('xdt')], [tr('xdt2')])
            S.op('dve', TT(Bw[:L, :, :, :].rearrange("p h g n -> p g h n"),
                           BN[:L, :].rearrange("p (g n) -> p g n", g=2).unsqueeze(2).to_broadcast([L, 2, 4, 64]),
                           wdec[:L, :].rearrange("p (g h) -> p g h", g=2).unsqueeze(3).to_broadcast([L, 2, 4, 64]),
                           ALU.mult), [tr('BN'), tr('wdec')], [tr('Bw')])
            S.op('pe', MMX([(PS[5][:L, h * 64:(h + 1) * 64], MT[:L, h, :L], xdt[:L, h, :], True, True) for h in range(8)]),
                 [tr('MT'), tr('xdt')], [PR[5]])
            S.op('dve', MSET(ysum[:, :], 0.0), [], [tr('ysum')])
            for b in range(NS):
                bb = 0
                for g in range(2):
                    S.dma('sp', DMA(h0N[bb][:64, :, :].rearrange("p (hh g) n -> p hh g n", g=2)[:, :, g, :],
                                    sssm_d[l, b].rearrange("(g hh) p n -> g p hh n", g=2)[g]), [], [tr('h0N%d' % bb)])
                S.op('pe', TRS([(PS[6][:, hh * 64:(hh + 1) * 64], h0N[bb][:64, 2 * hh:2 * hh + 2, :].rearrange("p g n -> p (g n)"),
                                 ident[:64, :64]) for hh in range(4)]),
                     [tr('h0N%d' % bb), constR], [PR[6]])
                S.op('act', CP(hTb[bb][:, :, :], PS[6][:, 0:256].rearrange("p (h c) -> p h c", c=64)), [PR[6]], [tr('hTb%d' % bb)])
                S.op('pe', MMX([(PS[6][:L, h * 64:(h + 1) * 64], Cblk_f[:, h // 4, :L], hTb[bb][:, h % 4, :], True, True)
                                for h in range(8)]), [tr('Cblk_f'), tr('hTb%d' % bb)], [PR[6]])
                S.op('dve', TS(tmpy[:L, :], PS[6][:L, :], rowm32[:L, b:b + 1], ALU.mult), [PR[6], constR], [tr('tmpy')])
                S.op('dve', TT(ysum[:L, :], ysum[:L, :], tmpy[:L, :], ALU.add), [tr('tmpy')], [tr('ysum')])
                S.op('dve', TS(Bwb[:L, :, :, :], Bw[:L, :, :, :], rowm32[:L, b:b + 1], ALU.mult), [tr('Bw'), constR], [tr('Bwb')])
                S.op('pe', MMX([(PS[7][:, hh * 128:(hh + 1) * 128], Bwb[:L, hh, :, :].rearrange("p g n -> p (g n)"),
                                 xdt2[:L, hh, :, :].rearrange("p g n -> p (g n)"), True, True) for hh in range(4)]),
                     [tr('Bwb'), tr('xdt2')], [PR[7]])
                for g in range(2):
                    gs = slice(g * 64, g * 64 + 64)
                    S.op('dve', TT(hst_t[gs, :, :], hTb[bb][gs, :, :],
                                   edec_all[gs, b, 4 * g:4 * g + 4].unsqueeze(2).to_broadcast([64, 4, 64]), ALU.mult),
                         [tr('hTb%d' % bb), tr('edec_all')], [tr('hst_t')])
                    S.op('dve', TT(newT[gs, :, :], hst_t[gs, :, :],
                                   PS[7][gs, :].rearrange("p (h c) -> p h c", c=128)[:, :, g * 64:g * 64 + 64], ALU.add),
                         [tr('hst_t'), PR[7]], [tr('newT')])
                S.op('pe', TRS([(PS[7][:64, hh * 128:(hh + 1) * 128], newT[:, hh, :], ident) for hh in range(4)]),
                     [tr('newT'), constR], [PR[7]])
                S.op('act', CP(sso2[:64, :, :], PS[7][:64, :].rearrange("p (h n) -> p h n", n=128)), [PR[7]], [tr('sso2')])
                for g in range(2):
                    S.dma('pool', DMA(ssms_d[l, b].rearrange("(g hh) p n -> g p hh n", g=2)[g], sso2[:64, :, g * 64:(g + 1) * 64]),
                          [tr('sso2')], [])
            S.op('dve', TT(ytmp[:L, :].rearrange("p (h d) -> p h d", d=64), ysum[:L, :].rearrange("p (h d) -> p h d", d=64),
                           expcs[:L, :].unsqueeze(2).to_broadcast([L, 8, 64]), ALU.mult), [tr('ysum'), tr('expcs')], [tr('ytmp')])
            S.op('dve', TT(yy[:L, :], ytmp[:L, :], PS[5][:L, :], ALU.add), [PR[5], tr('ytmp')], [tr('yy')])
            S.op('dve', TT(ytmp[:L, :].rearrange("p (h d) -> p h d", d=64), xN[:L, :].rearrange("p (h d) -> p h d", d=64),
                           sd_bc[:L, :].unsqueeze(2).to_broadcast([L, 8, 64]), ALU.mult), [tr('xN'), prm], [tr('ytmp')])
            S.op('dve', TT(yy[:L, :], yy[:L, :], ytmp[:L, :], ALU.add), [tr('ytmp')], [tr('yy')])
            S.op('dve', TT(yy[:L, :], yy[:L, :], zs[:L, :], ALU.mult), [tr('zs')], [tr('yy')])
            S.op('act', ACTF(ytmp[:L, :], yy[:L, :], AF.Square), [tr('yy')], [tr('ytmp')])
            S.op('dve', RED(ss2[:L, :], ytmp[:L, :].rearrange("p (g d) -> p g d", g=2)), [tr('ytmp')], [tr('ss2')])
            S.op('act', ACTF(ss2[:L, :], ss2[:L, :], AF.Sqrt, bias=epsc[:L, :], scale=1.0 / 256), [tr('ss2'), constR], [tr('ss2')])
            S.op('dve', RECIP(ss2[:L, :], ss2[:L, :]), [tr('ss2')], [tr('ss2')])
            S.op('dve', TT(yy[:L, :].rearrange("p (g d) -> p g d", g=2), yy[:L, :].rearrange("p (g d) -> p g d", g=2),
                           ss2[:L, :].unsqueeze(2).to_broadcast([L, 2, 256]), ALU.mult), [tr('ss2')], [tr('yy')])
            S.op('dve', TT(mixoN[:L, 256:768], yy[:L, :], sn_bc[:L, :], ALU.mult), [tr('yy'), prm], [tr('mixoN')])
            S.op('pe', MMx(PS[0][:, 0:L], Wg[:16, :], glr[:16, :L]), [tr('glr'), prm], [PR[0]])
            S.op('act', ACTF(lg[:, :L], PS[0][:, 0:L], AF.Exp, bias=ngb[:, 0:1], scale=-1.0), [PR[0], prm], [tr('lg')])
            S.op('act', ACTF(lg[:, :L], lg[:, :L], AF.Ln, bias=onec), [tr('lg'), constR], [tr('lg')])
            S.op('dve', SCAN(cl[:, :L], rst128[:, :L], lg[:, :L]), [tr('lg'), constR], [tr('cl')])
            S.op('act', ACTF(eq[:, :L], cl[:, :L], AF.Exp, scale=-1.0 / 16), [tr('cl')], [tr('eq')])
            S.op('act', ACTF(ek[:, :L], cl[:, :L], AF.Exp, scale=1.0 / 16), [tr('cl')], [tr('ek')])
            S.op('dve', STT(qt[:, :L], gqk[:, 0, :L], float(32 ** -0.5), eq[:, :L], ALU.mult, ALU.mult),
                 [tr('gqk'), tr('eq')], [tr('qt')])
            S.op('dve', TT(kt[:, :L], gqk[:, 1, :L], ek[:, :L], ALU.mult), [tr('gqk'), tr('ek')], [tr('kt')])
            cl3 = cl[:, :L].rearrange("p (b t) -> p b t", t=4)
            S.op('dve', TT(dl[:, :L].rearrange("p (b t) -> p b t", t=4), cl3[:, :, 3:4].to_broadcast([128, 32, 4]), cl3, ALU.subtract),
                 [tr('cl')], [tr('dl')])
            S.op('act', ACTF(dl[:, :L], dl[:, :L], AF.Exp, scale=-1.0 / 16), [tr('dl')], [tr('dl')])
            S.op('dve', TT(khT[:, :L], gqk[:, 1, :L], dl[:, :L], ALU.mult), [tr('gqk'), tr('dl')], [tr('khT')])
            S.op('act', ACTF(elast_all[:, :].unsqueeze(2), cl3[:, :, 3:4], AF.Exp, scale=-1.0 / 16), [tr('cl')], [tr('elast_all')])
            S.op('dve', TT(qblk[:, :, :L], qt[:, :L].unsqueeze(1).to_broadcast([128, 4, L]),
                           hm[:, :].unsqueeze(2).to_broadcast([128, 4, L]), ALU.mult), [tr('qt'), constR], [tr('qblk')])
            S.op('pe', MMx(PS[1][:L, 0:4 * L], kt[:, :L], qblk[:, :, :L]), [tr('kt'), tr('qblk')], [PR[1]])
            S.op('dve', TT(attm[:L, :, :L], PS[1][:L, 0:4 * L].rearrange("p (h l) -> p h l", l=L),
                           M01S[:L, :L].unsqueeze(1).to_broadcast([L, 4, L]), ALU.mult), [PR[1], constR], [tr('attm')])
            S.op('act', CP(vg[:L, :], dtgv[:L, 8:264]), [tr('dtgv')], [tr('vg')])
            S.op('pe', MMX([(PS[0][:L, 256 + h * 64:256 + (h + 1) * 64], attm[:L, h, :L], vg[:L, h * 64:(h + 1) * 64], True, True)
                            for h in range(4)]), [tr('attm'), tr('vg')], [PR[0]])
            S.op('pe', TRS([(PS[3][:L, 0:128], khT[:, :L], ident)]), [tr('khT'), constR], [PR[3]])
            S.op('act', CP(khN[:L, :], PS[3][:L, 0:128]), [PR[3]], [tr('khN')])
            for b in range(NS):
                bb = b % 2
                S.dma('sp', DMA(Sg0[bb][:, :], sgla_d[l, b]), [], [tr('Sg0%d' % bb)])
                S.op('act', CP(Sg0b[bb][:, :], Sg0[bb][:, :]), [tr('Sg0%d' % bb)], [tr('Sg0b%d' % bb)])
                S.op('pe', MMX([(PS[3][:L, h * 64:(h + 1) * 64], qblk[:, h, :L], Sg0b[bb][:, :], True, True) for h in range(4)]),
                     [tr('qblk'), tr('Sg0b%d' % bb)], [PR[3]])
                S.op('dve', TS(tmpy[:L, 0:256], PS[3][:L, 0:256], rowm32[:L, b:b + 1], ALU.mult), [PR[3], constR], [tr('tmpy')])
                S.op('dve', TT(osum[:L, :], osum[:L, :], tmpy[:L, 0:256], ALU.add), [tr('tmpy')], [tr('osum')])
                S.op('dve', TS(khNb[:L, :], khN[:L, :], rowm32[:L, b:b + 1], ALU.mult), [tr('khN'), constR], [tr('khNb')])
                S.op('pe', MMx(PS[7][:, 0:256], khNb[:L, :], vg[:L, :]), [tr('khNb'), tr('vg')], [PR[7]])
                S.op('dve', TT(tmpg[:, :, :], PS[7][:, 0:256].rearrange("p (h v) -> p h v", v=64),
                               hm[:, :].unsqueeze(2).to_broadcast([128, 4, 64]), ALU.mult), [PR[7], constR], [tr('tmpg')])
                S.op('dve', RED(red[:, :], tmpg[:, :, :].rearrange("p h v -> p v h")), [tr('tmpg')], [tr('red')])
                S.op('dve', STT(Sg0[bb][:, :], Sg0[bb][:, :], elast_all[:, b:b + 1], red[:, :], ALU.mult, ALU.add),
                     [tr('red'), tr('elast_all'), tr('Sg0b%d' % bb)], [tr('Sg0%d' % bb)])
                S.dma('pool', DMA(glas_d[l, b], Sg0[bb][:, :]), [tr('Sg0%d' % bb)], [])
            S.op('dve', TT(og2[:L, :], osum[:L, :], PS[0][:L, 256:512], ALU.add), [PR[0], tr('osum')], [tr('og2')])
            S.op('act', ACTF(og[:L, :], og2[:L, :], AF.Square), [tr('og2')], [tr('og')])
            S.op('dve', RED(ss4[:L, :], og[:L, :].rearrange("p (h v) -> p h v", v=64)), [tr('og')], [tr('ss4')])
            S.op('act', ACTF(ss4[:L, :], ss4[:L, :], AF.Sqrt, bias=epsc[:L, :], scale=1.0 / 64), [tr('ss4'), constR], [tr('ss4')])
            S.op('dve', RECIP(ss4[:L, :], ss4[:L, :]), [tr('ss4')], [tr('ss4')])
            S.op('dve', TT(og[:L, :].rearrange("p (h v) -> p h v", v=64), og2[:L, :].rearrange("p (h v) -> p h v", v=64),
                           ss4[:L, :].unsqueeze(2).to_broadcast([L, 4, 64]), ALU.mult), [tr('og2'), tr('ss4')], [tr('og')])
            S.op('dve', TT(og[:L, :].rearrange("p (h v) -> p h v", v=64), og[:L, :].rearrange("p (h v) -> p h v", v=64),
                           gn_bc[:L, :].unsqueeze(1).to_broadcast([L, 4, 64]), ALU.mult), [prm], [tr('og')])
            S.op('dve', TT(mixoN[:L, 768:1024], og[:L, :], zg[:L, :], ALU.mult), [tr('og'), tr('zg')], [tr('mixoN')])
            for half in range(2):
                S.op('pe', TRS([(PS[4 + half][:, j * L:(j + 1) * L], mixoN[:L, (half * 4 + j) * 128:(half * 4 + j + 1) * 128],
                                 ident[:L, :L]) for j in range(4)]), [tr('mixoN'), constR], [PR[4 + half]])
                S.op('act' if half else 'dve', CP(mixoT[:, half * 4:half * 4 + 4, :L],
                                                  PS[4 + half][:, 0:4 * L].rearrange("p (j l) -> p j l", l=L)),
                     [PR[4 + half]], [tr('mixoT')])
            for oc in range(8):
                bo = 6 + oc % 2
                wb_ = oc % 2
                S.dma('pool', DMA(wo_t[wb_][:, :, :], wmo_v[:, :, oc * 128:(oc + 1) * 128]), [], [woR[wb_]])
                S.op('pe', MMX([(PS[bo][:, 0:L], wo_t[wb_][:, k, :], mixoT[:, k, :L], k == 0, k == 7)
                                for k in range(8)]), [woR[wb_], tr('mixoT')], [PR[bo]])
                S.op('dve', STT(xT[:, oc, col0:col0 + NS4], PS[bo][:, 0:NS4], 1.0, xT[:, oc, col0:col0 + NS4], ALU.mult, ALU.add), [PR[bo]], [xcR])
        chunks = [(0, 16, 0)] + [(c + 1, 128, 16 + c * 128) for c in range(NCH)]
        Ucst, ONE, NEGM, M01 = C['U'], C['ones'], C['negm'], C['m01']
        prev_cin = None
        for (ci, L, col0) in chunks:
            cb_i = ci % 2
            if MSTEP < 1:
                continue
            norm_T(xT[:, :, col0:col0 + L], L, l * 3 + 1, hc[:, :, :L], sq, tmp, rstd, 7, [xcR], [tr('hc')])
            if MSTEP < 2:
                continue
            def tproj(bank, idx, c0, M=128):
                S.op('pe', MMX([(PS[bank][:M, idx * L:(idx + 1) * L], Wmi[:, k, c0:c0 + M], hc[:, k, :L], k == 0, k == 7)
                                for k in range(8)]), [wRm, tr('hc')], [PR[bank]])
            def nproj(bank, o0, c0, n):
                S.op('pe', MMX([(PS[bank][:L, o0:o0 + n], hc[:, k, :L], Wmi[:, k, c0:c0 + n], k == 0, k == 7)
                                for k in range(8)]), [wRm, tr('hc')], [PR[bank]])
            for i, c0 in enumerate((O_FQ, O_FQ + 128, O_FK, O_FK + 128)):
                tproj(0, i, c0)
            if MSTEP < 2.1:
                continue
            for i in range(4):
                tproj(1, i, O_XBC + i * 128)
            for i, c0 in enumerate((O_XBC + 512, O_XBC + 640, O_GQ, O_GK)):
                tproj(2, i, c0)
            if MSTEP < 2.2:
                continue
            nproj(3, 0, O_FV, 260)
            nproj(4, 0, O_SZ, 512)
            if MSTEP < 2.3:
                continue
            nproj(5, 0, O_DT, 8)
            nproj(5, 8, O_GV, 256)
            if MSTEP < 2.4:
                continue
            S.op('pe', MMX([(PS[5][:16, 320:320 + L], Wmi[:, k, O_LR:O_LR + 16], hc[:, k, :L], k == 0, k == 7)
                            for k in range(8)]), [wRm, tr('hc')], [PR[5]])
            if MSTEP < 2.45:
                continue
            nproj(6, 0, O_GG, 256)
            if MSTEP < 2.5:
                continue
            v0 = lambda b, n: PS[b][:, 0:n * L].rearrange("p (j l) -> p j l", l=L)
            S.op('dve', CP(qk4[:, :, :L], v0(0, 4)), [PR[0]], [tr('qk4')])
            S.op('act', ACTF(sq4[:, :, :L], qk4[:, :, :L], AF.Square), [tr('qk4')], [tr('sq4')])
            if MSTEP < 2.6:
                continue
            S.op('act', CP(cin[cb_i][:, 0:4, 3:3 + L], v0(1, 4)), [PR[1]], [tr('cin%d' % cb_i)])
            S.op('dve', CP(cin[cb_i][:, 4:6, 3:3 + L], v0(2, 2)), [PR[2]], [tr('cin%d' % cb_i)])
            S.op('dve', CP(gqk[:, :, :L], PS[2][:, 2 * L:4 * L].rearrange("p (j l) -> p j l", l=L)), [PR[2]], [tr('gqk')])
            if MSTEP < 2.7:
                continue
            S.op('act', CP(vN[:L, :], PS[3][:L, 0:260]), [PR[3]], [tr('vN')])
            S.op('act', ACTF(zs[:L, :], PS[4][:L, :], AF.Silu), [PR[4]], [tr('zs')])
            if MSTEP < 2.8:
                continue
            S.op('dve', CP(dtgv[:L, :], PS[5][:L, 0:264]), [PR[5]], [tr('dtgv')])
            if MSTEP < 2.85:
                continue
            S.op('dve', CP(glr[:16, :L], PS[5][:16, 320:320 + L]), [PR[5]], [tr('glr')])
            if MSTEP < 2.9:
                continue
            S.op('act', ACTF(zg[:L, :], PS[6][:L, 0:256], AF.Silu), [PR[6]], [tr('zg')])
            if MSTEP < 3:
                continue
            S.op('pe', MMx(PS[7][:, 0:4 * L], bd64_b[:, :], sq4[:, :, :L]), [tr('sq4'), constR], [PR[7]])
            if MSTEP < 3.1:
                continue
            S.op('act', ACTF(rstd4[:, :, :L], v0(7, 4), AF.Sqrt, bias=epsc, scale=1.0 / 64), [PR[7], constR], [tr('rstd4')])
            S.op('dve', RECIP(rstd4[:, :, :L], rstd4[:, :, :L]), [tr('rstd4')], [tr('rstd4')])
            if MSTEP < 3.2:
                continue
            S.op('dve', TT(qk4[:, :, :L], qk4[:, :, :L], rstd4[:, :, :L], ALU.mult), [tr('rstd4'), tr('qk4')], [tr('qk4')])
            S.op('dve', TT(qk4[:, :, :L], qk4[:, :, :L], gcol4[:, :].unsqueeze(2).to_broadcast([128, 4, L]), ALU.mult),
                 [prm, tr('qk4')], [tr('qk4')])
            for e_ in range(2):
                S.op('dve', TS(qT[:, :, e_, :L], qk4[:, 0:2, :L], C['bd64'][:, 64 * e_:64 * e_ + 1], ALU.mult),
                     [tr('qk4'), constR], [tr('qT')])
            S.op('dve', CP(kT_all[:, :, col0:col0 + L], qk4[:, 2:4, :L]), [tr('qk4')], [kTR])
            if MSTEP < 3.4:
                continue
            S.op('pe', TRS([(PS[0][:L, j * 128:(j + 1) * 128], qk4[:, 2 + j, :L], ident) for j in range(2)]),
                 [tr('qk4'), constR], [PR[0]])
            S.op('act', CP(kN[:L, :], PS[0][:L, 0:256]), [PR[0]], [tr('kN')])
            if MSTEP < 3.5:
                continue
            S.dma('sp', DMA(kp_d[l, col0:col0 + L, :], kN[:L, :]), [tr('kN')], [])
            if MSTEP < 4:
                continue
            S.dma('sp', DMA(vp_d[l, col0:col0 + L, :], vN[:L, 0:256]), [tr('vN')], [])
            S.op('dve', CP(V_all[:L, ci, :, 0:64], vN[:L, 0:256].rearrange("p (h d) -> p h d", d=64)), [tr('vN')], [VR])
            S.op('dve', TT(t4[:L, :], vN[:L, 256:260], fb_bc[:L, :], ALU.add), [tr('vN'), prm], [tr('t4')])
            S.op('act', ACTF(t4[:L, :], t4[:L, :], AF.Exp, scale=-1.0), [tr('t4')], [tr('t4')])
            S.op('act', ACTF(t4[:L, :], t4[:L, :], AF.Ln, bias=onec[:L, :]), [tr('t4'), constR], [tr('t4')])
            S.op('dve', TS(lf[:L, :], t4[:L, :], -1.0, ALU.mult), [tr('t4')], [tr('lf')])
            S.dma('sp', DMA(lfp_d[l, col0:col0 + L, :], lf[:L, :]), [tr('lf')], [])
            S.op('pe', MMX([(PS[3][:L, 0:4], Ucst[:L, :L], lf[:L, :], True, True),
                            (PS[3][:, 8:12], ONE[:L, :], lf[:L, :], True, True)]), [tr('lf'), constR], [PR[3]])
            S.op('dve', TT(F_all[:L, ci, :], PS[3][:L, 0:4], Ftot[:L, :], ALU.add), [PR[3], FtR], [FR])
            S.op('dve', TT(nb_all[:, 0:ci + 1, :], Ftot[:, :].unsqueeze(1).to_broadcast([128, ci + 1, 4]),
                           F_all[:, 0:ci + 1, :], ALU.subtract), [FtR, FR], [tr('nb')])
            S.op('dve', TT(Ftot[:, :], Ftot[:, :], PS[3][:, 8:12], ALU.add), [PR[3], tr('nb')], [FtR])
            if MSTEP < 5:
                continue
            for j in range(ci + 1):
                (_, Lj, cj) = chunks[j]
                bs = j % 2
                S.op('pe', MMX([(PS[bs][:Lj, 2 * hp * L:(2 * hp + 2) * L], kT_all[:, hp, cj:cj + Lj],
                                 qT[:, hp, :, :L], True, True) for hp in range(2)]),
                     [kTR, tr('qT')], [PR[bs]])
                if MSTEP < 5.2:
                    continue
                for h in range(4):
                    S.op('act', ACTF(Pt[bs][:Lj, h, :L], PS[bs][:Lj, h * L:(h + 1) * L], AF.Exp,
                                     bias=nb_all[:Lj, j, h:h + 1], scale=0.125), [PR[bs], tr('nb')], [tr('P%d' % bs)])
                if MSTEP < 5.3:
                    continue
                if j == ci:
                    S.op('dve', TT(Pt[bs][:L, :, :L], Pt[bs][:L, :, :L], M01[:L, :L].unsqueeze(1).to_broadcast([L, 4, L]),
                                   ALU.mult), [tr('P%d' % bs), constR], [tr('P%d' % bs)])
                if MSTEP < 5.4:
                    continue
                S.op('pe', MMX([(PS[2][:L, h * 80:h * 80 + 65], Pt[bs][:Lj, h, :L], V_all[:Lj, j, h, 0:65], (j == 0 and h == 0), (j == ci and h == 3))
                                for h in range(4)]), [tr('P%d' % bs), VR], [PR[2]])
            if MSTEP < 5.5:
                continue
            o4 = PS[2][:L, 0:320].rearrange("p (h d) -> p h d", d=80)
            S.op('dve', RECIP(rec4[:L, :].unsqueeze(2), o4[:, :, 64:65]), [PR[2]], [tr('rec4')])
            if MSTEP < 5.6:
                continue
            S.op('dve', TT(mixoN[:L, 0:256].rearrange("p (h d) -> p h d", d=64), o4[:, :, 0:64],
                           rec4[:L, :].unsqueeze(2).to_broadcast([L, 4, 64]), ALU.mult), [PR[2], tr('rec4')], [tr('mixoN')])
            if MSTEP < 6:
                continue
            cn = cin[cb_i]
            cnR = tr('cin%d' % cb_i)
            if prev_cin is None:
                S.op('dve', MSET(cn[:, :, 0:3], 0.0), [], [cnR])
            else:
                (pc, pL, pR) = prev_cin
                S.op('dve', CP(cn[:, :, 0:3], pc[:, :, pL:pL + 3]), [pR], [cnR])
            prev_cin = (cn, L, cnR)
            for tp in range(4):
                dst = acc if tp == 0 else tmpc
                S.op('dve', TT(dst[:, :, :L], cn[:, :, tp:tp + L], cw[:, tp, :].unsqueeze(2).to_broadcast([128, 6, L]), ALU.mult),
                     [cnR, prm], [tr('acc' if tp == 0 else 'xbcT')])
                if tp > 0:
                    S.op('dve', TT(acc[:, :, :L], acc[:, :, :L], tmpc[:, :, :L], ALU.add), [tr('xbcT')], [tr('acc')])
            S.op('dve', TT(acc[:, :, :L], acc[:, :, :L], cbb[:, :].unsqueeze(2).to_broadcast([128, 6, L]), ALU.add),
                 [prm], [tr('acc')])
            S.op('act', ACTF(xbcT[:, :, :L], acc[:, :, :L], AF.Silu), [tr('acc')], [tr('xbcT')])
            if ci == NCH:
                for r_ in range(3):
                    S.dma('sp', DMA(convp_d[l, r_].rearrange("(j p) -> p j", p=128), cn[:, :, L + r_], nonc=True), [cnR], [])
            if MSTEP < 7:
                continue
            S.op('dve', TT(t8[:L, :], dtgv[:L, 0:8], dtb_bc[:L, :], ALU.add), [tr('dtgv'), prm], [tr('t8')])
            S.op('act', ACTF(t8[:L, :], t8[:L, :], AF.Exp), [tr('t8')], [tr('t8')])
            S.op('act', ACTF(dt[:L, :], t8[:L, :], AF.Ln, bias=onec[:L, :]), [tr('t8'), constR], [tr('dt')])
            S.op('dve', TT(da[:L, :], dt[:L, :], a_bc[:L, :], ALU.mult), [tr('dt'), prm], [tr('da')])
            S.op('pe', MMX([(PS[3][:L, 16:24], Ucst[:L, :L], da[:L, :], True, True),
                            (PS[3][:L, 32:40], ONE[:L, :L], da[:L, :], True, True),
                            (PS[3][:, 48:56], ONE[:L, :], da[:L, :], True, True)]), [tr('da'), constR], [PR[3]])
            S.op('dve', CP(cs[:L, :], PS[3][:L, 16:24]), [PR[3]], [tr('cs')])
            S.op('dve', TS(negcs[:L, :], PS[3][:L, 16:24], -1.0, ALU.mult), [PR[3]], [tr('negcs')])
            S.op('dve', TT(wdec[:L, :], PS[3][:L, 32:40], cs[:L, :], ALU.subtract), [PR[3], tr('cs')], [tr('wdec')])
            S.op('act', ACTF(wdec[:L, :], wdec[:L, :], AF.Exp), [tr('wdec')], [tr('wdec')])
            S.op('act', ACTF(expcs[:L, :], cs[:L, :], AF.Exp), [tr('cs')], [tr('expcs')])
            S.op('act', ACTF(edec[:, :], PS[3][:, 48:56], AF.Exp), [PR[3]], [tr('edec')])
            S.op('dve', TT(daU[:L, :, :L], Ucst[:L, :L].unsqueeze(1).to_broadcast([L, 8, L]),
                           da[:L, :].unsqueeze(2).to_broadcast([L, 8, L]), ALU.mult), [tr('da'), constR], [tr('daU')])
            for g in range(2):
                S.op('pe', MMx(PS[g][:L, 0:4 * L], ONE[:L, :L], daU[:L, 4 * g:4 * g + 4, :L]), [tr('daU'), constR], [PR[g]])
                S.op('dve', TT(Em[:L, 4 * g:4 * g + 4, :L], PS[g][:L, 0:4 * L].rearrange("p (h l) -> p h l", l=L),
                               NEGM[:L, :L].unsqueeze(1).to_broadcast([L, 4, L]), ALU.add), [PR[g], constR], [tr('daU')])
            for h in range(8):
                S.op('act', ACTF(Em[:L, h, :L], Em[:L, h, :L], AF.Exp, bias=negcs[:L, h:h + 1]), [tr('daU'), tr('negcs')],
                     [tr('daU')])
            S.op('dve', CP(bcT_b[:, 0, :L], xbcT[:, 4, :L]), [tr('xbcT')], [tr('bcT')])
            for g in range(2):
                S.op('dve', TS(Cblk_f[:, g, :L], xbcT[:, 5, :L], C['bd64'][:, 64 * g:64 * g + 1], ALU.mult),
                     [tr('xbcT'), constR], [tr('Cblk_f')])
            S.op('dve', CP(Cblk_b[:, :, :L], Cblk_f[:, :, :L]), [tr('Cblk_f')], [tr('Cblk_b')])
            S.op('pe', MMx(PS[2][:L, 0:2 * L], bcT_b[:, 0, :L], Cblk_b[:, :, :L]), [tr('bcT'), tr('Cblk_b')], [PR[2]])
            S.op('dve', TT(MT[:L, :, :L].rearrange("p (g h) l -> p g h l", g=2),
                           Em[:L, :, :L].rearrange("p (g h) l -> p g h l", g=2),
                           PS[2][:L, 0:2 * L].rearrange("p (g l) -> p g l", l=L).unsqueeze(2).to_broadcast([L, 2, 4, L]),
                           ALU.mult), [tr('daU'), PR[2]], [tr('MT')])
            S.op('pe', TRS([(PS[3][:L, j * 128:(j + 1) * 128], xbcT[:, j, :L], ident) for j in range(4)]),
                 [tr('xbcT'), constR], [PR[3]])
            S.op('act', CP(xN[:L, :], PS[3][:L, :]), [PR[3]], [tr('xN')])
            S.op('pe', TRS([(PS[4][:L, 0:128], xbcT[:, 4, :L], ident)]), [tr('xbcT'), constR], [PR[4]])
            S.op('act', CP(BN[:L, :], PS[4][:L, 0:128]), [PR[4]], [tr('BN')])
            S.op('dve', TT(xdt[:L, :, :], xN[:L, :].rearrange("p (h d) -> p h d", d=64),
                           dt[:L, :].unsqueeze(2).to_broadcast([L, 8, 64]), ALU.mult), [tr('xN'), tr('dt')], [tr('xdt')])
            S.op('dve', CP(xdt2[:L, :, :, :].rearrange("p h g n -> p g h n"),
                           xdt[:L, :, :].rearrange("p (g h) n -> p g h n", g=2)), [tr('xdt')], [tr('xdt2')])
            S.op('dve', TT(Bw[:L, :, :, :].rearrange("p h g n -> p g h n"),
                           BN[:L, :].rearrange("p (g n) -> p g n", g=2).unsqueeze(2).to_broadcast([L, 2, 4, 64]),
                           wdec[:L, :].rearrange("p (g h) -> p g h", g=2).unsqueeze(3).to_broadcast([L, 2, 4, 64]),
                           ALU.mult), [tr('BN'), tr('wdec')], [tr('Bw')])
            S.op('pe', MMX([(PS[5][:L, h * 64:(h + 1) * 64], MT[:L, h, :L], xdt[:L, h, :], True, True) for h in range(8)]),
                 [tr('MT'), tr('xdt')], [PR[5]])
            S.op('pe', MMX([(PS[6][:L, h * 64:(h + 1) * 64], Cblk_f[:, h // 4, :L], hstT[:, h % 4, :], True, True)
                            for h in range(8)]), [tr('Cblk_f'), hsR], [PR[6]])
            S.op('dve', TT(ytmp[:L, :].rearrange("p (h d) -> p h d", d=64), PS[6][:L, :].rearrange("p (h d) -> p h d", d=64),
                           expcs[:L, :].unsqueeze(2).to_broadcast([L, 8, 64]), ALU.mult), [PR[6], tr('expcs')], [tr('ytmp')])
            S.op('dve', TT(yy[:L, :], ytmp[:L, :], PS[5][:L, :], ALU.add), [PR[5], tr('ytmp')], [tr('yy')])
            S.op('pe', MMX([(PS[7][:, hh * 128:(hh + 1) * 128],
                             Bw[:L, hh, :, :].rearrange("p g n -> p (g n)"),
                             xdt2[:L, hh, :, :].rearrange("p g n -> p (g n)"), True, True)
                            for hh in range(4)]), [tr('Bw'), tr('xdt2')], [PR[7]])
            for g in range(2):
                gs = slice(g * 64, g * 64 + 64)
                S.op('dve', TT(hst_t[gs, :, :], hstT[gs, :, :], edec[gs, 4 * g:4 * g + 4].unsqueeze(2).to_broadcast([64, 4, 64]),
                               ALU.mult), [hsR, tr('edec')], [tr('hst_t')])
                S.op('dve', TT(hstT[gs, :, :], hst_t[gs, :, :],
                               PS[7][gs, :].rearrange("p (h c) -> p h c", c=128)[:, :, g * 64:g * 64 + 64], ALU.add),
                     [tr('hst_t'), PR[7]], [hsR])
            S.op('dve', TT(ytmp[:L, :].rearrange("p (h d) -> p h d", d=64), xN[:L, :].rearrange("p (h d) -> p h d", d=64),
                           sd_bc[:L, :].unsqueeze(2).to_broadcast([L, 8, 64]), ALU.mult), [tr('xN'), prm], [tr('ytmp')])
            S.op('dve', TT(yy[:L, :], yy[:L, :], ytmp[:L, :], ALU.add), [tr('ytmp')], [tr('yy')])
            S.op('dve', TT(yy[:L, :], yy[:L, :], zs[:L, :], ALU.mult), [tr('zs')], [tr('yy')])
            S.op('act', ACTF(ytmp[:L, :], yy[:L, :], AF.Square), [tr('yy')], [tr('ytmp')])
            S.op('dve', RED(ss2[:L, :], ytmp[:L, :].rearrange("p (g d) -> p g d", g=2)), [tr('ytmp')], [tr('ss2')])
            S.op('act', ACTF(ss2[:L, :], ss2[:L, :], AF.Sqrt, bias=epsc[:L, :], scale=1.0 / 256), [tr('ss2'), constR], [tr('ss2')])
            S.op('dve', RECIP(ss2[:L, :], ss2[:L, :]), [tr('ss2')], [tr('ss2')])
            S.op('dve', TT(yy[:L, :].rearrange("p (g d) -> p g d", g=2), yy[:L, :].rearrange("p (g d) -> p g d", g=2),
                           ss2[:L, :].unsqueeze(2).to_broadcast([L, 2, 256]), ALU.mult), [tr('ss2')], [tr('yy')])
            S.op('dve', TT(mixoN[:L, 256:768], yy[:L, :], sn_bc[:L, :], ALU.mult), [tr('yy'), prm], [tr('mixoN')])
            if MSTEP < 8:
                continue
            S.op('pe', MMx(PS[0][:, 0:L], Wg[:16, :], glr[:16, :L]), [tr('glr'), prm], [PR[0]])
            S.op('act', ACTF(lg[:, :L], PS[0][:, 0:L], AF.Exp, bias=ngb[:, 0:1], scale=-1.0), [PR[0], prm], [tr('lg')])
            S.op('act', ACTF(lg[:, :L], lg[:, :L], AF.Ln, bias=onec), [tr('lg'), constR], [tr('lg')])
            S.op('dve', SCAN(cl[:, :L], ONE[:, :L], lg[:, :L]), [tr('lg'), constR], [tr('cl')])
            S.op('act', ACTF(eq[:, :L], cl[:, :L], AF.Exp, scale=-1.0 / 16), [tr('cl')], [tr('eq')])
            S.op('act', ACTF(ek[:, :L], cl[:, :L], AF.Exp, scale=1.0 / 16), [tr('cl')], [tr('ek')])
            S.op('dve', STT(qt[:, :L], gqk[:, 0, :L], float(32 ** -0.5), eq[:, :L], ALU.mult, ALU.mult),
                 [tr('gqk'), tr('eq')], [tr('qt')])
            S.op('dve', TT(kt[:, :L], gqk[:, 1, :L], ek[:, :L], ALU.mult), [tr('gqk'), tr('ek')], [tr('kt')])
            S.op('dve', TT(dl[:, :L], cl[:, L - 1:L].to_broadcast([128, L]), cl[:, :L], ALU.subtract), [tr('cl')], [tr('dl')])
            S.op('act', ACTF(dl[:, :L], dl[:, :L], AF.Exp, scale=-1.0 / 16), [tr('dl')], [tr('dl')])
            S.op('dve', TT(khT[:, :L], gqk[:, 1, :L], dl[:, :L], ALU.mult), [tr('gqk'), tr('dl')], [tr('khT')])
            S.op('act', ACTF(elast[:, :], cl[:, L - 1:L], AF.Exp, scale=-1.0 / 16), [tr('cl')], [tr('elast')])
            S.op('dve', TT(qblk[:, :, :L], qt[:, :L].unsqueeze(1).to_broadcast([128, 4, L]),
                           hm[:, :].unsqueeze(2).to_broadcast([128, 4, L]), ALU.mult), [tr('qt'), constR], [tr('qblk')])
            S.op('pe', MMx(PS[1][:L, 0:4 * L], kt[:, :L], qblk[:, :, :L]), [tr('kt'), tr('qblk')], [PR[1]])
            S.op('dve', TT(attm[:L, :, :L], PS[1][:L, 0:4 * L].rearrange("p (h l) -> p h l", l=L),
                           M01[:L, :L].unsqueeze(1).to_broadcast([L, 4, L]), ALU.mult), [PR[1], constR], [tr('attm')])
            S.op('act', CP(vg[:L, :], dtgv[:L, 8:264]), [tr('dtgv')], [tr('vg')])
            lst = []
            for h in range(4):
                lst.append((PS[0][:L, 256 + h * 64:256 + (h + 1) * 64], attm[:L, h, :L], vg[:L, h * 64:(h + 1) * 64], True, False))
                lst.append((PS[0][:L, 256 + h * 64:256 + (h + 1) * 64], qblk[:, h, :L], Sg_b[:, :], False, True))
            S.op('pe', MMS(lst), [tr('attm'), tr('vg'), tr('qblk'), SgR, tr('lg')], [PR[0]])
            S.op('pe', TRS([(PS[3][:L, 0:128], khT[:, :L], ident)]), [tr('khT'), constR], [PR[3]])
            S.op('act', CP(khN[:L, :], PS[3][:L, 0:128]), [PR[3]], [tr('khN')])
            S.op('pe', MMx(PS[7][:, 0:256], khN[:L, :], vg[:L, :]), [tr('khN'), tr('vg')], [PR[7]])
            S.op('dve', TT(tmpg[:, :, :], PS[7][:, 0:256].rearrange("p (h v) -> p h v", v=64),
                           hm[:, :].unsqueeze(2).to_broadcast([128, 4, 64]), ALU.mult), [PR[7], constR], [tr('tmpg')])
            S.op('dve', RED(red[:, :], tmpg[:, :, :].rearrange("p h v -> p v h")), [tr('tmpg')], [tr('red')])
            S.op('dve', STT(Sg[:, :], Sg[:, :], elast[:, 0:1], red[:, :], ALU.mult, ALU.add), [tr('red'), tr('elast')], [SgR])
            S.op('dve', CP(Sg_b[:, :], Sg[:, :]), [], [SgR])
            og_ps = PS[0][:L, 256:512]
            S.op('act', ACTF(og[:L, :], og_ps, AF.Square), [PR[0]], [tr('og')])
            S.op('dve', RED(ss4[:L, :], og[:L, :].rearrange("p (h v) -> p h v", v=64)), [tr('og')], [tr('ss4')])
            S.op('act', ACTF(ss4[:L, :], ss4[:L, :], AF.Sqrt, bias=epsc[:L, :], scale=1.0 / 64), [tr('ss4'), constR], [tr('ss4')])
            S.op('dve', RECIP(ss4[:L, :], ss4[:L, :]), [tr('ss4')], [tr('ss4')])
            S.op('dve', TT(og[:L, :].rearrange("p (h v) -> p h v", v=64), og_ps.rearrange("p (h v) -> p h v", v=64),
                           ss4[:L, :].unsqueeze(2).to_broadcast([L, 4, 64]), ALU.mult), [PR[0], tr('ss4')], [tr('og')])
            S.op('dve', TT(og[:L, :].rearrange("p (h v) -> p h v", v=64), og[:L, :].rearrange("p (h v) -> p h v", v=64),
                           gn_bc[:L, :].unsqueeze(1).to_broadcast([L, 4, 64]), ALU.mult), [prm], [tr('og')])
            S.op('dve', TT(mixoN[:L, 768:1024], og[:L, :], zg[:L, :], ALU.mult), [tr('og'), tr('zg')], [tr('mixoN')])
            if MSTEP < 9:
                continue
            for half in range(2):
                S.op('pe', TRS([(PS[4 + half][:, j * L:(j + 1) * L], mixoN[:L, (half * 4 + j) * 128:(half * 4 + j + 1) * 128],
                                 ident[:L, :L]) for j in range(4)]), [tr('mixoN'), constR], [PR[4 + half]])
                S.op('act' if half else 'dve', CP(mixoT[:, half * 4:half * 4 + 4, :L],
                                                  PS[4 + half][:, 0:4 * L].rearrange("p (j l) -> p j l", l=L)),
                     [PR[4 + half]], [tr('mixoT')])
            for oc in range(8):
                bo = 6 + oc % 2
                wb_ = oc % 2
                S.dma('pool', DMA(wo_t[wb_][:, :, :], wmo_v[:, :, oc * 128:(oc + 1) * 128]), [], [woR[wb_]])
                S.op('pe', MMX([(PS[bo][:, 0:L], wo_t[wb_][:, k, :], mixoT[:, k, :L], k == 0, k == 7)
                                for k in range(8)]), [woR[wb_], tr('mixoT')], [PR[bo]])
                S.op('dve', STT(xT[:, oc, col0:col0 + L], PS[bo][:, 0:L], 1.0, xT[:, oc, col0:col0 + L], ALU.mult, ALU.add), [PR[bo]], [xcR])
        if MSTEP < 9.5:
            S.barrier()
            return
        S.op('pe', TRS([(PS[0][:64, hh * 128:(hh + 1) * 128], hstT[:, hh, :], ident) for hh in range(4)]),
             [hsR, constR], [PR[0]])
        S.op('act', CP(sso[:64, :, :], PS[0][:64, :].rearrange("p (h n) -> p h n", n=128)), [PR[0]], [tr('sso')])
        for g in range(2):
            if MSTEP < 9.6:
                continue
            S.dma('sp', DMA(ssmp_d[l].rearrange("(g hh p) n -> g p hh n", g=2, hh=4)[g], sso[:64, :, g * 64:(g + 1) * 64]),
                  [tr('sso')], [])
        if MSTEP < 9.7:
            S.barrier()
            return
        S.op('act', CP(red[:, :], Sg[:, :]), [SgR], [tr('red')])
        S.dma('pool', DMA(glap_d[l], red[:, :]), [tr('red')], [])
        S.barrier()
        if SAMPLE:
            sample_chunk()
            S.barrier()

    load_phase()
    S.barrier()
    for l in range(DEPTH):
        if STOP >= 1:
            ffn_phase(l, 0)
        if STOP >= 2 and MIXER:
            mixer_phase(l)
        if STOP >= 3:
            ffn_phase(l, 1)
    store_phase()
    S.barrier()

    if cfg.get('SIMCHK', True):
        semv = {}
        pos = {e: 0 for e in ENG}
        progress = True
        while progress:
            progress = False
            for e in ENG:
                q = S.q[e]
                while pos[e] < len(q):
                    waits, fn, tok, inc = q[pos[e]]
                    if all(semv.get(k, 0) >= v for k, v in waits):
                        if fn is not None:
                            semv[tok[0]] = semv.get(tok[0], 0) + inc
                            assert semv[tok[0]] == tok[1], (tok, semv[tok[0]])
                        pos[e] += 1
                        progress = True
                    else:
                        break
        for e in ENG:
            if pos[e] < len(S.q[e]):
                print('DEADLOCK', e, pos[e], len(S.q[e]), S.q[e][pos[e]][0], {k: semv.get(k, 0) for k, _ in S.q[e][pos[e]][0]})
        print('max sem', max(semv.values()), 'max waits/instr', max(len(w) for e in ENG for (w, _, _, _) in S.q[e]))

    import contextlib
    keys = S.sem_keys()
    with contextlib.ExitStack() as es:
        sems = {k: es.enter_context(nc.semaphore(k)) for k in keys}
        block = es.enter_context(nc.Block())

        def run(e, name):
            for waits, fn, tok, inc in S.q[name]:
                for k, v in waits:
                    e.wait_ge(sems[k], v)
                if fn is not None:
                    fn(e).then_inc(sems[tok[0]], inc)

        @block.tensor
        def _(e):
            run(e, 'pe')

        @block.scalar
        def _(e):
            run(e, 'act')

        @block.vector
        def _(e):
            run(e, 'dve')

        @block.gpsimd
        def _(e):
            run(e, 'pool')

        @block.sync
        def _(e):
            run(e, 'sp')
    print('SBUF peak bytes/partition', st['peak'] - SB_BASE, 'ops', {k: len(v) for k, v in S.q.items()})
    return nc


def prep_inputs(cfg, inp):
    NCH, NS, NPG, NPOOL, NCORES, DEPTH = (cfg[k] for k in ('NCH', 'NS', 'NPG', 'NPOOL', 'NCORES', 'DEPTH'))
    f = lambda a: np.ascontiguousarray(np.asarray(a), dtype=np.float32)
    consts = make_consts(NS, NPG)
    shared = {
        'meta': f(inp['meta_tokens']),
        'w1i': f(inp['ffn1_w_in']), 'w1o': f(inp['ffn1_w_out']),
        'w2i': f(inp['ffn2_w_in']), 'w2o': f(inp['ffn2_w_out']),
        'wmi': f(inp['w_mix_in']), 'wmo': f(inp['w_mix_out']),
        'gains': np.ascontiguousarray(np.stack([f(inp['ffn1_norm']), f(inp['mix_norm']), f(inp['ffn2_norm'])], axis=1)),
        'fqn': f(inp['fox_q_norm']), 'fkn': f(inp['fox_k_norm']), 'fb': f(inp['fox_f_bias']),
        'cw': f(inp['ssd_conv_w']), 'cb': f(inp['ssd_conv_b']), 'dtb': f(inp['ssd_dt_bias']),
        'alog': f(inp['ssd_a_log']), 'sd': f(inp['ssd_d']), 'sn': f(inp['ssd_norm']),
        'wg': f(inp['gla_w_gate']), 'gb': f(inp['gla_gate_bias']), 'gn': f(inp['gla_norm']),
        'cst': np.ascontiguousarray(np.stack([consts[n] for n in CONST_ORDER + (CONST_S if cfg.get('SAMPLE', False) else [])])),
        'hmc': consts['hm'], 'seqm': consts['seqmask'], 'rowm': consts['rowmask'], 'rstm': consts['rst'],
        'pairm': consts['pairmask'],
    }
    if cfg.get('SAMPLE', False):
        shared['rowm32'] = consts['rowmask32']; shared['rst128'] = consts['rst128']
    if cfg.get('GATHER', False):
        shared['radd'] = consts['radd']
        shared['ck'] = f(inp['cache_fox_k']).reshape(DEPTH, NPOOL * 128, 256)
        shared['cv'] = f(inp['cache_fox_v']).reshape(DEPTH, NPOOL * 128, 256)
        shared['clf'] = f(inp['cache_fox_logf']).reshape(DEPTH, NPOOL * 128, 4)
    maps = []
    xp = f(inp['x_prompt'])
    xs = f(inp['x_sample'])
    pt = np.ascontiguousarray(np.asarray(inp['page_table']), dtype=np.int32)
    sssm = f(inp['state_ssm'])
    sconv = f(inp['state_conv'])
    sgla = f(inp['state_gla'])
    for c in range(NCORES):
        m = dict(shared)
        m['xp'] = np.ascontiguousarray(xp[c])
        m['xs'] = np.ascontiguousarray(xs[c * NS:(c + 1) * NS].reshape(NS * 4, D))
        m['pt'] = np.ascontiguousarray(pt[c * NS:(c + 1) * NS])
        m['sssm'] = np.ascontiguousarray(sssm[:, c * NS:(c + 1) * NS])
        m['sconv'] = np.ascontiguousarray(sconv[:, c * NS:(c + 1) * NS].reshape(DEPTH, NS * 3, 768))
        m['sgla'] = np.ascontiguousarray(sgla[:, c * NS:(c + 1) * NS].reshape(DEPTH, NS, 128, 64))
        maps.append(m)
    return maps


def gather_outputs(cfg, res):
    NCH, NS, NCORES, DEPTH = (cfg[k] for k in ('NCH', 'NS', 'NCORES', 'DEPTH'))
    NP = NCH * 128
    TP = 16 + NP
    R = res.results
    cat = lambda key, ax: np.concatenate([np.asarray(r[key]) for r in R], axis=ax)
    stk = lambda key: np.stack([np.asarray(r[key]) for r in R], axis=1)
    yp = np.stack([np.asarray(r['yp']) for r in R], axis=0)
    ys = cat('ys', 0).reshape(NCORES * NS, 4, D)
    kp = stk('kp').reshape(DEPTH, NCORES, TP, 4, 64)
    vp = stk('vp').reshape(DEPTH, NCORES, TP, 4, 64)
    lfp = stk('lfp').reshape(DEPTH, NCORES, TP, 4)
    ssmp = stk('ssmp').reshape(DEPTH, NCORES, 8, 64, 64)
    convp = stk('convp').reshape(DEPTH, NCORES, 3, 768)
    glap = stk('glap').reshape(DEPTH, NCORES, 4, 32, 64)
    ks = cat('ks', 1).reshape(DEPTH, NCORES * NS, 4, 4, 64)
    vs = cat('vs', 1).reshape(DEPTH, NCORES * NS, 4, 4, 64)
    lfs = cat('lfs', 1).reshape(DEPTH, NCORES * NS, 4, 4)
    ssms = cat('ssms', 1).reshape(DEPTH, NCORES * NS, 8, 64, 64)
    convs = cat('convs', 1).reshape(DEPTH, NCORES * NS, 3, 768)
    glas = cat('glas', 1).reshape(DEPTH, NCORES * NS, 4, 32, 64)
    outs = (yp, ys, kp, vp, lfp, ssmp, convp, glap, ks, vs, lfs, ssms, convs, glas)
    return tuple(np.ascontiguousarray(o, dtype=np.float32) for o in outs)


def kernel(**inputs):
    cfg = dict(CFG)
    nc = build(cfg)
    maps = prep_inputs(cfg, inputs)
    res = run_bass_kernel_spmd(nc, maps, core_ids=list(range(cfg['NCORES'])))
    return gather_outputs(cfg, res)
```

```python
import numpy as np
import ml_dtypes
import concourse.bass as bass
import concourse.mybir as mybir
from concourse.bass_utils import run_bass_kernel_spmd

F32 = mybir.dt.float32
BF16 = mybir.dt.bfloat16
I32 = mybir.dt.int32
AF = mybir.ActivationFunctionType
ALU = mybir.AluOpType
AX = mybir.AxisListType

CFG = dict(NCH=16, NS=16, NPG=16, NPOOL=2560, NCORES=8, DEPTH=2, STOP=99, MIXER=True, SAMPLE=True, GATHER=True)
D = 1024
KC = 8
DFF = 2816
GSZ = 2
NIN = 2844
EPS = 1e-6
ENG = ('pe', 'act', 'dve', 'pool', 'sp')
SAME_SYNC = True


class Res:
    __slots__ = ('w', 'r', 'excl')

    def __init__(self, excl=False):
        self.w = None
        self.r = {}
        self.excl = excl


class Sched:
    def __init__(self, ndma=8):
        self.q = {e: [] for e in ENG}
        self.cnt = {e: 0 for e in ENG}
        self.seen = {e: {} for e in ENG}
        self.ndma = ndma
        self.dcnt = {qn: [0] * ndma for qn in ('sp', 'pool', 'act')}
        self.drr = {qn: 0 for qn in ('sp', 'pool', 'act')}

    def _waits(self, eng, toks):
        need = {}
        for (k, v) in toks:
            if v <= 0:
                continue
            if k == eng and (eng == 'pe' or not SAME_SYNC):
                continue
            if need.get(k, 0) < v:
                need[k] = v
        out = []
        for k, v in need.items():
            if self.seen[eng].get(k, 0) >= v:
                continue
            self.seen[eng][k] = v
            out.append((k, v))
        return out

    def _collect(self, reads, writes):
        toks = []
        for r in reads:
            if r.w is not None:
                toks.append(r.w)
        for w in writes:
            if w.w is not None:
                toks.append(w.w)
            toks.extend(w.r.items())
        return toks

    def _commit(self, tok, reads, writes):
        for r in reads:
            if r.r.get(tok[0], 0) < tok[1]:
                r.r[tok[0]] = tok[1]
        for w in writes:
            w.w = tok
            w.r = {}

    def op(self, eng, fn, reads=(), writes=()):
        ex = [r for r in reads if r.excl]
        if ex:
            reads = [r for r in reads if not r.excl]
            writes = list(writes) + ex
        toks = self._collect(reads, writes)
        self.cnt[eng] += 1
        tok = (eng, self.cnt[eng])
        self.q[eng].append((self._waits(eng, toks), fn, tok, 1))
        self._commit(tok, reads, writes)
        return tok

    def dma(self, qn, fn, reads=(), writes=()):
        k = self.drr[qn]
        self.drr[qn] = (k + 1) % self.ndma
        key = 'd_%s_%d' % (qn, k)
        toks = self._collect(reads, writes)
        toks.append((key, self.dcnt[qn][k]))
        self.dcnt[qn][k] += 16
        tok = (key, self.dcnt[qn][k])
        self.q[qn].append((self._waits(qn, toks), fn, tok, 16))
        self._commit(tok, reads, writes)
        return tok

    def barrier(self):
        alltoks = [(e, self.cnt[e]) for e in ENG]
        for qn in self.dcnt:
            for k in range(self.ndma):
                alltoks.append(('d_%s_%d' % (qn, k), self.dcnt[qn][k]))
        for e in ENG:
            w = self._waits(e, alltoks)
            if w:
                self.q[e].append((w, None, None, 0))

    def sem_keys(self):
        ks = ['pe', 'act', 'dve', 'pool']
        for qn in self.dcnt:
            for k in range(self.ndma):
                ks.append('d_%s_%d' % (qn, k))
        return ks


def MMS(lst, explicit=False):
    def f(e):
        ins = None
        for ent in lst:
            (o, l, r, st, sp) = ent[:5]
            if explicit or (len(ent) > 5 and ent[5] is not None):
                base = ent[5] if len(ent) > 5 else 0
                ins = e.matmul(o, l, r, start=st, stop=sp, tile_position=(base, 0))
            else:
                ins = e.matmul(o, l, r, start=st, stop=sp)
        return ins
    return f


def MMX(lst):
    return MMS(lst, explicit=False)


def MM(o, l, r, st=True, sp=True):
    return MMS([(o, l, r, st, sp)])


def MMx(o, l, r, st=True, sp=True):
    return MMS([(o, l, r, st, sp)], explicit=False)


def TRS(lst):
    def f(e):
        ins = None
        for (o, i, idn) in lst:
            ins = e.transpose(o, i, idn)
        return ins
    return f


def ACTF(out, in_, func, bias=None, scale=None):
    def f(e):
        kw = {}
        if bias is not None:
            kw['bias'] = bias
        if scale is not None:
            kw['scale'] = scale
        return e.activation(out=out, in_=in_, func=func, **kw)
    return f


def TT(out, a, b, op):
    return lambda e: e.tensor_tensor(out=out, in0=a, in1=b, op=op)


def TS(out, a, s1, op0, s2=None, op1=None):
    def f(e):
        if op1 is None:
            return e.tensor_scalar(out=out, in0=a, scalar1=s1, scalar2=None, op0=op0)
        return e.tensor_scalar(out=out, in0=a, scalar1=s1, scalar2=s2, op0=op0, op1=op1)
    return f


def STT(out, in0, scalar, in1, op0, op1):
    return lambda e: e.scalar_tensor_tensor(out=out, in0=in0, scalar=scalar, in1=in1, op0=op0, op1=op1)


def CP(out, in_):
    def f(e):
        if hasattr(e, 'tensor_copy'):
            return e.tensor_copy(out=out, in_=in_)
        return e.activation(out=out, in_=in_, func=AF.Copy)
    return f


def RECIP(out, in_):
    return lambda e: e.reciprocal(out=out, in_=in_)


def RED(out, in_, op=None):
    return lambda e: e.tensor_reduce(out=out, in_=in_, axis=AX.X, op=(op or ALU.add))


def SCAN(out, d0, d1):
    return lambda e: e.tensor_tensor_scan(out=out, data0=d0, data1=d1, initial=0.0, op0=ALU.mult, op1=ALU.add)


def MSET(ap, v):
    return lambda e: e.memset(ap, v)


def IDMA(out, in_, idx):
    return lambda e: e.indirect_dma_start(out=out, out_offset=None, in_=in_, in_offset=bass.IndirectOffsetOnAxis(ap=idx, axis=0))


def DMA(out, in_, nonc=False):
    def f(e):
        if nonc:
            return e.dma_start(out=out, in_=in_, allow_slow_non_contiguous=True)
        return e.dma_start(out=out, in_=in_)
    return f


def token_tiles(tok):
    t = []
    c = 0
    while c < tok:
        n = min(512, tok - c)
        t.append((c, n))
        c += n
    return t


O_FQ, O_FK, O_FV, O_FF, O_SZ, O_XBC, O_DT, O_GQ, O_GK, O_GV, O_LR, O_GG = (
    0, 256, 512, 768, 772, 1284, 2052, 2060, 2188, 2316, 2572, 2588)


def make_consts(NS, NPG=16):
    c = {}
    c['ident'] = np.eye(128, dtype=np.float32)
    c['ones'] = np.ones((128, 128), np.float32)
    k = np.arange(128)
    U = (k[:, None] <= k[None, :]).astype(np.float32)
    c['U'] = U
    c['LT'] = np.ascontiguousarray(U.T)
    c['negm'] = np.where(U > 0, 0.0, -30000.0).astype(np.float32)
    c['m01'] = U.copy()
    ns4 = NS * 4
    s = np.arange(128)
    same = (s[:, None] // 4 == s[None, :] // 4)
    Us = (same & (s[:, None] <= s[None, :])).astype(np.float32)
    pad = lambda a: np.pad(a, ((0, 128 - a.shape[0]), (0, 128 - a.shape[1])))
    c['Us'] = pad(Us)
    c['blk'] = pad(same.astype(np.float32))
    c['negms'] = pad(np.where(Us > 0, 0.0, -30000.0).astype(np.float32))
    c['m01s'] = pad(Us)
    c['bd64'] = (k[:, None] // 64 == k[None, :] // 64).astype(np.float32)
    order = (k % NPG) * 8 + k // NPG
    c['Mgt'] = (k[:, None] > k[None, :]).astype(np.float32)
    c['radd'] = k.astype(np.float32).reshape(128, 1)
    hm = np.zeros((128, 4), np.float32)
    hm[k, k // 32] = 1.0
    c['hm'] = hm
    sm = np.zeros((128, NS, ns4), np.float32)
    for b in range(NS):
        sm[:, b, 4 * b:4 * b + 4] = 1.0
    c['seqmask'] = sm.reshape(128, NS * ns4)
    rm = np.zeros((128, 32), np.float32)
    rm[s, s // 4] = 1.0
    c['rowmask'] = rm[:, :NS].copy()
    c['rowmask32'] = rm
    rs = np.ones((128, 128), np.float32)
    rs[:, 0::4] = 0.0
    c['rst'] = rs[:, :ns4].copy()
    c['rst128'] = rs
    s = np.arange(ns4)
    sel = np.zeros((128, 128), np.float32)
    sel[:ns4] = ((s[:, None] // 4) % 2 == (k[None, :] // 64)).astype(np.float32)
    c['sel'] = sel
    pm = np.zeros((128, max(NS // 2, 1)), np.float32)
    pm[s, s // 8] = 1.0
    c['pairmask'] = pm
    return c


CONST_ORDER = ['ident', 'ones', 'U', 'negm', 'm01', 'bd64']
CONST_S = ['Us', 'blk', 'negms', 'Mgt']


def build(cfg):
    NCH, NS, NPG, NPOOL, DEPTH = cfg['NCH'], cfg['NS'], cfg['NPG'], cfg['NPOOL'], cfg['DEPTH']
    STOP = cfg.get('STOP', 99)
    MSTEP = cfg.get('MSTEP', 99)
    MIXER = cfg.get('MIXER', False)
    NP = NCH * 128
    NS4 = NS * 4
    TP = 16 + NP
    TOK = TP + NS4
    NBP = NS // 2
    nc = bass.Bass("TRN2", target_bir_lowering=False)
    S = Sched()

    def din(name, shape, dt=F32):
        return nc.dram_tensor(name, list(shape), dt, kind="ExternalInput").ap()

    def dout(name, shape, dt=F32):
        return nc.dram_tensor(name, list(shape), dt, kind="ExternalOutput").ap()

    xp_d = din('xp', [NP, D])
    xs_d = din('xs', [NS4, D])
    meta_d = din('meta', [16, D])
    if cfg.get('GATHER', False):
        ck_d = din('ck', [DEPTH, NPOOL * 128, 256])
        cv_d = din('cv', [DEPTH, NPOOL * 128, 256])
        clf_d = din('clf', [DEPTH, NPOOL * 128, 4])
    sssm_d = din('sssm', [DEPTH, NS, 8, 64, 64])
    sconv_d = din('sconv', [DEPTH, NS * 3, 768])
    sgla_d = din('sgla', [DEPTH, NS, 128, 64])
    pt_d = din('pt', [NS, NPG], I32)
    w1i_d = din('w1i', [DEPTH, D, 2 * DFF])
    w1o_d = din('w1o', [DEPTH, DFF, D])
    w2i_d = din('w2i', [DEPTH, D, 2 * DFF])
    w2o_d = din('w2o', [DEPTH, DFF, D])
    wmi_d = din('wmi', [DEPTH, D, NIN])
    wmo_d = din('wmo', [DEPTH, D, D])
    gains_d = din('gains', [DEPTH, 3, D])
    fqn_d = din('fqn', [DEPTH, 64])
    fkn_d = din('fkn', [DEPTH, 64])
    fb_d = din('fb', [DEPTH, 4])
    cw_d = din('cw', [DEPTH, 4, 768])
    cb_d = din('cb', [DEPTH, 768])
    dtb_d = din('dtb', [DEPTH, 8])
    alog_d = din('alog', [DEPTH, 8])
    sd_d = din('sd', [DEPTH, 8])
    sn_d = din('sn', [DEPTH, 512])
    wg_d = din('wg', [DEPTH, 16, 128])
    gb_d = din('gb', [DEPTH, 128])
    gn_d = din('gn', [DEPTH, 64])
    SAMPLE = cfg.get('SAMPLE', False)
    CORD = CONST_ORDER + (CONST_S if SAMPLE else [])
    cst_d = din('cst', [len(CORD), 128, 128])
    GATHER = cfg.get('GATHER', False)
    if GATHER:
        radd_d = din('radd', [128, 1])
    if SAMPLE:
        rowm32_d = din('rowm32', [128, 32])
        rst128_d = din('rst128', [128, 128])
    hm_d = din('hmc', [128, 4])
    seqm_d = din('seqm', [128, NS * NS4])
    rowm_d = din('rowm', [128, NS])
    rst_d = din('rstm', [128, NS4])
    pairm_d = din('pairm', [128, max(NBP, 1)])

    yp_d = dout('yp', [NP, D])
    ys_d = dout('ys', [NS4, D])
    kp_d = dout('kp', [DEPTH, TP, 256])
    vp_d = dout('vp', [DEPTH, TP, 256])
    lfp_d = dout('lfp', [DEPTH, TP, 4])
    ssmp_d = dout('ssmp', [DEPTH, 512, 64])
    convp_d = dout('convp', [DEPTH, 3, 768])
    glap_d = dout('glap', [DEPTH, 128, 64])
    ks_d = dout('ks', [DEPTH, NS4, 256])
    vs_d = dout('vs', [DEPTH, NS4, 256])
    lfs_d = dout('lfs', [DEPTH, NS4, 4])
    ssms_d = dout('ssms', [DEPTH, NS, 8, 64, 64])
    convs_d = dout('convs', [DEPTH, NS, 3, 768])
    glas_d = dout('glas', [DEPTH, NS, 128, 64])
    xsp_d = nc.dram_tensor('xspill', [128, 8, TOK], F32, kind="Internal").ap()

    SB_BASE, SB_END = 16512, cfg.get("SB_END", 229376 - 256)
    st = {'p': SB_BASE, 'n': 0, 'peak': SB_BASE}

    def sb(shape, dt=F32, at=None):
        if at is not None:
            st['n'] += 1
            return nc.alloc_sbuf_tensor_at('t%d' % st['n'], list(shape), dt, offset=(SB_BASE if cfg.get('DRY') else at))
        nbytes = int(np.prod(shape[1:])) * (2 if dt == BF16 else 4)
        off = (st['p'] + 63) // 64 * 64
        st['last'] = off
        st['p'] = off + nbytes
        st['peak'] = max(st['peak'], st['p'])
        if cfg.get('DRY'):
            st['n'] += 1
            return nc.alloc_sbuf_tensor_at('t%d' % st['n'], list(shape), dt, offset=SB_BASE)
        assert st['p'] <= SB_END, ('SBUF overflow', st['p'] - SB_END)
        st['n'] += 1
        import sys as _s
        return nc.alloc_sbuf_tensor_at('t%d_%d' % (st['n'], _s._getframe(1).f_lineno), list(shape), dt, offset=off)

    PS = [nc.alloc_psum_tensor('ps%d' % i, [128, 512], F32) for i in range(8)]
    PR = [Res(excl=True) for _ in range(8)]

    xT = sb([128, 8, TOK])
    xR = None
    cst = sb([128, len(CORD), 128])
    C = {n: cst[:, i, :] for i, n in enumerate(CORD)}
    ones_b = sb([128, 128], BF16)
    bd64_b = sb([128, 128], BF16)
    hm = sb([128, 4])
    rowm = sb([128, NS])
    rstm = sb([128, NS4])
    pairm = sb([128, max(NBP, 1)])
    gains = sb([128, DEPTH * 3, 8])
    cvals = sb([128, 4])
    constR = Res()

    S.dma('sp', DMA(cst[:, :, :], cst_d.rearrange("c p n -> p c n")), [], [constR])
    S.dma('pool', DMA(ones_b[:, :], cst_d[CONST_ORDER.index('ones')]), [], [constR])
    S.dma('pool', DMA(bd64_b[:, :], cst_d[CONST_ORDER.index('bd64')]), [], [constR])
    S.dma('sp', DMA(hm[:, :], hm_d), [], [constR])
    S.dma('sp', DMA(rowm[:, :], rowm_d), [], [constR])
    S.dma('sp', DMA(rstm[:, :], rst_d), [], [constR])
    S.dma('sp', DMA(pairm[:, :], pairm_d), [], [constR])
    if SAMPLE:
        rowm32 = sb([128, 32]); rst128 = sb([128, 128])
        S.dma('sp', DMA(rowm32[:, :], rowm32_d), [], [constR])
        S.dma('sp', DMA(rst128[:, :], rst128_d), [], [constR])
    S.dma('sp', DMA(gains[:, :, :], gains_d.rearrange("l w (c p) -> p (l w) c", p=128), nonc=True), [], [constR])
    S.op('dve', MSET(cvals[:, 0:1], EPS), [], [constR])
    S.op('dve', MSET(cvals[:, 1:2], 1.0), [], [constR])
    S.op('dve', MSET(cvals[:, 2:3], 0.0), [], [constR])
    epsc = cvals[:, 0:1]
    onec = cvals[:, 1:2]
    ident = C['ident']

    mark_persist = st['p']

    def load_phase():
        xin = [sb([128, D]) for _ in range(2)]
        xinR = [Res(), Res()]
        groups = [(meta_d, 16, 0)]
        for c in range(NCH):
            groups.append((xp_d[c * 128:(c + 1) * 128, :], 128, 16 + c * 128))
        groups.append((xs_d, NS4, TP))
        for gi, (src, L, col0) in enumerate(groups):
            b = gi % 2
            S.dma('sp', DMA(xin[b][:L, :], src), [], [xinR[b]])
            for half in range(2):
                pb = 2 * (gi % 2) + half
                S.op('pe', TRS([(PS[pb][:, j * L:(j + 1) * L], xin[b][:L, (half * 4 + j) * 128:(half * 4 + j + 1) * 128],
                                 ident[:L, :L]) for j in range(4)]), [xinR[b], constR], [PR[pb]])
                S.op('act' if half else 'dve',
                     CP(xT[:, half * 4:half * 4 + 4, col0:col0 + L],
                        PS[pb][:, 0:4 * L].rearrange("p (j l) -> p j l", l=L)), [PR[pb]], [xTall])

    xTall = Res()

    def norm_T(src3, n, gidx, dst3, sq, tmp, rstd, psb, rd, wr):
        S.op('act', ACTF(sq[:, :, :n], src3, AF.Square), rd, [sqR])
        S.op('pe', MMS([(PS[psb][:, :n], ones_b[:, :], sq[:, k, :n], k == 0, k == 7) for k in range(8)]),
             [sqR, constR], [PR[psb]])
        S.op('act', ACTF(rstd[:, :n], PS[psb][:, :n], AF.Sqrt, bias=epsc, scale=1.0 / D), [PR[psb], constR], [rstdR])
        S.op('dve', RECIP(rstd[:, :n], rstd[:, :n]), [rstdR], [rstdR])
        S.op('dve', TT(tmp[:, :, :n], src3, rstd[:, None, :n].to_broadcast([128, 8, n]), ALU.mult),
             list(rd) + [rstdR], [tmpR])
        S.op('dve', TT(dst3, tmp[:, :, :n], gains[:, gidx, :, None].to_broadcast([128, 8, n]), ALU.mult),
             [tmpR, constR], wr)

    sqR, rstdR, tmpR = Res(), Res(), Res()

    def ffn_phase(l, which):
        st['p'] = mark_persist
        wi_d = (w1i_d if which == 0 else w2i_d)[l]
        wo_d = (w1o_d if which == 0 else w2o_d)[l]
        gidx = l * 3 + (0 if which == 0 else 2)
        tiles = token_tiles(TOK)
        NT = len(tiles)
        hT = sb([128, 8, TOK], BF16)
        hR = [Res() for _ in tiles]
        xtR = [Res() for _ in tiles]
        sq = sb([128, 8, 512], BF16)
        tmp = sb([128, 8, 512])
        rstd = sb([128, 512])
        NB = 3
        wi = [sb([128, 8, 512], BF16) for _ in range(NB)]
        wo = [sb([128, GSZ, D], BF16) for _ in range(NB)]
        wR = [Res() for _ in range(NB)]
        act = [sb([128, GSZ, TOK], BF16) for _ in range(2)]
        actR = [[Res() for _ in tiles] for _ in range(2)]
        sg = [sb([128, 512], BF16) for _ in range(2)]
        sgR = [Res(), Res()]
        for ti, (c0, n) in enumerate(tiles):
            norm_T(xT[:, :, c0:c0 + n], n, gidx, hT[:, :, c0:c0 + n], sq, tmp, rstd, 6, [xtR[ti]], [hR[ti]])
        NG = DFF // (128 * GSZ)
        wi_v = wi_d.rearrange("(k p) n -> p k n", p=128)
        wo_v = wo_d.rearrange("(j p) n -> p j n", p=128)
        cnt = 0
        for g in range(NG):
            b = g % NB
            S.dma('pool', DMA(wi[b][:, :, 0:256], wi_v[:, :, g * 256:(g + 1) * 256]), [], [wR[b]])
            S.dma('pool', DMA(wi[b][:, :, 256:512], wi_v[:, :, DFF + g * 256:DFF + (g + 1) * 256]), [], [wR[b]])
            S.dma('pool', DMA(wo[b][:, :, :], wo_v[:, g * GSZ:(g + 1) * GSZ, :]), [], [wR[b]])
            ab = g % 2
            for j in range(GSZ):
                for ti, (c0, n) in enumerate(tiles):
                    pg, pu = cnt % 2, 2 + cnt % 2
                    sb_ = cnt % 2
                    cnt += 1
                    S.op('pe', MMS([(PS[pg][:, :n], wi[b][:, k, j * 128:(j + 1) * 128], hT[:, k, c0:c0 + n], k == 0, k == 7)
                                    for k in range(8)]), [wR[b], hR[ti]], [PR[pg]])
                    S.op('pe', MMS([(PS[pu][:, :n], wi[b][:, k, 256 + j * 128:256 + (j + 1) * 128], hT[:, k, c0:c0 + n],
                                     k == 0, k == 7) for k in range(8)]), [wR[b], hR[ti]], [PR[pu]])
                    S.op('act', ACTF(sg[sb_][:, :n], PS[pg][:, :n], AF.Silu), [PR[pg]], [sgR[sb_]])
                    S.op('dve', TT(act[ab][:, j, c0:c0 + n], PS[pu][:, :n], sg[sb_][:, :n], ALU.mult),
                         [PR[pu], sgR[sb_]], [actR[ab][ti]])
            oc_cnt = 0
            for oc in range(8):
                for ti, (c0, n) in enumerate(tiles):
                    po = 4 + oc_cnt % 2
                    oc_cnt += 1
                    S.op('pe', MMS([(PS[po][:, :n], wo[b][:, j, oc * 128:(oc + 1) * 128], act[ab][:, j, c0:c0 + n],
                                     j == 0, j == GSZ - 1) for j in range(GSZ)]), [wR[b], actR[ab][ti]], [PR[po]])
                    S.op('dve', STT(xT[:, oc, c0:c0 + n], PS[po][:, :n], 0.5, xT[:, oc, c0:c0 + n], ALU.mult, ALU.add),
                         [PR[po]], [xtR[ti]])
        S.barrier()

    def store_phase():
        st['p'] = mark_persist
        xo = [sb([128, D]) for _ in range(2)]
        xoR = [Res(), Res()]
        groups = []
        for c in range(NCH):
            groups.append((yp_d[c * 128:(c + 1) * 128, :], 128, 16 + c * 128))
        groups.append((ys_d, NS4, TP))
        for gi, (dst, L, col0) in enumerate(groups):
            b = gi % 2
            for half in range(2):
                pb = 2 * (gi % 2) + half
                S.op('pe', TRS([(PS[pb][:L, j * 128:(j + 1) * 128], xT[:, half * 4 + j, col0:col0 + L], ident)
                                for j in range(4)]), [constR], [PR[pb]])
                S.op('act' if half else 'dve', CP(xo[b][:L, half * 512:(half + 1) * 512], PS[pb][:L, :]),
                     [PR[pb]], [xoR[b]])
            S.dma('sp', DMA(dst, xo[b][:L, :]), [xoR[b]], [])


    def mixer_phase(l):
        st['p'] = mark_persist
        R_ = Res
        Wmi = sb([128, 8, NIN], BF16)
        wo_t = [sb([128, 8, 128], BF16) for _ in range(2)]
        woR = [R_(), R_()]
        wmo_v = wmo_d[l].rearrange("(k p) n -> p k n", p=128)
        wRm = R_()
        wmi_v = wmi_d[l].rearrange("(k p) n -> p k n", p=128)
        for c0_ in range(0, NIN, 948):
            S.dma('pool', DMA(Wmi[:, :, c0_:c0_ + 948], wmi_v[:, :, c0_:c0_ + 948]), [], [wRm])
        prm = R_()
        gcol4 = sb([128, 4])
        for c4, src in ((0, fqn_d), (1, fqn_d), (2, fkn_d), (3, fkn_d)):
            for hf in range(2):
                S.dma('sp', DMA(gcol4[hf * 64:(hf + 1) * 64, c4:c4 + 1], src[l].rearrange("(p o) -> p o", o=1), nonc=True),
                      [], [prm])
        fb_bc = sb([128, 4]); S.dma('sp', DMA(fb_bc[:, :], fb_d[l].partition_broadcast(128)), [], [prm])
        cw = sb([128, 4, 6])
        for tp_ in range(4):
            S.dma('sp', DMA(cw[:, tp_, :], cw_d[l, tp_].rearrange("(j p) -> p j", p=128), nonc=True), [], [prm])
        cbb = sb([128, 6]); S.dma('sp', DMA(cbb[:, :], cb_d[l].rearrange("(j p) -> p j", p=128), nonc=True), [], [prm])
        dtb_bc = sb([128, 8]); S.dma('sp', DMA(dtb_bc[:, :], dtb_d[l].partition_broadcast(128)), [], [prm])
        a_bc = sb([128, 8]); S.dma('sp', DMA(a_bc[:, :], alog_d[l].partition_broadcast(128)), [], [prm])
        sd_bc = sb([128, 8]); S.dma('sp', DMA(sd_bc[:, :], sd_d[l].partition_broadcast(128)), [], [prm])
        sn_bc = sb([128, 512]); S.dma('sp', DMA(sn_bc[:, :], sn_d[l].partition_broadcast(128)), [], [prm])
        gn_bc = sb([128, 64]); S.dma('sp', DMA(gn_bc[:, :], gn_d[l].partition_broadcast(128)), [], [prm])
        Wg = sb([128, 128]); S.dma('sp', DMA(Wg[:16, :], wg_d[l]), [], [prm])
        ngb = sb([128, 1]); S.dma('sp', DMA(ngb[:, :], gb_d[l].rearrange("(p o) -> p o", o=1), nonc=True), [], [prm])
        S.op('act', ACTF(a_bc[:, :], a_bc[:, :], AF.Exp), [prm], [prm])
        S.op('dve', TS(a_bc[:, :], a_bc[:, :], -1.0, ALU.mult), [prm], [prm])
        S.op('dve', TS(ngb[:, :], ngb[:, :], -1.0, ALU.mult), [prm], [prm])

        NCK = NCH + 1
        kT_all = sb([128, 2, TP], BF16); off_kT = st['last']
        V_all = sb([128, NCK, 4, 80], BF16); R1 = (off_kT, st['p'] - off_kT)
        F_all = sb([128, NCK, 4])
        Ftot = sb([128, 4])
        hstT = sb([128, 4, 64])
        Sg = sb([128, 64])
        Sg_b = sb([128, 64], BF16)
        kTR, VR, FR, FtR, hsR, SgR = R_(), R_(), R_(), R_(), R_(), R_()
        S.op('dve', MSET(V_all[:, :, :, 64:65], 1.0), [], [VR])
        S.op('dve', MSET(Ftot[:, :], 0.0), [], [FtR])
        S.op('dve', MSET(hstT[:, :, :], 0.0), [], [hsR])
        S.op('dve', MSET(Sg[:, :], 0.0), [], [SgR])
        S.op('dve', MSET(Sg_b[:, :], 0.0), [], [SgR])
        S.op('dve', MSET(F_all[:, :, :], 0.0), [], [FR])

        LM = 128
        sq = sb([128, 8, LM], BF16); mixoN = sb([128, 1024]); tmp = sb([128, 8, LM], at=st['last']); rstd = sb([128, LM]); hc = sb([128, 8, LM], BF16)
        qk4 = sb([128, 4, LM]); sq4 = sb([128, 4, LM], BF16); ytmp = sb([128, 512]); ytmp_off = st['last']; rstd4 = sb([128, 4, LM], at=st['last'])
        qT = sb([128, 2, 2, LM], BF16); kN = sb([128, 256]); vN = sb([128, 260])
        t4 = sb([128, 4]); lf = sb([128, 4]); nb_all = sb([128, NCK, 4])
        Pt = [sb([128, 4, LM], BF16) for _ in range(2)]
        p0_ = (st['p'] + 63) // 64 * 64
        cin = [sb([128, 6, 3 + LM]) for _ in range(2)]
        R2 = (p0_, st['p'] - p0_)
        acc = sb([128, 6, LM]); off_acc = st['last']; xbcT = sb([128, 6, LM]); tmpc = xbcT; bcT_b = sb([128, 2, LM], BF16); Cblk_f = sb([128, 2, LM]); Cblk_b = sb([128, 2, LM], BF16)
        zs = sb([128, 512]); dtgv = sb([128, 264]); zg = sb([128, 256]); glr = sb([128, LM]); gqk = sb([128, 2, LM])
        dt = sb([128, 8]); da = sb([128, 8]); cs = sb([128, 8]); negcs = sb([128, 8]); wdec = sb([128, 8])
        expcs = sb([128, 8]); edec = sb([128, 8]); t8 = sb([128, 8])
        daU = sb([128, 8, LM]); off_daU = st['last']; Em = daU; MT = sb([128, 8, LM], BF16)
        xN = sb([128, 512]); BN = sb([128, 128]); xdt = sb([128, 8, 64], BF16); Bw = sb([128, 4, 2, 64], BF16); xdt2 = sb([128, 4, 2, 64], BF16)
        yy = sb([128, 512]); off_yy = st['last']; ss2 = sb([128, 2]); hst_t = sb([128, 4, 64])
        lg = sb([128, LM]); cl = sb([128, LM]); eq = sb([128, LM]); ek = sb([128, LM]); dl = sb([128, LM])
        qt = sb([128, LM], BF16); kt = sb([128, LM], BF16); khT = sb([128, LM]); qblk = sb([128, 4, LM], BF16)
        attm = sb([128, 4, LM], BF16); vg = sb([128, 256], BF16); khN = sb([128, 128], BF16)
        tmpg = sb([128, 4, 64]); red = sb([128, 64]); elast = sb([128, 1]); ss4 = sb([128, 4]); og = kN
        mixoT = sq; rec4 = sb([128, 4]); sso = sb([128, 4, 128], at=ytmp_off)
        T = {}
        T['mixoN'] = tmpR
        T['mixoT'] = sqR

        def tr(name):
            if name not in T:
                T[name] = R_()
            return T[name]

        T['sso'] = tr('ytmp'); T['rstd4'] = tr('ytmp'); T['og'] = tr('kN')
        xcR = R_()

        def sample_chunk():
            L = 128
            col0 = TP
            Us_, BLK, NEGMS, M01S = C['Us'], C['blk'], C['negms'], C['Us']
            v0 = lambda b, n: PS[b][:, 0:n * L].rearrange("p (j l) -> p j l", l=L)
            def region(off, size):
                reg = {'p': off, 'end': off + size}
                def alloc(shape, dt=F32):
                    nbytes = int(np.prod(shape[1:])) * (2 if dt == BF16 else 4)
                    o = (reg['p'] + 63) // 64 * 64
                    if o + nbytes <= reg['end']:
                        reg['p'] = o + nbytes
                        return sb(shape, dt, at=o)
                    return sb(shape, dt)
                return alloc
            a1 = region(*R1); a2 = region(*R2)
            cin_s = a1([128, 6, 32, 7])
            hN = sb([128, 768], at=off_daU); convN = hN; T['hN'] = tr('daU'); T['convN'] = tr('daU')
            edec_all = a2([128, 16, 8]); elast_all = a2([128, 32])
            kTs = a2([128, 2, L], BF16); Vs = a2([128, 4, 80], BF16); Fs = a2([128, 4])
            h0N = [a1([128, 8, 64])]; hTb = [a1([128, 4, 64])]
            ysum = sb([128, 512], at=off_acc); T['ysum'] = tr('acc')
            osum = kN; T['osum'] = tr('kN')
            tmpy = ytmp; T['tmpy'] = tr('ytmp')
            Sg0 = [a1([128, 64]) for _ in range(2)]; Sg0b = [a1([128, 64], BF16) for _ in range(2)]
            Bwb = a1([128, 4, 2, 64], BF16); khNb = a1([128, 128], BF16)
            newT = a1([128, 4, 64]); sso2 = sb([128, 4, 128], at=off_yy); T['sso2'] = tr('yy'); og2 = a1([128, 256])
            S.op('dve', MSET(cin_s[:, :, :, :], 0.0), [], [tr('cin_s')])
            S.op('dve', MSET(Vs[:, :, 64:65], 1.0), [], [tr('Vs')])
            S.dma('sp', DMA(hN[:NS * 3, :], sconv_d[l]), [], [tr('hN')])
            S.op('pe', TRS([(PS[0][:, j * 64:j * 64 + NS * 3], hN[:NS * 3, j * 128:(j + 1) * 128], ident[:NS * 3, :NS * 3])
                            for j in range(6)]), [tr('hN'), constR], [PR[0]])
            S.op('dve', CP(cin_s[:, :, 0:NS, 0:3],
                           PS[0][:, 0:384].rearrange("p (j x) -> p j x", x=64)[:, :, 0:NS * 3].rearrange("p j (b r) -> p j b r", r=3)),
                 [PR[0]], [tr('cin_s')])
            norm_T(xT[:, :, col0:col0 + NS4], NS4, l * 3 + 1, hc[:, :, :NS4], sq, tmp, rstd, 7, [xcR], [tr('hc')])
            S.op('dve', MSET(hc[:, :, NS4:L], 0.0), [], [tr('hc')])
            def tproj(bank, idx, c0, M=128):
                S.op('pe', MMX([(PS[bank][:M, idx * L:(idx + 1) * L], Wmi[:, k, c0:c0 + M], hc[:, k, :L], k == 0, k == 7)
                                for k in range(8)]), [wRm, tr('hc')], [PR[bank]])
            def nproj(bank, o0, c0, n):
                S.op('pe', MMX([(PS[bank][:L, o0:o0 + n], hc[:, k, :L], Wmi[:, k, c0:c0 + n], k == 0, k == 7)
                                for k in range(8)]), [wRm, tr('hc')], [PR[bank]])
            for i, c0 in enumerate((O_FQ, O_FQ + 128, O_FK, O_FK + 128)):
                tproj(0, i, c0)
            for i in range(4):
                tproj(1, i, O_XBC + i * 128)
            for i, c0 in enumerate((O_XBC + 512, O_XBC + 640, O_GQ, O_GK)):
                tproj(2, i, c0)
            nproj(3, 0, O_FV, 260)
            nproj(4, 0, O_SZ, 512)
            nproj(5, 0, O_DT, 8)
            nproj(5, 8, O_GV, 256)
            S.op('pe', MMX([(PS[5][:16, 320:320 + L], Wmi[:, k, O_LR:O_LR + 16], hc[:, k, :L], k == 0, k == 7)
                            for k in range(8)]), [wRm, tr('hc')], [PR[5]])
            nproj(6, 0, O_GG, 256)
            S.op('dve', CP(qk4[:, :, :L], v0(0, 4)), [PR[0]], [tr('qk4')])
            S.op('act', ACTF(sq4[:, :, :L], qk4[:, :, :L], AF.Square), [tr('qk4')], [tr('sq4')])
            S.op('act', CP(cin_s[:, 0:4, :, 3:7], v0(1, 4).rearrange("p j (b t) -> p j b t", t=4)), [PR[1]], [tr('cin_s')])
            S.op('dve', CP(cin_s[:, 4:6, :, 3:7], v0(2, 2).rearrange("p j (b t) -> p j b t", t=4)), [PR[2]], [tr('cin_s')])
            S.op('dve', CP(gqk[:, :, :L], PS[2][:, 2 * L:4 * L].rearrange("p (j l) -> p j l", l=L)), [PR[2]], [tr('gqk')])
            S.op('act', CP(vN[:L, :], PS[3][:L, 0:260]), [PR[3]], [tr('vN')])
            S.op('act', ACTF(zs[:L, :], PS[4][:L, :], AF.Silu), [PR[4]], [tr('zs')])
            S.op('dve', CP(dtgv[:L, :], PS[5][:L, 0:264]), [PR[5]], [tr('dtgv')])
            S.op('dve', CP(glr[:16, :L], PS[5][:16, 320:320 + L]), [PR[5]], [tr('glr')])
            S.op('act', ACTF(zg[:L, :], PS[6][:L, 0:256], AF.Silu), [PR[6]], [tr('zg')])
            S.op('pe', MMx(PS[7][:, 0:4 * L], bd64_b[:, :], sq4[:, :, :L]), [tr('sq4'), constR], [PR[7]])
            S.op('act', ACTF(rstd4[:, :, :L], v0(7, 4), AF.Sqrt, bias=epsc, scale=1.0 / 64), [PR[7], constR], [tr('rstd4')])
            S.op('dve', RECIP(rstd4[:, :, :L], rstd4[:, :, :L]), [tr('rstd4')], [tr('rstd4')])
            S.op('dve', TT(qk4[:, :, :L], qk4[:, :, :L], rstd4[:, :, :L], ALU.mult), [tr('rstd4'), tr('qk4')], [tr('qk4')])
            S.op('dve', TT(qk4[:, :, :L], qk4[:, :, :L], gcol4[:, :].unsqueeze(2).to_broadcast([128, 4, L]), ALU.mult),
                 [prm, tr('qk4')], [tr('qk4')])
            for e_ in range(2):
                S.op('dve', TS(qT[:, :, e_, :L], qk4[:, 0:2, :L], C['bd64'][:, 64 * e_:64 * e_ + 1], ALU.mult),
                     [tr('qk4'), constR], [tr('qT')])
            S.op('dve', CP(kTs[:, :, :L], qk4[:, 2:4, :L]), [tr('qk4')], [tr('kTs')])
            S.op('pe', TRS([(PS[0][:L, j * 128:(j + 1) * 128], qk4[:, 2 + j, :L], ident) for j in range(2)]),
                 [tr('qk4'), constR], [PR[0]])
            S.op('act', CP(kN[:L, :], PS[0][:L, 0:256]), [PR[0]], [tr('kN')])
            S.dma('sp', DMA(ks_d[l], kN[:NS4, :]), [tr('kN')], [])
            S.dma('sp', DMA(vs_d[l], vN[:NS4, 0:256]), [tr('vN')], [])
            S.op('dve', CP(Vs[:L, :, 0:64], vN[:L, 0:256].rearrange("p (h d) -> p h d", d=64)), [tr('vN')], [tr('Vs')])
            S.op('dve', TT(t4[:L, :], vN[:L, 256:260], fb_bc[:L, :], ALU.add), [tr('vN'), prm], [tr('t4')])
            S.op('act', ACTF(t4[:L, :], t4[:L, :], AF.Exp, scale=-1.0), [tr('t4')], [tr('t4')])
            S.op('act', ACTF(t4[:L, :], t4[:L, :], AF.Ln, bias=onec[:L, :]), [tr('t4'), constR], [tr('t4')])
            S.op('dve', TS(lf[:L, :], t4[:L, :], -1.0, ALU.mult), [tr('t4')], [tr('lf')])
            S.dma('sp', DMA(lfs_d[l], lf[:NS4, :]), [tr('lf')], [])
            S.op('pe', MMx(PS[3][:L, 0:4], Us_[:L, :L], lf[:L, :]), [tr('lf'), constR], [PR[3]])
            S.op('dve', TS(Fs[:L, :], PS[3][:L, 0:4], -1.0, ALU.mult), [PR[3]], [tr('Fs')])
            S.op('pe', MMX([(PS[0][:L, 2 * hp * L:(2 * hp + 2) * L], kTs[:, hp, :L], qT[:, hp, :, :L], True, True)
                            for hp in range(2)]), [tr('kTs'), tr('qT')], [PR[0]])
            for h in range(4):
                S.op('act', ACTF(Pt[0][:L, h, :L], PS[0][:L, h * L:(h + 1) * L], AF.Exp, bias=Fs[:L, h:h + 1], scale=0.125),
                     [PR[0], tr('Fs')], [tr('P0')])
            S.op('dve', TT(Pt[0][:L, :, :L], Pt[0][:L, :, :L], M01S[:L, :L].unsqueeze(1).to_broadcast([L, 4, L]), ALU.mult),
                 [tr('P0'), constR], [tr('P0')])
            S.op('pe', MMX([(PS[2][:64, h * 80:h * 80 + 65], Pt[0][:L, h, 0:64], Vs[:L, h, 0:65], h == 0, (h == 3 and not GATHER))
                            for h in range(4)]), [tr('P0'), tr('Vs')], [PR[2]])
            if GATHER:
                NPGS = NS * NPG
                lfA = [a2([128, NPG, 4])]; rs = a2([128, NPG, 4]); biasT = a2([128, NPG, 4]); scb = a2([128, 16])
                Kp = [a1([128, 256]) for _ in range(2)]; Vp = [a1([128, 256]) for _ in range(2)]
                Vbp = [a1([128, 4, 80], BF16) for _ in range(2)]; kTt = [a1([128, 2, 128], BF16) for _ in range(2)]
                Pz = [a2([128, 4, 64], BF16) for _ in range(2)]; qb = a2([128, 2, 16, 8], BF16)
                ptB = a2([128, NPGS], I32); idxF = a2([128, NPGS]); idxI = ptB; raddF = sb([128, 1])
                lfA.append(idxF[:, 0:NPG * 4].rearrange("p (g h) -> p g h", h=4))
                ck2 = ck_d.rearrange("l r c -> (l r) c"); cv2 = cv_d.rearrange("l r c -> (l r) c"); clf2 = clf_d.rearrange("l r c -> (l r) c")
                for hp in range(2):
                    S.op('dve', CP(qb[:, hp, 0:NS, :].rearrange("p b (e q) -> p b e q", q=4),
                                   qT[:, hp, :, 0:NS4].rearrange("p e (b q) -> p b e q", q=4)), [tr('qT')], [tr('qb')])
                for kb in range(2):
                    S.op('dve', MSET(Pz[kb][:, :, :], 0.0), [], [tr('Pz%d' % kb)])
                    S.op('dve', MSET(Vbp[kb][:, :, 64:65], 1.0), [], [tr('Vbp%d' % kb)])
                S.dma('sp', DMA(ptB[:, :], pt_d.rearrange("b g -> (b g)").partition_broadcast(128)), [], [tr('idx')])
                S.dma('sp', DMA(raddF[:, :], radd_d), [], [tr('idx')])
                S.op('dve', CP(idxF[:, :], ptB[:, :]), [tr('idx')], [tr('idx')])
                S.op('dve', TS(idxF[:, :], idxF[:, :], 128.0, ALU.mult, float(l * NPOOL * 128), ALU.add), [tr('idx')], [tr('idx')])
                S.op('dve', TT(idxF[:, :], idxF[:, :], raddF[:, 0:1].to_broadcast([128, NPGS]), ALU.add), [tr('idx')], [tr('idx')])
                S.op('dve', CP(idxI[:, :], idxF[:, :]), [tr('idx')], [tr('idx')])
                def lf_gather(b_):
                    for pg_ in range(NPG):
                        S.dma('pool', IDMA(lfA[b_ % 2][:, pg_, :], clf2, idxI[:, b_ * NPG + pg_:b_ * NPG + pg_ + 1]),
                              [tr('idx')], [tr('lf_all%d' % (b_ % 2))])
                lf_gather(0)
                for b in range(NS):
                    if b + 1 < NS:
                        lf_gather(b + 1)
                    lf_all = lfA[b % 2]; lfR = tr('lf_all%d' % (b % 2))
                    S.op('dve', MSET(rs[:, NPG - 1, :], 0.0), [], [tr('rs')])
                    for pg in range(NPG - 2, -1, -1):
                        S.op('dve', TT(rs[:, pg, :], rs[:, pg + 1, :], lf_all[:, pg + 1, :], ALU.add), [lfR], [tr('rs')])
                    S.op('pe', MMX([(PS[3][:, 0:NPG * 4], C['Mgt'], lf_all[:, :, :].rearrange("p g h -> p (g h)"), True, False),
                                    (PS[3][:, 0:NPG * 4], C['ones'], rs[:, :, :].rearrange("p g h -> p (g h)"), False, True)]),
                         [lfR, tr('rs'), constR], [PR[3]])
                    S.op('dve', CP(biasT[:, :, :], PS[3][:, 0:NPG * 4].rearrange("p (g h) -> p g h", h=4)), [PR[3]], [tr('biasT')])
                    for pg in range(NPG):
                        kb = pg % 2
                        col = b * NPG + pg
                        S.dma('pool', IDMA(Kp[kb][:, :], ck2, idxI[:, col:col + 1]), [tr('idx')], [tr('Kp%d' % kb)])
                        S.dma('pool', IDMA(Vp[kb][:, :], cv2, idxI[:, col:col + 1]), [tr('idx')], [tr('Vp%d' % kb)])
                        S.op('act', CP(Vbp[kb][:, :, 0:64], Vp[kb][:, :].rearrange("p (h d) -> p h d", d=64)), [tr('Vp%d' % kb)], [tr('Vbp%d' % kb)])
                        S.op('pe', TRS([(PS[kb][:, j * 128:(j + 1) * 128], Kp[kb][:, j * 128:(j + 1) * 128], ident) for j in range(2)]),
                             [tr('Kp%d' % kb), constR], [PR[kb]])
                        S.op('dve', CP(kTt[kb][:, :, :], PS[kb][:, 0:256].rearrange("p (j x) -> p j x", x=128)), [PR[kb]], [tr('kTt%d' % kb)])
                        S.op('pe', MMX([(PS[4][:, hp * 8:hp * 8 + 8], kTt[kb][:, hp, :], qb[:, hp, b, :], True, True) for hp in range(2)]),
                             [tr('kTt%d' % kb), tr('qb')], [PR[4]])
                        S.op('dve', STT(scb[:, 0:16].rearrange("p (h q) -> p h q", q=4), PS[4][:, 0:16].rearrange("p (h q) -> p h q", q=4), 0.125,
                                        biasT[:, pg, :].unsqueeze(2).to_broadcast([128, 4, 4]), ALU.mult, ALU.add),
                             [PR[4], tr('biasT')], [tr('scb')])
                        S.op('act', ACTF(Pz[kb][:, :, 4 * b:4 * b + 4], scb[:, 0:16].rearrange("p (h q) -> p h q", q=4), AF.Exp),
                             [tr('scb')], [tr('Pz%d' % kb)])
                        S.op('pe', MMX([(PS[2][:64, h * 80:h * 80 + 65], Pz[kb][:, h, :], Vbp[kb][:, h, 0:65], False,
                                         (b == NS - 1 and pg == NPG - 1 and h == 3)) for h in range(4)]),
                             [tr('Pz%d' % kb), tr('Vbp%d' % kb)], [PR[2]])
                    for kb in range(2):
                        S.op('dve', MSET(Pz[kb][:, :, 4 * b:4 * b + 4], 0.0), [tr('Pz%d' % kb)], [tr('Pz%d' % kb)])
            o4 = PS[2][:64, 0:320].rearrange("p (h d) -> p h d", d=80)
            S.op('dve', RECIP(rec4[:64, :].unsqueeze(2), o4[:, :, 64:65]), [PR[2]], [tr('rec4')])
            S.op('dve', TT(mixoN[:64, 0:256].rearrange("p (h d) -> p h d", d=64), o4[:, :, 0:64],
                           rec4[:64, :].unsqueeze(2).to_broadcast([64, 4, 64]), ALU.mult), [PR[2], tr('rec4')], [tr('mixoN')])
            S.op('dve', MSET(mixoN[64:128, 0:256], 0.0), [], [tr('mixoN')])
            if GATHER:
                S.barrier()
            S.op('dve', MSET(osum[:, :], 0.0), [], [tr('osum')])
            for j in range(6):
                for tp in range(4):
                    dst = acc if tp == 0 else tmpc
                    S.op('dve', TS(dst[:, j, :L].rearrange("p (b t) -> p b t", t=4), cin_s[:, j, :, tp:tp + 4], cw[:, tp, j:j + 1], ALU.mult),
                         [tr('cin_s'), prm], [tr('acc' if tp == 0 else 'xbcT')])
                    if tp > 0:
                        S.op('dve', TT(acc[:, j, :L], acc[:, j, :L], tmpc[:, j, :L], ALU.add), [tr('xbcT')], [tr('acc')])
            S.op('dve', TT(acc[:, :, :L], acc[:, :, :L], cbb[:, :].unsqueeze(2).to_broadcast([128, 6, L]), ALU.add), [prm], [tr('acc')])
            S.op('act', ACTF(xbcT[:, :, :L], acc[:, :, :L], AF.Silu), [tr('acc')], [tr('xbcT')])
            S.op('dve', CP(acc[:, :, 0:NS * 3].rearrange("p j (b r) -> p j b r", r=3), cin_s[:, :, 0:NS, 4:7]), [tr('cin_s'), tr('xbcT')], [tr('acc')])
            for a_, (j0, j1) in enumerate(((0, 4), (4, 6))):
                S.op('pe', TRS([(PS[a_][:NS * 3, (j - j0) * 128:(j - j0 + 1) * 128], acc[:, j, 0:NS * 3], ident) for j in range(j0, j1)]),
                     [tr('acc'), constR], [PR[a_]])
                S.op('act' if a_ else 'dve', CP(convN[:NS * 3, j0 * 128:j1 * 128], PS[a_][:NS * 3, 0:(j1 - j0) * 128]), [PR[a_]], [tr('convN')])
            S.dma('sp', DMA(convs_d[l].rearrange("b r c -> (b r) c"), convN[:NS * 3, :]), [tr('convN')], [])
            S.op('dve', TT(t8[:L, :], dtgv[:L, 0:8], dtb_bc[:L, :], ALU.add), [tr('dtgv'), prm], [tr('t8')])
            S.op('act', ACTF(t8[:L, :], t8[:L, :], AF.Exp), [tr('t8')], [tr('t8')])
            S.op('act', ACTF(dt[:L, :], t8[:L, :], AF.Ln, bias=onec[:L, :]), [tr('t8'), constR], [tr('dt')])
            S.op('dve', TT(da[:L, :], dt[:L, :], a_bc[:L, :], ALU.mult), [tr('dt'), prm], [tr('da')])
            S.op('pe', MMX([(PS[3][:L, 16:24], Us_[:L, :L], da[:L, :], True, True),
                            (PS[3][:L, 32:40], BLK[:L, :L], da[:L, :], True, True)]), [tr('da'), constR], [PR[3]])
            S.op('dve', CP(cs[:L, :], PS[3][:L, 16:24]), [PR[3]], [tr('cs')])
            S.op('dve', TS(negcs[:L, :], PS[3][:L, 16:24], -1.0, ALU.mult), [PR[3]], [tr('negcs')])
            S.op('dve', TT(wdec[:L, :], PS[3][:L, 32:40], cs[:L, :], ALU.subtract), [PR[3], tr('cs')], [tr('wdec')])
            S.op('act', ACTF(wdec[:L, :], wdec[:L, :], AF.Exp), [tr('wdec')], [tr('wdec')])
            S.op('act', ACTF(expcs[:L, :], cs[:L, :], AF.Exp), [tr('cs')], [tr('expcs')])
            S.op('pe', MMX([(PS[4][:, b * 8:b * 8 + 8], rowm32[:, b:b + 1].to_broadcast([128, 128]), da[:L, :], True, True)
                            for b in range(NS)]), [tr('da'), constR], [PR[4]])
            S.op('act', ACTF(edec_all[:, 0:NS, :], PS[4][:, 0:NS * 8].rearrange("p (b h) -> p b h", h=8), AF.Exp), [PR[4]], [tr('edec_all')])
            S.op('dve', TT(daU[:L, :, :L], Us_[:L, :L].unsqueeze(1).to_broadcast([L, 8, L]),
                           da[:L, :].unsqueeze(2).to_broadcast([L, 8, L]), ALU.mult), [tr('da'), constR], [tr('daU')])
            for g in range(2):
                S.op('pe', MMx(PS[g][:L, 0:4 * L], BLK[:L, :L], daU[:L, 4 * g:4 * g + 4, :L]), [tr('daU'), constR], [PR[g]])
                S.op('dve', TT(Em[:L, 4 * g:4 * g + 4, :L], PS[g][:L, 0:4 * L].rearrange("p (h l) -> p h l", l=L),
                               NEGMS[:L, :L].unsqueeze(1).to_broadcast([L, 4, L]), ALU.add), [PR[g], constR], [tr('daU')])
            for h in range(8):
                S.op('act', ACTF(Em[:L, h, :L], Em[:L, h, :L], AF.Exp, bias=negcs[:L, h:h + 1]), [tr('daU'), tr('negcs')], [tr('daU')])
            S.op('dve', CP(bcT_b[:, 0, :L], xbcT[:, 4, :L]), [tr('xbcT')], [tr('bcT')])
            for g in range(2):
                S.op('dve', TS(Cblk_f[:, g, :L], xbcT[:, 5, :L], C['bd64'][:, 64 * g:64 * g + 1], ALU.mult),
                     [tr('xbcT'), constR], [tr('Cblk_f')])
            S.op('dve', CP(Cblk_b[:, :, :L], Cblk_f[:, :, :L]), [tr('Cblk_f')], [tr('Cblk_b')])
            S.op('pe', MMx(PS[2][:L, 0:2 * L], bcT_b[:, 0, :L], Cblk_b[:, :, :L]), [tr('bcT'), tr('Cblk_b')], [PR[2]])
            S.op('dve', TT(MT[:L, :, :L].rearrange("p (g h) l -> p g h l", g=2),
                           Em[:L, :, :L].rearrange("p (g h) l -> p g h l", g=2),
                           PS[2][:L, 0:2 * L].rearrange("p (g l) -> p g l", l=L).unsqueeze(2).to_broadcast([L, 2, 4, L]),
                           ALU.mult), [tr('daU'), PR[2]], [tr('MT')])
            S.op('pe', TRS([(PS[3][:L, j * 128:(j + 1) * 128], xbcT[:, j, :L], ident) for j in range(4)]),
                 [tr('xbcT'), constR], [PR[3]])
            S.op('act', CP(xN[:L, :], PS[3][:L, :]), [PR[3]], [tr('xN')])
            S.op('pe', TRS([(PS[4][:L, 0:128], xbcT[:, 4, :L], ident)]), [tr('xbcT'), constR], [PR[4]])
            S.op('act', CP(BN[:L, :], PS[4][:L, 0:128]), [PR[4]], [tr('BN')])
            S.op('dve', TT(xdt[:L, :, :], xN[:L, :].rearrange("p (h d) -> p h d", d=64),
                           dt[:L, :].unsqueeze(2).to_broadcast([L, 8, 64]), ALU.mult), [tr('xN'), tr('dt')], [tr('xdt')])
            S.op('dve', CP(xdt2[:L, :, :, :].rearrange("p h g n -> p g h n"),
                           xdt[:L, :, :].rearrange("p (g h) n -> p g h n", g=2)), [tr('xdt')], [tr('xdt2')])
            S.op('dve', TT(Bw[:L, :, :, :].rearrange("p h g n -> p g h n"),
                           BN[:L, :].rearrange("p (g n) -> p g n", g=2).unsqueeze(2).to_broadcast([L, 2, 4, 64]),
                           wdec[:L, :].rearrange("p (g h) -> p g h", g=2).unsqueeze(3).to_broadcast([L, 2, 4, 64]),
                           ALU.mult), [tr('BN'), tr('wdec')], [tr('Bw')])
            S.op('pe', MMX([(PS[5][:L, h * 64:(h + 1) * 64], MT[:L, h, :L], xdt[:L, h, :], True, True) for h in range(8)]),
                 [tr('MT'), tr('xdt')], [PR[5]])
            S.op('dve', MSET(ysum[:, :], 0.0), [], [tr('ysum')])
            for b in range(NS):
                bb = 0
                for g in range(2):
                    S.dma('sp', DMA(h0N[bb][:64, :, :].rearrange("p (hh g) n -> p hh g n", g=2)[:, :, g, :],
                                    sssm_d[l, b].rearrange("(g hh) p n -> g p hh n", g=2)[g]), [], [tr('h0N%d' % bb)])
                S.op('pe', TRS([(PS[6][:, hh * 64:(hh + 1) * 64], h0N[bb][:64, 2 * hh:2 * hh + 2, :].rearrange("p g n -> p (g n)"),
                                 ident[:64, :64]) for hh in range(4)]),
                     [tr('h0N%d' % bb), constR], [PR[6]])
                S.op('act', CP(hTb[bb][:, :, :], PS[6][:, 0:256].rearrange("p (h c) -> p h c", c=64)), [PR[6]], [tr('hTb%d' % bb)])
                S.op('pe', MMX([(PS[6][:L, h * 64:(h + 1) * 64], Cblk_f[:, h // 4, :L], hTb[bb][:, h % 4, :], True, True)
                                for h in range(8)]), [tr('Cblk_f'), tr('hTb%d' % bb)], [PR[6]])
                S.op('dve', TS(tmpy[:L, :], PS[6][:L, :], rowm32[:L, b:b + 1], ALU.mult), [PR[6], constR], [tr('tmpy')])
                S.op('dve', TT(ysum[:L, :], ysum[:L, :], tmpy[:L, :], ALU.add), [tr('tmpy')], [tr('ysum')])
                S.op('dve', TS(Bwb[:L, :, :, :], Bw[:L, :, :, :], rowm32[:L, b:b + 1], ALU.mult), [tr('Bw'), constR], [tr('Bwb')])
                S.op('pe', MMX([(PS[7][:, hh * 128:(hh + 1) * 128], Bwb[:L, hh, :, :].rearrange("p g n -> p (g n)"),
                                 xdt2[:L, hh, :, :].rearrange("p g n -> p (g n)"), True, True) for hh in range(4)]),
                     [tr('Bwb'), tr('xdt2')], [PR[7]])
                for g in range(2):
                    gs = slice(g * 64, g * 64 + 64)
                    S.op('dve', TT(hst_t[gs, :, :], hTb[bb][gs, :, :],
                                   edec_all[gs, b, 4 * g:4 * g + 4].unsqueeze(2).to_broadcast([64, 4, 64]), ALU.mult),
                         [tr('hTb%d' % bb), tr('edec_all')], [tr('hst_t')])
                    S.op('dve', TT(newT[gs, :, :], hst_t[gs, :, :],
                                   PS[7][gs, :].rearrange("p (h c) -> p h c", c=128)[:, :, g * 64:g * 64 + 64], ALU.add),
                         [tr('hst_t'), PR[7]], [tr('newT')])
                S.op('pe', TRS([(PS[7][:64, hh * 128:(hh + 1) * 128], newT[:, hh, :], ident) for hh in range(4)]),
                     [tr('newT'), constR], [PR[7]])
                S.op('act', CP(sso2[:64, :, :], PS[7][:64, :].rearrange("p (h n) -> p h n", n=128)), [PR[7]], [tr('sso2')])
                for g in range(2):
                    S.dma('pool', DMA(ssms_d[l, b].rearrange("(g hh) p n -> g p hh n", g=2)[g], sso2[:64, :, g * 64:(g + 1) * 64]),
                          [tr('sso2')], [])
            S.op('dve', TT(ytmp[:L, :].rearrange("p (h d) -> p h d", d=64), ysum[:L, :].rearrange("p (h d) -> p h d", d=64),
                           expcs[:L, :].unsqueeze(2).to_broadcast([L, 8, 64]), ALU.mult), [tr('ysum'), tr('expcs')], [tr('ytmp')])
            S.op('dve', TT(yy[:L, :], ytmp[:L, :], PS[5][:L, :], ALU.add), [PR[5], tr('ytmp')], [tr('yy')])
            S.op('dve', TT(ytmp[:L, :].rearrange("p (h d) -> p h d", d=64), xN[:L, :].rearrange("p (h d) -> p h d", d=64),
                           sd_bc[:L, :].unsqueeze(2).to_broadcast([L, 8, 64]), ALU.mult), [tr('xN'), prm], [tr('ytmp')])
            S.op('dve', TT(yy[:L, :], yy[:L, :], ytmp[:L, :], ALU.add), [tr('ytmp')], [tr('yy')])
            S.op('dve', TT(yy[:L, :], yy[:L, :], zs[:L, :], ALU.mult), [tr('zs')], [tr('yy')])
            S.op('act', ACTF(ytmp[:L, :], yy[:L, :], AF.Square), [tr('yy')], [tr('ytmp')])
            S.op('dve', RED(ss2[:L, :], ytmp[:L, :].rearrange("p (g d) -> p g d", g=2)), [tr('ytmp')], [tr('ss2')])
            S.op('act', ACTF(ss2[:L, :], ss2[:L, :], AF.Sqrt, bias=epsc[:L, :], scale=1.0 / 256), [tr('ss2'), constR], [tr('ss2')])
            S.op('dve', RECIP(ss2[:L, :], ss2[:L, :]), [tr('ss2')], [tr('ss2')])
            S.op('dve', TT(yy[:L, :].rearrange("p (g d) -> p g d", g=2), yy[:L, :].rearrange("p (g d) -> p g d", g=2),
                           ss2[:L, :].unsqueeze(2).to_broadcast([L, 2, 256]), ALU.mult), [tr('ss2')], [tr('yy')])
            S.op('dve', TT(mixoN[:L, 256:768], yy[:L, :], sn_bc[:L, :], ALU.mult), [tr('yy'), prm], [tr('mixoN')])
            S.op('pe', MMx(PS[0][:, 0:L], Wg[:16, :], glr[:16, :L]), [tr('glr'), prm], [PR[0]])
            S.op('act', ACTF(lg[:, :L], PS[0][:, 0:L], AF.Exp, bias=ngb[:, 0:1], scale=-1.0), [PR[0], prm], [tr('lg')])
            S.op('act', ACTF(lg[:, :L], lg[:, :L], AF.Ln, bias=onec), [tr('lg'), constR], [tr('lg')])
            S.op('dve', SCAN(cl[:, :L], rst128[:, :L], lg[:, :L]), [tr('lg'), constR], [tr('cl')])
            S.op('act', ACTF(eq[:, :L], cl[:, :L], AF.Exp, scale=-1.0 / 16), [tr('cl')], [tr('eq')])
            S.op('act', ACTF(ek[:, :L], cl[:, :L], AF.Exp, scale=1.0 / 16), [tr('cl')], [tr('ek')])
            S.op('dve', STT(qt[:, :L], gqk[:, 0, :L], float(32 ** -0.5), eq[:, :L], ALU.mult, ALU.mult),
                 [tr('gqk'), tr('eq')], [tr('qt')])
            S.op('dve', TT(kt[:, :L], gqk[:, 1, :L], ek[:, :L], ALU.mult), [tr('gqk'), tr('ek')], [tr('kt')])
            cl3 = cl[:, :L].rearrange("p (b t) -> p b t", t=4)
            S.op('dve', TT(dl[:, :L].rearrange("p (b t) -> p b t", t=4), cl3[:, :, 3:4].to_broadcast([128, 32, 4]), cl3, ALU.subtract),
                 [tr('cl')], [tr('dl')])
            S.op('act', ACTF(dl[:, :L], dl[:, :L], AF.Exp, scale=-1.0 / 16), [tr('dl')], [tr('dl')])
            S.op('dve', TT(khT[:, :L], gqk[:, 1, :L], dl[:, :L], ALU.mult), [tr('gqk'), tr('dl')], [tr('khT')])
            S.op('act', ACTF(elast_all[:, :].unsqueeze(2), cl3[:, :, 3:4], AF.Exp, scale=-1.0 / 16), [tr('cl')], [tr('elast_all')])
            S.op('dve', TT(qblk[:, :, :L], qt[:, :L].unsqueeze(1).to_broadcast([128, 4, L]),
                           hm[:, :].unsqueeze(2).to_broadcast([128, 4, L]), ALU.mult), [tr('qt'), constR], [tr('qblk')])
            S.op('pe', MMx(PS[1][:L, 0:4 * L], kt[:, :L], qblk[:, :, :L]), [tr('kt'), tr('qblk')], [PR[1]])
            S.op('dve', TT(attm[:L, :, :L], PS[1][:L, 0:4 * L].rearrange("p (h l) -> p h l", l=L),
                           M01S[:L, :L].unsqueeze(1).to_broadcast([L, 4, L]), ALU.mult), [PR[1], constR], [tr('attm')])
            S.op('act', CP(vg[:L, :], dtgv[:L, 8:264]), [tr('dtgv')], [tr('vg')])
            S.op('pe', MMX([(PS[0][:L, 256 + h * 64:256 + (h + 1) * 64], attm[:L, h, :L], vg[:L, h * 64:(h + 1) * 64], True, True)
                            for h in range(4)]), [tr('attm'), tr('vg')], [PR[0]])
            S.op('pe', TRS([(PS[3][:L, 0:128], khT[:, :L], ident)]), [tr('khT'), constR], [PR[3]])
            S.op('act', CP(khN[:L, :], PS[3][:L, 0:128]), [PR[3]], [tr('khN')])
            for b in range(NS):
                bb = b % 2
                S.dma('sp', DMA(Sg0[bb][:, :], sgla_d[l, b]), [], [tr('Sg0%d' % bb)])
                S.op('act', CP(Sg0b[bb][:, :], Sg0[bb][:, :]), [tr('Sg0%d' % bb)], [tr('Sg0b%d' % bb)])
                S.op('pe', MMX([(PS[3][:L, h * 64:(h + 1) * 64], qblk[:, h, :L], Sg0b[bb][:, :], True, True) for h in range(4)]),
                     [tr('qblk'), tr('Sg0b%d' % bb)], [PR[3]])
                S.op('dve', TS(tmpy[:L, 0:256], PS[3][:L, 0:256], rowm32[:L, b:b + 1], ALU.mult), [PR[3], constR], [tr('tmpy')])
                S.op('dve', TT(osum[:L, :], osum[:L, :], tmpy[:L, 0:256], ALU.add), [tr('tmpy')], [tr('osum')])
                S.op('dve', TS(khNb[:L, :], khN[:L, :], rowm32[:L, b:b + 1], ALU.mult), [tr('khN'), constR], [tr('khNb')])
                S.op('pe', MMx(PS[7][:, 0:256], khNb[:L, :], vg[:L, :]), [tr('khNb'), tr('vg')], [PR[7]])
                S.op('dve', TT(tmpg[:, :, :], PS[7][:, 0:256].rearrange("p (h v) -> p h v", v=64),
                               hm[:, :].unsqueeze(2).to_broadcast([128, 4, 64]), ALU.mult), [PR[7], constR], [tr('tmpg')])
                S.op('dve', RED(red[:, :], tmpg[:, :, :].rearrange("p h v -> p v h")), [tr('tmpg')], [tr('red')])
                S.op('dve', STT(Sg0[bb][:, :], Sg0[bb][:, :], elast_all[:, b:b + 1], red[:, :], ALU.mult, ALU.add),
                     [tr('red'), tr('elast_all'), tr('Sg0b%d' % bb)], [tr('Sg0%d' % bb)])
                S.dma('pool', DMA(glas_d[l, b], Sg0[bb][:, :]), [tr('Sg0%d' % bb)], [])
            S.op('dve', TT(og2[:L, :], osum[:L, :], PS[0][:L, 256:512], ALU.add), [PR[0], tr('osum')], [tr('og2')])
            S.op('act', ACTF(og[:L, :], og2[:L, :], AF.Square), [tr('og2')], [tr('og')])
            S.op('dve', RED(ss4[:L, :], og[:L, :].rearrange("p (h v) -> p h v", v=64)), [tr('og')], [tr('ss4')])
            S.op('act', ACTF(ss4[:L, :], ss4[:L, :], AF.Sqrt, bias=epsc[:L, :], scale=1.0 / 64), [tr('ss4'), constR], [tr('ss4')])
            S.op('dve', RECIP(ss4[:L, :], ss4[:L, :]), [tr('ss4')], [tr('ss4')])
            S.op('dve', TT(og[:L, :].rearrange("p (h v) -> p h v", v=64), og2[:L, :].rearrange("p (h v) -> p h v", v=64),
                           ss4[:L, :].unsqueeze(2).to_broadcast([L, 4, 64]), ALU.mult), [tr('og2'), tr('ss4')], [tr('og')])
            S.op('dve', TT(og[:L, :].rearrange("p (h v) -> p h v", v=64), og[:L, :].rearrange("p (h v) -> p h v", v=64),
                           gn_bc[:L, :].unsqueeze(1).to_broadcast([L, 4, 64]), ALU.mult), [prm], [tr('og')])
            S.op('dve', TT(mixoN[:L, 768:1024], og[:L, :], zg[:L, :], ALU.mult), [tr('og'), tr('zg')], [tr('mixoN')])
            for half in range(2):
                S.op('pe', TRS([(PS[4 + half][:, j * L:(j + 1) * L], mixoN[:L, (half * 4 + j) * 128:(half * 4 + j + 1) * 128],
                                 ident[:L, :L]) for j in range(4)]), [tr('mixoN'), constR], [PR[4 + half]])
                S.op('act' if half else 'dve', CP(mixoT[:, half * 4:half * 4 + 4, :L],
                                                  PS[4 + half][:, 0:4 * L].rearrange("p (j l) -> p j l", l=L)),
                     [PR[4 + half]], [tr('mixoT')])
            for oc in range(8):
                bo = 6 + oc % 2
                wb_ = oc % 2
                S.dma('pool', DMA(wo_t[wb_][:, :, :], wmo_v[:, :, oc * 128:(oc + 1) * 128]), [], [woR[wb_]])
                S.op('pe', MMX([(PS[bo][:, 0:L], wo_t[wb_][:, k, :], mixoT[:, k, :L], k == 0, k == 7)
                                for k in range(8)]), [woR[wb_], tr('mixoT')], [PR[bo]])
                S.op('dve', STT(xT[:, oc, col0:col0 + NS4], PS[bo][:, 0:NS4], 1.0, xT[:, oc, col0:col0 + NS4], ALU.mult, ALU.add), [PR[bo]], [xcR])
        chunks = [(0, 16, 0)] + [(c + 1, 128, 16 + c * 128) for c in range(NCH)]
        Ucst, ONE, NEGM, M01 = C['U'], C['ones'], C['negm'], C['m01']
        prev_cin = None
        for (ci, L, col0) in chunks:
            cb_i = ci % 2
            if MSTEP < 1:
                continue
            norm_T(xT[:, :, col0:col0 + L], L, l * 3 + 1, hc[:, :, :L], sq, tmp, rstd, 7, [xcR], [tr('hc')])
            if MSTEP < 2:
                continue
            def tproj(bank, idx, c0, M=128):
                S.op('pe', MMX([(PS[bank][:M, idx * L:(idx + 1) * L], Wmi[:, k, c0:c0 + M], hc[:, k, :L], k == 0, k == 7)
                                for k in range(8)]), [wRm, tr('hc')], [PR[bank]])
            def nproj(bank, o0, c0, n):
                S.op('pe', MMX([(PS[bank][:L, o0:o0 + n], hc[:, k, :L], Wmi[:, k, c0:c0 + n], k == 0, k == 7)
                                for k in range(8)]), [wRm, tr('hc')], [PR[bank]])
            for i, c0 in enumerate((O_FQ, O_FQ + 128, O_FK, O_FK + 128)):
                tproj(0, i, c0)
            if MSTEP < 2.1:
                continue
            for i in range(4):
                tproj(1, i, O_XBC + i * 128)
            for i, c0 in enumerate((O_XBC + 512, O_XBC + 640, O_GQ, O_GK)):
                tproj(2, i, c0)
            if MSTEP < 2.2:
                continue
            nproj(3, 0, O_FV, 260)
            nproj(4, 0, O_SZ, 512)
            if MSTEP < 2.3:
                continue
            nproj(5, 0, O_DT, 8)
            nproj(5, 8, O_GV, 256)
            if MSTEP < 2.4:
                continue
            S.op('pe', MMX([(PS[5][:16, 320:320 + L], Wmi[:, k, O_LR:O_LR + 16], hc[:, k, :L], k == 0, k == 7)
                            for k in range(8)]), [wRm, tr('hc')], [PR[5]])
            if MSTEP < 2.45:
                continue
            nproj(6, 0, O_GG, 256)
            if MSTEP < 2.5:
                continue
            v0 = lambda b, n: PS[b][:, 0:n * L].rearrange("p (j l) -> p j l", l=L)
            S.op('dve', CP(qk4[:, :, :L], v0(0, 4)), [PR[0]], [tr('qk4')])
            S.op('act', ACTF(sq4[:, :, :L], qk4[:, :, :L], AF.Square), [tr('qk4')], [tr('sq4')])
            if MSTEP < 2.6:
                continue
            S.op('act', CP(cin[cb_i][:, 0:4, 3:3 + L], v0(1, 4)), [PR[1]], [tr('cin%d' % cb_i)])
            S.op('dve', CP(cin[cb_i][:, 4:6, 3:3 + L], v0(2, 2)), [PR[2]], [tr('cin%d' % cb_i)])
            S.op('dve', CP(gqk[:, :, :L], PS[2][:, 2 * L:4 * L].rearrange("p (j l) -> p j l", l=L)), [PR[2]], [tr('gqk')])
            if MSTEP < 2.7:
                continue
            S.op('act', CP(vN[:L, :], PS[3][:L, 0:260]), [PR[3]], [tr('vN')])
            S.op('act', ACTF(zs[:L, :], PS[4][:L, :], AF.Silu), [PR[4]], [tr('zs')])
            if MSTEP < 2.8:
                continue
            S.op('dve', CP(dtgv[:L, :], PS[5][:L, 0:264]), [PR[5]], [tr('dtgv')])
            if MSTEP < 2.85:
                continue
            S.op('dve', CP(glr[:16, :L], PS[5][:16, 320:320 + L]), [PR[5]], [tr('glr')])
            if MSTEP < 2.9:
                continue
            S.op('act', ACTF(zg[:L, :], PS[6][:L, 0:256], AF.Silu), [PR[6]], [tr('zg')])
            if MSTEP < 3:
                continue
            S.op('pe', MMx(PS[7][:, 0:4 * L], bd64_b[:, :], sq4[:, :, :L]), [tr('sq4'), constR], [PR[7]])
            if MSTEP < 3.1:
                continue
            S.op('act', ACTF(rstd4[:, :, :L], v0(7, 4), AF.Sqrt, bias=epsc, scale=1.0 / 64), [PR[7], constR], [tr('rstd4')])
            S.op('dve', RECIP(rstd4[:, :, :L], rstd4[:, :, :L]), [tr('rstd4')], [tr('rstd4')])
            if MSTEP < 3.2:
                continue
            S.op('dve', TT(qk4[:, :, :L], qk4[:, :, :L], rstd4[:, :, :L], ALU.mult), [tr('rstd4'), tr('qk4')], [tr('qk4')])
            S.op('dve', TT(qk4[:, :, :L], qk4[:, :, :L], gcol4[:, :].unsqueeze(2).to_broadcast([128, 4, L]), ALU.mult),
                 [prm, tr('qk4')], [tr('qk4')])
            for e_ in range(2):
                S.op('dve', TS(qT[:, :, e_, :L], qk4[:, 0:2, :L], C['bd64'][:, 64 * e_:64 * e_ + 1], ALU.mult),
                     [tr('qk4'), constR], [tr('qT')])
            S.op('dve', CP(kT_all[:, :, col0:col0 + L], qk4[:, 2:4, :L]), [tr('qk4')], [kTR])
            if MSTEP < 3.4:
                continue
            S.op('pe', TRS([(PS[0][:L, j * 128:(j + 1) * 128], qk4[:, 2 + j, :L], ident) for j in range(2)]),
                 [tr('qk4'), constR], [PR[0]])
            S.op('act', CP(kN[:L, :], PS[0][:L, 0:256]), [PR[0]], [tr('kN')])
            if MSTEP < 3.5:
                continue
            S.dma('sp', DMA(kp_d[l, col0:col0 + L, :], kN[:L, :]), [tr('kN')], [])
            if MSTEP < 4:
                continue
            S.dma('sp', DMA(vp_d[l, col0:col0 + L, :], vN[:L, 0:256]), [tr('vN')], [])
            S.op('dve', CP(V_all[:L, ci, :, 0:64], vN[:L, 0:256].rearrange("p (h d) -> p h d", d=64)), [tr('vN')], [VR])
            S.op('dve', TT(t4[:L, :], vN[:L, 256:260], fb_bc[:L, :], ALU.add), [tr('vN'), prm], [tr('t4')])
            S.op('act', ACTF(t4[:L, :], t4[:L, :], AF.Exp, scale=-1.0), [tr('t4')], [tr('t4')])
            S.op('act', ACTF(t4[:L, :], t4[:L, :], AF.Ln, bias=onec[:L, :]), [tr('t4'), constR], [tr('t4')])
            S.op('dve', TS(lf[:L, :], t4[:L, :], -1.0, ALU.mult), [tr('t4')], [tr('lf')])
            S.dma('sp', DMA(lfp_d[l, col0:col0 + L, :], lf[:L, :]), [tr('lf')], [])
            S.op('pe', MMX([(PS[3][:L, 0:4], Ucst[:L, :L], lf[:L, :], True, True),
                            (PS[3][:, 8:12], ONE[:L, :], lf[:L, :], True, True)]), [tr('lf'), constR], [PR[3]])
            S.op('dve', TT(F_all[:L, ci, :], PS[3][:L, 0:4], Ftot[:L, :], ALU.add), [PR[3], FtR], [FR])
            S.op('dve', TT(nb_all[:, 0:ci + 1, :], Ftot[:, :].unsqueeze(1).to_broadcast([128, ci + 1, 4]),
                           F_all[:, 0:ci + 1, :], ALU.subtract), [FtR, FR], [tr('nb')])
            S.op('dve', TT(Ftot[:, :], Ftot[:, :], PS[3][:, 8:12], ALU.add), [PR[3], tr('nb')], [FtR])
            if MSTEP < 5:
                continue
            for j in range(ci + 1):
                (_, Lj, cj) = chunks[j]
                bs = j % 2
                S.op('pe', MMX([(PS[bs][:Lj, 2 * hp * L:(2 * hp + 2) * L], kT_all[:, hp, cj:cj + Lj],
                                 qT[:, hp, :, :L], True, True) for hp in range(2)]),
                     [kTR, tr('qT')], [PR[bs]])
                if MSTEP < 5.2:
                    continue
                for h in range(4):
                    S.op('act', ACTF(Pt[bs][:Lj, h, :L], PS[bs][:Lj, h * L:(h + 1) * L], AF.Exp,
                                     bias=nb_all[:Lj, j, h:h + 1], scale=0.125), [PR[bs], tr('nb')], [tr('P%d' % bs)])
                if MSTEP < 5.3:
                    continue
                if j == ci:
                    S.op('dve', TT(Pt[bs][:L, :, :L], Pt[bs][:L, :, :L], M01[:L, :L].unsqueeze(1).to_broadcast([L, 4, L]),
                                   ALU.mult), [tr('P%d' % bs), constR], [tr('P%d' % bs)])
                if MSTEP < 5.4:
                    continue
                S.op('pe', MMX([(PS[2][:L, h * 80:h * 80 + 65], Pt[bs][:Lj, h, :L], V_all[:Lj, j, h, 0:65], (j == 0 and h == 0), (j == ci and h == 3))
                                for h in range(4)]), [tr('P%d' % bs), VR], [PR[2]])
            if MSTEP < 5.5:
                continue
            o4 = PS[2][:L, 0:320].rearrange("p (h d) -> p h d", d=80)
            S.op('dve', RECIP(rec4[:L, :].unsqueeze(2), o4[:, :, 64:65]), [PR[2]], [tr('rec4')])
            if MSTEP < 5.6:
                continue
            S.op('dve', TT(mixoN[:L, 0:256].rearrange("p (h d) -> p h d", d=64), o4[:, :, 0:64],
                           rec4[:L, :].unsqueeze(2).to_broadcast([L, 4, 64]), ALU.mult), [PR[2], tr('rec4')], [tr('mixoN')])
            if MSTEP < 6:
                continue
            cn = cin[cb_i]
            cnR = tr('cin%d' % cb_i)
            if prev_cin is None:
                S.op('dve', MSET(cn[:, :, 0:3], 0.0), [], [cnR])
            else:
                (pc, pL, pR) = prev_cin
                S.op('dve', CP(cn[:, :, 0:3], pc[:, :, pL:pL + 3]), [pR], [cnR])
            prev_cin = (cn, L, cnR)
            for tp in range(4):
                dst = acc if tp == 0 else tmpc
                S.op('dve', TT(dst[:, :, :L], cn[:, :, tp:tp + L], cw[:, tp, :].unsqueeze(2).to_broadcast([128, 6, L]), ALU.mult),
                     [cnR, prm], [tr('acc' if tp == 0 else 'xbcT')])
                if tp > 0:
                    S.op('dve', TT(acc[:, :, :L], acc[:, :, :L], tmpc[:, :, :L], ALU.add), [tr('xbcT')], [tr('acc')])
            S.op('dve', TT(acc[:, :, :L], acc[:, :, :L], cbb[:, :].unsqueeze(2).to_broadcast([128, 6, L]), ALU.add),
                 [prm], [tr('acc')])
            S.op('act', ACTF(xbcT[:, :, :L], acc[:, :, :L], AF.Silu), [tr('acc')], [tr('xbcT')])
            if ci == NCH:
                for r_ in range(3):
                    S.dma('sp', DMA(convp_d[l, r_].rearrange("(j p) -> p j", p=128), cn[:, :, L + r_], nonc=True), [cnR], [])
            if MSTEP < 7:
                continue
            S.op('dve', TT(t8[:L, :], dtgv[:L, 0:8], dtb_bc[:L, :], ALU.add), [tr('dtgv'), prm], [tr('t8')])
            S.op('act', ACTF(t8[:L, :], t8[:L, :], AF.Exp), [tr('t8')], [tr('t8')])
            S.op('act', ACTF(dt[:L, :], t8[:L, :], AF.Ln, bias=onec[:L, :]), [tr('t8'), constR], [tr('dt')])
            S.op('dve', TT(da[:L, :], dt[:L, :], a_bc[:L, :], ALU.mult), [tr('dt'), prm], [tr('da')])
            S.op('pe', MMX([(PS[3][:L, 16:24], Ucst[:L, :L], da[:L, :], True, True),
                            (PS[3][:L, 32:40], ONE[:L, :L], da[:L, :], True, True),
                            (PS[3][:, 48:56], ONE[:L, :], da[:L, :], True, True)]), [tr('da'), constR], [PR[3]])
            S.op('dve', CP(cs[:L, :], PS[3][:L, 16:24]), [PR[3]], [tr('cs')])
            S.op('dve', TS(negcs[:L, :], PS[3][:L, 16:24], -1.0, ALU.mult), [PR[3]], [tr('negcs')])
            S.op('dve', TT(wdec[:L, :], PS[3][:L, 32:40], cs[:L, :], ALU.subtract), [PR[3], tr('cs')], [tr('wdec')])
            S.op('act', ACTF(wdec[:L, :], wdec[:L, :], AF.Exp), [tr('wdec')], [tr('wdec')])
            S.op('act', ACTF(expcs[:L, :], cs[:L, :], AF.Exp), [tr('cs')], [tr('expcs')])
            S.op('act', ACTF(edec[:, :], PS[3][:, 48:56], AF.Exp), [PR[3]], [tr('edec')])
            S.op('dve', TT(daU[:L, :, :L], Ucst[:L, :L].unsqueeze(1).to_broadcast([L, 8, L]),
                           da[:L, :].unsqueeze(2).to_broadcast([L, 8, L]), ALU.mult), [tr('da'), constR], [tr('daU')])
            for g in range(2):
                S.op('pe', MMx(PS[g][:L, 0:4 * L], ONE[:L, :L], daU[:L, 4 * g:4 * g + 4, :L]), [tr('daU'), constR], [PR[g]])
                S.op('dve', TT(Em[:L, 4 * g:4 * g + 4, :L], PS[g][:L, 0:4 * L].rearrange("p (h l) -> p h l", l=L),
                               NEGM[:L, :L].unsqueeze(1).to_broadcast([L, 4, L]), ALU.add), [PR[g], constR], [tr('daU')])
            for h in range(8):
                S.op('act', ACTF(Em[:L, h, :L], Em[:L, h, :L], AF.Exp, bias=negcs[:L, h:h + 1]), [tr('daU'), tr('negcs')],
                     [tr('daU')])
            S.op('dve', CP(bcT_b[:, 0, :L], xbcT[:, 4, :L]), [tr('xbcT')], [tr('bcT')])
            for g in range(2):
                S.op('dve', TS(Cblk_f[:, g, :L], xbcT[:, 5, :L], C['bd64'][:, 64 * g:64 * g + 1], ALU.mult),
                     [tr('xbcT'), constR], [tr('Cblk_f')])
            S.op('dve', CP(Cblk_b[:, :, :L], Cblk_f[:, :, :L]), [tr('Cblk_f')], [tr('Cblk_b')])
            S.op('pe', MMx(PS[2][:L, 0:2 * L], bcT_b[:, 0, :L], Cblk_b[:, :, :L]), [tr('bcT'), tr('Cblk_b')], [PR[2]])
            S.op('dve', TT(MT[:L, :, :L].rearrange("p (g h) l -> p g h l", g=2),
                           Em[:L, :, :L].rearrange("p (g h) l -> p g h l", g=2),
                           PS[2][:L, 0:2 * L].rearrange("p (g l) -> p g l", l=L).unsqueeze(2).to_broadcast([L, 2, 4, L]),
                           ALU.mult), [tr('daU'), PR[2]], [tr('MT')])
            S.op('pe', TRS([(PS[3][:L, j * 128:(j + 1) * 128], xbcT[:, j, :L], ident) for j in range(4)]),
                 [tr('xbcT'), constR], [PR[3]])
            S.op('act', CP(xN[:L, :], PS[3][:L, :]), [PR[3]], [tr('xN')])
            S.op('pe', TRS([(PS[4][:L, 0:128], xbcT[:, 4, :L], ident)]), [tr('xbcT'), constR], [PR[4]])
            S.op('act', CP(BN[:L, :], PS[4][:L, 0:128]), [PR[4]], [tr('BN')])
            S.op('dve', TT(xdt[:L, :, :], xN[:L, :].rearrange("p (h d) -> p h d", d=64),
                           dt[:L, :].unsqueeze(2).to_broadcast([L, 8, 64]), ALU.mult), [tr('xN'), tr('dt')], [tr('xdt')])
            S.op('dve', CP(xdt2[:L, :, :, :].rearrange("p h g n -> p g h n"),
                           xdt[:L, :, :].rearrange("p (g h) n -> p g h n", g=2)), [tr('xdt')], [tr('xdt2')])
            S.op('dve', TT(Bw[:L, :, :, :].rearrange("p h g n -> p g h n"),
                           BN[:L, :].rearrange("p (g n) -> p g n", g=2).unsqueeze(2).to_broadcast([L, 2, 4, 64]),
                           wdec[:L, :].rearrange("p (g h) -> p g h", g=2).unsqueeze(3).to_broadcast([L, 2, 4, 64]),
                           ALU.mult), [tr('BN'), tr('wdec')], [tr('Bw')])
            S.op('pe', MMX([(PS[5][:L, h * 64:(h + 1) * 64], MT[:L, h, :L], xdt[:L, h, :], True, True) for h in range(8)]),
                 [tr('MT'), tr('xdt')], [PR[5]])
            S.op('pe', MMX([(PS[6][:L, h * 64:(h + 1) * 64], Cblk_f[:, h // 4, :L], hstT[:, h % 4, :], True, True)
                            for h in range(8)]), [tr('Cblk_f'), hsR], [PR[6]])
            S.op('dve', TT(ytmp[:L, :].rearrange("p (h d) -> p h d", d=64), PS[6][:L, :].rearrange("p (h d) -> p h d", d=64),
                           expcs[:L, :].unsqueeze(2).to_broadcast([L, 8, 64]), ALU.mult), [PR[6], tr('expcs')], [tr('ytmp')])
            S.op('dve', TT(yy[:L, :], ytmp[:L, :], PS[5][:L, :], ALU.add), [PR[5], tr('ytmp')], [tr('yy')])
            S.op('pe', MMX([(PS[7][:, hh * 128:(hh + 1) * 128],
                             Bw[:L, hh, :, :].rearrange("p g n -> p (g n)"),
                             xdt2[:L, hh, :, :].rearrange("p g n -> p (g n)"), True, True)
                            for hh in range(4)]), [tr('Bw'), tr('xdt2')], [PR[7]])
            for g in range(2):
                gs = slice(g * 64, g * 64 + 64)
                S.op('dve', TT(hst_t[gs, :, :], hstT[gs, :, :], edec[gs, 4 * g:4 * g + 4].unsqueeze(2).to_broadcast([64, 4, 64]),
                               ALU.mult), [hsR, tr('edec')], [tr('hst_t')])
                S.op('dve', TT(hstT[gs, :, :], hst_t[gs, :, :],
                               PS[7][gs, :].rearrange("p (h c) -> p h c", c=128)[:, :, g * 64:g * 64 + 64], ALU.add),
                     [tr('hst_t'), PR[7]], [hsR])
            S.op('dve', TT(ytmp[:L, :].rearrange("p (h d) -> p h d", d=64), xN[:L, :].rearrange("p (h d) -> p h d", d=64),
                           sd_bc[:L, :].unsqueeze(2).to_broadcast([L, 8, 64]), ALU.mult), [tr('xN'), prm], [tr('ytmp')])
            S.op('dve', TT(yy[:L, :], yy[:L, :], ytmp[:L, :], ALU.add), [tr('ytmp')], [tr('yy')])
            S.op('dve', TT(yy[:L, :], yy[:L, :], zs[:L, :], ALU.mult), [tr('zs')], [tr('yy')])
            S.op('act', ACTF(ytmp[:L, :], yy[:L, :], AF.Square), [tr('yy')], [tr('ytmp')])
            S.op('dve', RED(ss2[:L, :], ytmp[:L, :].rearrange("p (g d) -> p g d", g=2)), [tr('ytmp')], [tr('ss2')])
            S.op('act', ACTF(ss2[:L, :], ss2[:L, :], AF.Sqrt, bias=epsc[:L, :], scale=1.0 / 256), [tr('ss2'), constR], [tr('ss2')])
            S.op('dve', RECIP(ss2[:L, :], ss2[:L, :]), [tr('ss2')], [tr('ss2')])
            S.op('dve', TT(yy[:L, :].rearrange("p (g d) -> p g d", g=2), yy[:L, :].rearrange("p (g d) -> p g d", g=2),
                           ss2[:L, :].unsqueeze(2).to_broadcast([L, 2, 256]), ALU.mult), [tr('ss2')], [tr('yy')])
            S.op('dve', TT(mixoN[:L, 256:768], yy[:L, :], sn_bc[:L, :], ALU.mult), [tr('yy'), prm], [tr('mixoN')])
            if MSTEP < 8:
                continue
            S.op('pe', MMx(PS[0][:, 0:L], Wg[:16, :], glr[:16, :L]), [tr('glr'), prm], [PR[0]])
            S.op('act', ACTF(lg[:, :L], PS[0][:, 0:L], AF.Exp, bias=ngb[:, 0:1], scale=-1.0), [PR[0], prm], [tr('lg')])
            S.op('act', ACTF(lg[:, :L], lg[:, :L], AF.Ln, bias=onec), [tr('lg'), constR], [tr('lg')])
            S.op('dve', SCAN(cl[:, :L], ONE[:, :L], lg[:, :L]), [tr('lg'), constR], [tr('cl')])
            S.op('act', ACTF(eq[:, :L], cl[:, :L], AF.Exp, scale=-1.0 / 16), [tr('cl')], [tr('eq')])
            S.op('act', ACTF(ek[:, :L], cl[:, :L], AF.Exp, scale=1.0 / 16), [tr('cl')], [tr('ek')])
            S.op('dve', STT(qt[:, :L], gqk[:, 0, :L], float(32 ** -0.5), eq[:, :L], ALU.mult, ALU.mult),
                 [tr('gqk'), tr('eq')], [tr('qt')])
            S.op('dve', TT(kt[:, :L], gqk[:, 1, :L], ek[:, :L], ALU.mult), [tr('gqk'), tr('ek')], [tr('kt')])
            S.op('dve', TT(dl[:, :L], cl[:, L - 1:L].to_broadcast([128, L]), cl[:, :L], ALU.subtract), [tr('cl')], [tr('dl')])
            S.op('act', ACTF(dl[:, :L], dl[:, :L], AF.Exp, scale=-1.0 / 16), [tr('dl')], [tr('dl')])
            S.op('dve', TT(khT[:, :L], gqk[:, 1, :L], dl[:, :L], ALU.mult), [tr('gqk'), tr('dl')], [tr('khT')])
            S.op('act', ACTF(elast[:, :], cl[:, L - 1:L], AF.Exp, scale=-1.0 / 16), [tr('cl')], [tr('elast')])
            S.op('dve', TT(qblk[:, :, :L], qt[:, :L].unsqueeze(1).to_broadcast([128, 4, L]),
                           hm[:, :].unsqueeze(2).to_broadcast([128, 4, L]), ALU.mult), [tr('qt'), constR], [tr('qblk')])
            S.op('pe', MMx(PS[1][:L, 0:4 * L], kt[:, :L], qblk[:, :, :L]), [tr('kt'), tr('qblk')], [PR[1]])
            S.op('dve', TT(attm[:L, :, :L], PS[1][:L, 0:4 * L].rearrange("p (h l) -> p h l", l=L),
                           M01[:L, :L].unsqueeze(1).to_broadcast([L, 4, L]), ALU.mult), [PR[1], constR], [tr('attm')])
            S.op('act', CP(vg[:L, :], dtgv[:L, 8:264]), [tr('dtgv')], [tr('vg')])
            lst = []
            for h in range(4):
                lst.append((PS[0][:L, 256 + h * 64:256 + (h + 1) * 64], attm[:L, h, :L], vg[:L, h * 64:(h + 1) * 64], True, False))
                lst.append((PS[0][:L, 256 + h * 64:256 + (h + 1) * 64], qblk[:, h, :L], Sg_b[:, :], False, True))
            S.op('pe', MMS(lst), [tr('attm'), tr('vg'), tr('qblk'), SgR, tr('lg')], [PR[0]])
            S.op('pe', TRS([(PS[3][:L, 0:128], khT[:, :L], ident)]), [tr('khT'), constR], [PR[3]])
            S.op('act', CP(khN[:L, :], PS[3][:L, 0:128]), [PR[3]], [tr('khN')])
            S.op('pe', MMx(PS[7][:, 0:256], khN[:L, :], vg[:L, :]), [tr('khN'), tr('vg')], [PR[7]])
            S.op('dve', TT(tmpg[:, :, :], PS[7][:, 0:256].rearrange("p (h v) -> p h v", v=64),
                           hm[:, :].unsqueeze(2).to_broadcast([128, 4, 64]), ALU.mult), [PR[7], constR], [tr('tmpg')])
            S.op('dve', RED(red[:, :], tmpg[:, :, :].rearrange("p h v -> p v h")), [tr('tmpg')], [tr('red')])
            S.op('dve', STT(Sg[:, :], Sg[:, :], elast[:, 0:1], red[:, :], ALU.mult, ALU.add), [tr('red'), tr('elast')], [SgR])
            S.op('dve', CP(Sg_b[:, :], Sg[:, :]), [], [SgR])
            og_ps = PS[0][:L, 256:512]
            S.op('act', ACTF(og[:L, :], og_ps, AF.Square), [PR[0]], [tr('og')])
            S.op('dve', RED(ss4[:L, :], og[:L, :].rearrange("p (h v) -> p h v", v=64)), [tr('og')], [tr('ss4')])
            S.op('act', ACTF(ss4[:L, :], ss4[:L, :], AF.Sqrt, bias=epsc[:L, :], scale=1.0 / 64), [tr('ss4'), constR], [tr('ss4')])
            S.op('dve', RECIP(ss4[:L, :], ss4[:L, :]), [tr('ss4')], [tr('ss4')])
            S.op('dve', TT(og[:L, :].rearrange("p (h v) -> p h v", v=64), og_ps.rearrange("p (h v) -> p h v", v=64),
                           ss4[:L, :].unsqueeze(2).to_broadcast([L, 4, 64]), ALU.mult), [PR[0], tr('ss4')], [tr('og')])
            S.op('dve', TT(og[:L, :].rearrange("p (h v) -> p h v", v=64), og[:L, :].rearrange("p (h v) -> p h v", v=64),
                           gn_bc[:L, :].unsqueeze(1).to_broadcast([L, 4, 64]), ALU.mult), [prm], [tr('og')])
            S.op('dve', TT(mixoN[:L, 768:1024], og[:L, :], zg[:L, :], ALU.mult), [tr('og'), tr('zg')], [tr('mixoN')])
            if MSTEP < 9:
                continue
            for half in range(2):
                S.op('pe', TRS([(PS[4 + half][:, j * L:(j + 1) * L], mixoN[:L, (half * 4 + j) * 128:(half * 4 + j + 1) * 128],
                                 ident[:L, :L]) for j in range(4)]), [tr('mixoN'), constR], [PR[4 + half]])
                S.op('act' if half else 'dve', CP(mixoT[:, half * 4:half * 4 + 4, :L],
                                                  PS[4 + half][:, 0:4 * L].rearrange("p (j l) -> p j l", l=L)),
                     [PR[4 + half]], [tr('mixoT')])
            for oc in range(8):
                bo = 6 + oc % 2
                wb_ = oc % 2
                S.dma('pool', DMA(wo_t[wb_][:, :, :], wmo_v[:, :, oc * 128:(oc + 1) * 128]), [], [woR[wb_]])
                S.op('pe', MMX([(PS[bo][:, 0:L], wo_t[wb_][:, k, :], mixoT[:, k, :L], k == 0, k == 7)
                                for k in range(8)]), [woR[wb_], tr('mixoT')], [PR[bo]])
                S.op('dve', STT(xT[:, oc, col0:col0 + L], PS[bo][:, 0:L], 1.0, xT[:, oc, col0:col0 + L], ALU.mult, ALU.add), [PR[bo]], [xcR])
        if MSTEP < 9.5:
            S.barrier()
            return
        S.op('pe', TRS([(PS[0][:64, hh * 128:(hh + 1) * 128], hstT[:, hh, :], ident) for hh in range(4)]),
             [hsR, constR], [PR[0]])
        S.op('act', CP(sso[:64, :, :], PS[0][:64, :].rearrange("p (h n) -> p h n", n=128)), [PR[0]], [tr('sso')])
        for g in range(2):
            if MSTEP < 9.6:
                continue
            S.dma('sp', DMA(ssmp_d[l].rearrange("(g hh p) n -> g p hh n", g=2, hh=4)[g], sso[:64, :, g * 64:(g + 1) * 64]),
                  [tr('sso')], [])
        if MSTEP < 9.7:
            S.barrier()
            return
        S.op('act', CP(red[:, :], Sg[:, :]), [SgR], [tr('red')])
        S.dma('pool', DMA(glap_d[l], red[:, :]), [tr('red')], [])
        S.barrier()
        if SAMPLE:
            sample_chunk()
            S.barrier()

    load_phase()
    S.barrier()
    for l in range(DEPTH):
        if STOP >= 1:
            ffn_phase(l, 0)
        if STOP >= 2 and MIXER:
            mixer_phase(l)
        if STOP >= 3:
            ffn_phase(l, 1)
    store_phase()
    S.barrier()

    if cfg.get('SIMCHK', True):
        semv = {}
        pos = {e: 0 for e in ENG}
        progress = True
        while progress:
            progress = False
            for e in ENG:
                q = S.q[e]
                while pos[e] < len(q):
                    waits, fn, tok, inc = q[pos[e]]
                    if all(semv.get(k, 0) >= v for k, v in waits):
                        if fn is not None:
                            semv[tok[0]] = semv.get(tok[0], 0) + inc
                            assert semv[tok[0]] == tok[1], (tok, semv[tok[0]])
                        pos[e] += 1
                        progress = True
                    else:
                        break
        for e in ENG:
            if pos[e] < len(S.q[e]):
                print('DEADLOCK', e, pos[e], len(S.q[e]), S.q[e][pos[e]][0], {k: semv.get(k, 0) for k, _ in S.q[e][pos[e]][0]})
        print('max sem', max(semv.values()), 'max waits/instr', max(len(w) for e in ENG for (w, _, _, _) in S.q[e]))

    import contextlib
    keys = S.sem_keys()
    with contextlib.ExitStack() as es:
        sems = {k: es.enter_context(nc.semaphore(k)) for k in keys}
        block = es.enter_context(nc.Block())

        def run(e, name):
            for waits, fn, tok, inc in S.q[name]:
                for k, v in waits:
                    e.wait_ge(sems[k], v)
                if fn is not None:
                    fn(e).then_inc(sems[tok[0]], inc)

        @block.tensor
        def _(e):
            run(e, 'pe')

        @block.scalar
        def _(e):
            run(e, 'act')

        @block.vector
        def _(e):
            run(e, 'dve')

        @block.gpsimd
        def _(e):
            run(e, 'pool')

        @block.sync
        def _(e):
            run(e, 'sp')
    print('SBUF peak bytes/partition', st['peak'] - SB_BASE, 'ops', {k: len(v) for k, v in S.q.items()})
    return nc


def prep_inputs(cfg, inp):
    NCH, NS, NPG, NPOOL, NCORES, DEPTH = (cfg[k] for k in ('NCH', 'NS', 'NPG', 'NPOOL', 'NCORES', 'DEPTH'))
    f = lambda a: np.ascontiguousarray(np.asarray(a), dtype=np.float32)
    consts = make_consts(NS, NPG)
    shared = {
        'meta': f(inp['meta_tokens']),
        'w1i': f(inp['ffn1_w_in']), 'w1o': f(inp['ffn1_w_out']),
        'w2i': f(inp['ffn2_w_in']), 'w2o': f(inp['ffn2_w_out']),
        'wmi': f(inp['w_mix_in']), 'wmo': f(inp['w_mix_out']),
        'gains': np.ascontiguousarray(np.stack([f(inp['ffn1_norm']), f(inp['mix_norm']), f(inp['ffn2_norm'])], axis=1)),
        'fqn': f(inp['fox_q_norm']), 'fkn': f(inp['fox_k_norm']), 'fb': f(inp['fox_f_bias']),
        'cw': f(inp['ssd_conv_w']), 'cb': f(inp['ssd_conv_b']), 'dtb': f(inp['ssd_dt_bias']),
        'alog': f(inp['ssd_a_log']), 'sd': f(inp['ssd_d']), 'sn': f(inp['ssd_norm']),
        'wg': f(inp['gla_w_gate']), 'gb': f(inp['gla_gate_bias']), 'gn': f(inp['gla_norm']),
        'cst': np.ascontiguousarray(np.stack([consts[n] for n in CONST_ORDER + (CONST_S if cfg.get('SAMPLE', False) else [])])),
        'hmc': consts['hm'], 'seqm': consts['seqmask'], 'rowm': consts['rowmask'], 'rstm': consts['rst'],
        'pairm': consts['pairmask'],
    }
    if cfg.get('SAMPLE', False):
        shared['rowm32'] = consts['rowmask32']; shared['rst128'] = consts['rst128']
    if cfg.get('GATHER', False):
        shared['radd'] = consts['radd']
        shared['ck'] = f(inp['cache_fox_k']).reshape(DEPTH, NPOOL * 128, 256)
        shared['cv'] = f(inp['cache_fox_v']).reshape(DEPTH, NPOOL * 128, 256)
        shared['clf'] = f(inp['cache_fox_logf']).reshape(DEPTH, NPOOL * 128, 4)
    maps = []
    xp = f(inp['x_prompt'])
    xs = f(inp['x_sample'])
    pt = np.ascontiguousarray(np.asarray(inp['page_table']), dtype=np.int32)
    sssm = f(inp['state_ssm'])
    sconv = f(inp['state_conv'])
    sgla = f(inp['state_gla'])
    for c in range(NCORES):
        m = dict(shared)
        m['xp'] = np.ascontiguousarray(xp[c])
        m['xs'] = np.ascontiguousarray(xs[c * NS:(c + 1) * NS].reshape(NS * 4, D))
        m['pt'] = np.ascontiguousarray(pt[c * NS:(c + 1) * NS])
        m['sssm'] = np.ascontiguousarray(sssm[:, c * NS:(c + 1) * NS])
        m['sconv'] = np.ascontiguousarray(sconv[:, c * NS:(c + 1) * NS].reshape(DEPTH, NS * 3, 768))
        m['sgla'] = np.ascontiguousarray(sgla[:, c * NS:(c + 1) * NS].reshape(DEPTH, NS, 128, 64))
        maps.append(m)
    return maps


def gather_outputs(cfg, res):
    NCH, NS, NCORES, DEPTH = (cfg[k] for k in ('NCH', 'NS', 'NCORES', 'DEPTH'))
    NP = NCH * 128
    TP = 16 + NP
    R = res.results
    cat = lambda key, ax: np.concatenate([np.asarray(r[key]) for r in R], axis=ax)
    stk = lambda key: np.stack([np.asarray(r[key]) for r in R], axis=1)
    yp = np.stack([np.asarray(r['yp']) for r in R], axis=0)
    ys = cat('ys', 0).reshape(NCORES * NS, 4, D)
    kp = stk('kp').reshape(DEPTH, NCORES, TP, 4, 64)
    vp = stk('vp').reshape(DEPTH, NCORES, TP, 4, 64)
    lfp = stk('lfp').reshape(DEPTH, NCORES, TP, 4)
    ssmp = stk('ssmp').reshape(DEPTH, NCORES, 8, 64, 64)
    convp = stk('convp').reshape(DEPTH, NCORES, 3, 768)
    glap = stk('glap').reshape(DEPTH, NCORES, 4, 32, 64)
    ks = cat('ks', 1).reshape(DEPTH, NCORES * NS, 4, 4, 64)
    vs = cat('vs', 1).reshape(DEPTH, NCORES * NS, 4, 4, 64)
    lfs = cat('lfs', 1).reshape(DEPTH, NCORES * NS, 4, 4)
    ssms = cat('ssms', 1).reshape(DEPTH, NCORES * NS, 8, 64, 64)
    convs = cat('convs', 1).reshape(DEPTH, NCORES * NS, 3, 768)
    glas = cat('glas', 1).reshape(DEPTH, NCORES * NS, 4, 32, 64)
    outs = (yp, ys, kp, vp, lfp, ssmp, convp, glap, ks, vs, lfs, ssms, convs, glas)
    return tuple(np.ascontiguousarray(o, dtype=np.float32) for o in outs)


def kernel(**inputs):
    cfg = dict(CFG)
    nc = build(cfg)
    maps = prep_inputs(cfg, inputs)
    res = run_bass_kernel_spmd(nc, maps, core_ids=list(range(cfg['NCORES'])))
    return gather_outputs(cfg, res)
```
